# Optimizing a Trainium2 kernel written in Bass

```python
import jax
import jax.numpy as jnp
from jax import lax
import numpy as np

D_MODEL = 1024
BATCH = 32
SEQ = 2048
DEPTH = 4

GRID_W = 64
ROPE_THETA = 10000.0
NORM_EPS = 1e-6

A_HEADS = 8
A_KV_HEADS = 2
A_HEAD_DIM = D_MODEL // 16
A_WIDTH = A_HEADS * A_HEAD_DIM
A_KV_WIDTH = A_KV_HEADS * A_HEAD_DIM
Q_BLOCK = 128

B_HEADS = 8
B_HEAD_DIM = D_MODEL // 16
B_WIDTH = B_HEADS * B_HEAD_DIM
DECAY_LORA = 64
ICLR_LORA = 64
LNX_EPS = 64e-5

C_HEADS = 4
C_QK_HEAD_DIM = D_MODEL // C_HEADS
C_V_HEAD_DIM = 2 * D_MODEL // C_HEADS
C_QK_WIDTH = C_HEADS * C_QK_HEAD_DIM
C_V_WIDTH = C_HEADS * C_V_HEAD_DIM
RET_CHUNK = 128
GN_EPS = 1e-5

A_SPLITS = (A_WIDTH, A_KV_WIDTH, A_KV_WIDTH, A_WIDTH)
B_SHIFT_SPLITS = (B_WIDTH, B_WIDTH, B_WIDTH, DECAY_LORA, DECAY_LORA, ICLR_LORA, ICLR_LORA)
A_IN = sum(A_SPLITS)
B_SHIFT_IN = sum(B_SHIFT_SPLITS)
EVEN_IN = A_IN + B_SHIFT_IN + B_WIDTH
EVEN_MIX = A_WIDTH + B_WIDTH
ODD_SPLITS = (C_QK_WIDTH, C_QK_WIDTH, C_V_WIDTH, C_V_WIDTH)
ODD_IN = sum(ODD_SPLITS)
N_EVEN = (DEPTH + 1) // 2
N_ODD = DEPTH // 2

kernel_name = 'hybrid_gqa_rwkv7_retention_encoder'


def split_cols(x, sizes):
    cuts = [int(c) for c in np.cumsum(sizes)[:-1]]
    return jnp.split(x, cuts, axis=-1)


def rms_norm(x, g, eps=NORM_EPS):
    xf = x.astype(jnp.float32)
    y = xf * lax.rsqrt(jnp.mean(xf * xf, axis=-1, keepdims=True) + eps)
    return (y * g.astype(jnp.float32)).astype(x.dtype)


def head_norm(y, g, eps):
    yf = y.astype(jnp.float32)
    mean = jnp.mean(yf, axis=-1, keepdims=True)
    var = jnp.mean(jnp.square(yf - mean), axis=-1, keepdims=True)
    out = (yf - mean) * lax.rsqrt(var + eps)
    return out.reshape(y.shape[0], y.shape[1], -1) * g.astype(jnp.float32)


def axial_angles(T, dim):
    rows_count = T // GRID_W
    row = jnp.repeat(jnp.arange(rows_count, dtype=jnp.float32), GRID_W)
    col = jnp.tile(jnp.arange(GRID_W, dtype=jnp.float32), rows_count)
    half = dim // 2
    inv_freq = ROPE_THETA ** (-jnp.arange(0, half, 2, dtype=jnp.float32) / half)
    return row[:, None] * inv_freq, col[:, None] * inv_freq


def rotate_half_rope(x, ang):
    x1, x2 = jnp.split(x, 2, axis=-1)
    c = jnp.cos(ang)[None, :, None, :].astype(x.dtype)
    s = jnp.sin(ang)[None, :, None, :].astype(x.dtype)
    return jnp.concatenate([x1 * c - x2 * s, x2 * c + x1 * s], axis=-1)


def axial_rope(x):
    T, dim = x.shape[1], x.shape[-1]
    ang_row, ang_col = axial_angles(T, dim)
    half = dim // 2
    return jnp.concatenate([rotate_half_rope(x[..., :half], ang_row),
                            rotate_half_rope(x[..., half:], ang_col)], axis=-1)


def gqa_attention(q, k, v):
    B, T, Hq, d = q.shape
    G = k.shape[2]
    R = Hq // G
    nb = T // Q_BLOCK
    qb = q.reshape(B, nb, Q_BLOCK, G, R, d).transpose(1, 0, 2, 3, 4, 5)
    scale = d ** -0.5

    def block(qi):
        s = jnp.einsum('bqgrd,bkgd->bgrqk', qi, k).astype(jnp.float32) * scale
        p = jax.nn.softmax(s, axis=-1).astype(v.dtype)
        return jnp.einsum('bgrqk,bkgd->bqgrd', p, v)

    o = lax.map(block, qb)
    return o.transpose(1, 0, 2, 3, 4, 5).reshape(B, T, Hq * d)


def centred_shift(u, mu):
    prev = jnp.pad(u[:, :-1], ((0, 0), (1, 0), (0, 0)))
    nxt = jnp.pad(u[:, 1:], ((0, 0), (0, 1), (0, 0)))
    return u + mu * (0.5 * (prev + nxt) - u)


def rwkv7_scan(r, w, k, v, kk, a):
    B, T, H, N = r.shape

    def step(S, inp):
        r_t, w_t, k_t, v_t, kk_t, a_t = inp
        sa = jnp.einsum('bhij,bhj->bhi', S, -kk_t)
        S = (S * w_t[:, :, None, :] + sa[..., None] * (kk_t * a_t)[:, :, None, :]
             + v_t[..., None] * k_t[:, :, None, :])
        return S, jnp.einsum('bhij,bhj->bhi', S, r_t)

    S0 = jnp.zeros((B, H, N, N), jnp.float32)
    xs = tuple(jnp.swapaxes(t, 0, 1) for t in (r, w, k, v, kk, a))
    _, y = lax.scan(step, S0, xs)
    return jnp.swapaxes(y, 0, 1)


def rwkv7_direction(r, k_raw, v, kk, w_lo, a_lo, w0, w2, a0, a2, k_a, reverse):
    f = lambda t: t.astype(jnp.float32)
    B, T, _ = r.shape
    w_log = -jax.nn.softplus(-(f(w0) + jnp.tanh(f(w_lo)) @ f(w2))) - 0.5
    decay = jnp.exp(-jnp.exp(w_log))
    a = jax.nn.sigmoid(f(a0) + f(a_lo) @ f(a2))
    k = f(k_raw) * (1.0 + (a - 1.0) * f(k_a))
    heads = lambda t: t.reshape(B, T, B_HEADS, B_HEAD_DIM)
    seqs = [heads(f(r)), heads(decay), heads(k), heads(f(v)), kk, heads(a)]
    if reverse:
        seqs = [jnp.flip(t, axis=1) for t in seqs]
    y = rwkv7_scan(*seqs)
    return jnp.flip(y, axis=1) if reverse else y


def retention_direction(q, k, v, log_gamma, strict):
    B, T, H, dk = q.shape
    dv = v.shape[-1]
    C = RET_CHUNK
    n = T // C
    idx = jnp.arange(C, dtype=jnp.float32)
    diff = idx[:, None] - idx[None, :]
    mask = (diff > 0) if strict else (diff >= 0)
    d_intra = jnp.where(mask[None], jnp.exp(jnp.where(mask, diff, 0.0)[None] * log_gamma[:, None, None]), 0.0)
    q_decay = jnp.exp((idx + 1.0)[:, None] * log_gamma[None, :])
    k_decay = jnp.exp((C - 1.0 - idx)[:, None] * log_gamma[None, :])
    chunk_decay = jnp.exp(C * log_gamma)
    qc = jnp.swapaxes(q.reshape(B, n, C, H, dk), 0, 1)
    kc = jnp.swapaxes(k.reshape(B, n, C, H, dk), 0, 1)
    vc = jnp.swapaxes(v.reshape(B, n, C, H, dv), 0, 1)

    def step(Rs, inp):
        qi, ki, vi = inp
        s = jnp.einsum('bchd,blhd->bhcl', qi, ki) * d_intra[None]
        intra = jnp.einsum('bhcl,blhe->bche', s, vi)
        cross = jnp.einsum('bchd,bhde->bche', qi, Rs) * q_decay[None, :, :, None]
        Rs = (Rs * chunk_decay[None, :, None, None]
              + jnp.einsum('blhd,blhe->bhde', ki * k_decay[None, :, :, None], vi))
        return Rs, intra + cross

    R0 = jnp.zeros((B, H, dk, dv), jnp.float32)
    _, o = lax.scan(step, R0, (qc, kc, vc))
    return jnp.swapaxes(o, 0, 1).reshape(B, T, H, dv)


def even_mixer(h, w_in, mu, q_gain, k_gain, k_k, k_a, r_k, w0_f, w2_f, a0_f, a2_f,
               w0_b, w2_b, a0_b, a2_b, lnx_g, lnx_b, w_out):
    B, T, _ = h.shape
    proj = h @ w_in
    a_part, b_shift, b_gate = split_cols(proj, (A_IN, B_SHIFT_IN, B_WIDTH))
    a_q, a_k, a_v, a_g = split_cols(a_part, A_SPLITS)
    q = rms_norm(a_q.reshape(B, T, A_HEADS, A_HEAD_DIM), q_gain)
    k = rms_norm(a_k.reshape(B, T, A_KV_HEADS, A_HEAD_DIM), k_gain)
    v = a_v.reshape(B, T, A_KV_HEADS, A_HEAD_DIM)
    out_a = gqa_attention(axial_rope(q), axial_rope(k), v) * jax.nn.silu(a_g)
    u = centred_shift(b_shift, mu)
    r, kb, vb, w_lo_f, w_lo_b, a_lo_f, a_lo_b = split_cols(u, B_SHIFT_SPLITS)
    heads = lambda t: t.astype(jnp.float32).reshape(B, T, B_HEADS, B_HEAD_DIM)
    kk = heads(kb * k_k)
    kk = kk / jnp.maximum(jnp.sqrt(jnp.sum(kk * kk, axis=-1, keepdims=True)), 1e-12)
    y_f = rwkv7_direction(r, kb, vb, kk, w_lo_f, a_lo_f, w0_f, w2_f, a0_f, a2_f, k_a, False)
    y_b = rwkv7_direction(r, kb, vb, kk, w_lo_b, a_lo_b, w0_b, w2_b, a0_b, a2_b, k_a, True)
    y = head_norm(y_f + y_b, lnx_g, LNX_EPS) + lnx_b.astype(jnp.float32)
    rh, kh, vh = heads(r), heads(kb), heads(vb)
    bonus = (jnp.sum(rh * kh * r_k.astype(jnp.float32), axis=-1, keepdims=True) * vh).reshape(B, T, B_WIDTH)
    out_b = ((y + bonus) * jax.nn.silu(b_gate.astype(jnp.float32))).astype(h.dtype)
    return jnp.concatenate([out_a.astype(h.dtype), out_b], axis=-1) @ w_out


def odd_mixer(h, w_in, gn_g, w_out):
    B, T, _ = h.shape
    q, k, v, g = split_cols(h @ w_in, ODD_SPLITS)
    q = axial_rope(q.reshape(B, T, C_HEADS, C_QK_HEAD_DIM)).astype(jnp.float32)
    k = axial_rope(k.reshape(B, T, C_HEADS, C_QK_HEAD_DIM)).astype(jnp.float32) * (C_QK_HEAD_DIM ** -0.5)
    v = v.reshape(B, T, C_HEADS, C_V_HEAD_DIM).astype(jnp.float32)
    log_gamma_fwd = jnp.log(1.0 - 2.0 ** (-5.0 - jnp.arange(C_HEADS, dtype=jnp.float32)))
    log_gamma_bwd = log_gamma_fwd[::-1]
    o_f = retention_direction(q, k, v, log_gamma_fwd, False)
    o_b = jnp.flip(retention_direction(jnp.flip(q, 1), jnp.flip(k, 1), jnp.flip(v, 1), log_gamma_bwd, True), 1)
    y = head_norm(o_f + o_b, gn_g, GN_EPS)
    return (jax.nn.silu(g.astype(jnp.float32)) * y).astype(h.dtype) @ w_out


def setup_inputs(seed: int = 0) -> dict:
    key = jax.random.key(seed)
    ks = jax.random.split(key, 24)
    nrm = lambda k, shape, scale: scale * jax.random.normal(k, shape, jnp.float32)
    gain = lambda k, shape: 1.0 + 0.02 * jax.random.normal(k, shape, jnp.float32)
    uni = lambda k, shape, lo, hi: jax.random.uniform(k, shape, jnp.float32, lo, hi)
    return {
        'x': nrm(ks[0], (BATCH, SEQ, D_MODEL), 1.0),
        'pre_gain': gain(ks[1], (DEPTH, D_MODEL)),
        'post_gain': gain(ks[2], (DEPTH, D_MODEL)),
        'even_w_in': nrm(ks[3], (N_EVEN, D_MODEL, EVEN_IN), D_MODEL ** -0.5),
        'even_mu': uni(ks[4], (N_EVEN, B_SHIFT_IN), 0.0, 1.0),
        'even_q_gain': gain(ks[5], (N_EVEN, A_HEAD_DIM)),
        'even_k_gain': gain(ks[6], (N_EVEN, A_HEAD_DIM)),
        'even_k_k': 0.85 + 0.02 * jax.random.normal(ks[7], (N_EVEN, B_WIDTH), jnp.float32),
        'even_k_a': gain(ks[8], (N_EVEN, B_WIDTH)),
        'even_r_k': nrm(ks[9], (N_EVEN, B_HEADS, B_HEAD_DIM), 0.1),
        'even_w0_f': uni(ks[10], (N_EVEN, B_WIDTH), -6.0, 1.0),
        'even_w2_f': nrm(ks[11], (N_EVEN, DECAY_LORA, B_WIDTH), 0.5 * DECAY_LORA ** -0.5),
        'even_a0_f': nrm(ks[12], (N_EVEN, B_WIDTH), 0.1),
        'even_a2_f': nrm(ks[13], (N_EVEN, ICLR_LORA, B_WIDTH), 0.5 * ICLR_LORA ** -0.5),
        'even_w0_b': uni(ks[14], (N_EVEN, B_WIDTH), -6.0, 1.0),
        'even_w2_b': nrm(ks[15], (N_EVEN, DECAY_LORA, B_WIDTH), 0.5 * DECAY_LORA ** -0.5),
        'even_a0_b': nrm(ks[16], (N_EVEN, B_WIDTH), 0.1),
        'even_a2_b': nrm(ks[17], (N_EVEN, ICLR_LORA, B_WIDTH), 0.5 * ICLR_LORA ** -0.5),
        'even_lnx_g': gain(ks[18], (N_EVEN, B_WIDTH)),
        'even_lnx_b': nrm(ks[19], (N_EVEN, B_WIDTH), 0.02),
        'even_w_out': nrm(ks[20], (N_EVEN, EVEN_MIX, D_MODEL), EVEN_MIX ** -0.5),
        'odd_w_in': nrm(ks[21], (N_ODD, D_MODEL, ODD_IN), D_MODEL ** -0.5),
        'odd_gn_g': gain(ks[22], (N_ODD, C_V_WIDTH)),
        'odd_w_out': nrm(ks[23], (N_ODD, C_V_WIDTH, D_MODEL), C_V_WIDTH ** -0.5),
    }


def reference(x, pre_gain, post_gain, even_w_in, even_mu, even_q_gain, even_k_gain, even_k_k,
              even_k_a, even_r_k, even_w0_f, even_w2_f, even_a0_f, even_a2_f, even_w0_b,
              even_w2_b, even_a0_b, even_a2_b, even_lnx_g, even_lnx_b, even_w_out,
              odd_w_in, odd_gn_g, odd_w_out):
    h = x
    for layer in range(DEPTH):
        hn = rms_norm(h, pre_gain[layer])
        if layer % 2 == 0:
            i = layer // 2
            m = even_mixer(hn, even_w_in[i], even_mu[i], even_q_gain[i], even_k_gain[i],
                           even_k_k[i], even_k_a[i], even_r_k[i], even_w0_f[i], even_w2_f[i],
                           even_a0_f[i], even_a2_f[i], even_w0_b[i], even_w2_b[i], even_a0_b[i],
                           even_a2_b[i], even_lnx_g[i], even_lnx_b[i], even_w_out[i])
        else:
            j = layer // 2
            m = odd_mixer(hn, odd_w_in[j], odd_gn_g[j], odd_w_out[j])
        h = h + rms_norm(m, post_gain[layer])
    return h
```

```python
import os
import numpy as np
from contextlib import ExitStack
import concourse.bass as bass
import concourse.mybir as mybir
from concourse.bass_utils import run_bass_kernel_spmd

F32 = mybir.dt.float32
BF16 = mybir.dt.bfloat16
AF = mybir.ActivationFunctionType
ALU = mybir.AluOpType
AX = mybir.AxisListType

ENGS = ['pe', 'dve', 'act', 'pool', 'sp']
ENG_ATTR = {'pe': 'tensor', 'dve': 'vector', 'act': 'scalar', 'pool': 'gpsimd', 'sp': 'sync'}
N_DMA_SEM = 12

T = 2048
D = 1024
NT = T // 128
EVEN_IN = 3584
ODD_IN = 6144
EPS = 1e-6
WDEC = float(np.exp(-0.5))


class Prog:
    def __init__(self, nc):
        self.nc = nc
        self.st = ExitStack()
        self.ops = {e: [] for e in ENGS}
        self.cnt = {e: 0 for e in ENGS}
        self.seen = {e: {} for e in ENGS}
        self.last_w = {}
        self.readers = {}
        self.dma_rr = {e: 0 for e in ENGS}
        self.dma_use = {}
        self.semnames = set()

    def sb(self, name, shape, dt):
        return self.st.enter_context(self.nc.sbuf_tensor(name, list(shape), dt))

    def ps(self, name, shape, dt):
        return self.st.enter_context(self.nc.psum_tensor(name, list(shape), dt))

    def dram(self, name, shape, dt, kind=None):
        if kind is None:
            return self.nc.dram_tensor(name, list(shape), dt).ap()
        return self.nc.dram_tensor(name, list(shape), dt, kind=kind).ap()

    def _deps(self, eng, reads, writes):
        deps = []
        for r in reads:
            w = self.last_w.get(r)
            if w is not None:
                deps.append(w)
        for w_ in writes:
            w = self.last_w.get(w_)
            if w is not None:
                deps.append(w)
            deps.extend(self.readers.get(w_, ()))
        waits = []
        seen = self.seen[eng]
        best = {}
        for s, v in deps:
            if seen.get(s, 0) >= v:
                continue
            if best.get(s, 0) < v:
                best[s] = v
        for s, v in best.items():
            seen[s] = v
            waits.append((s, v))
        return waits

    def _update(self, me, reads, writes):
        for r in reads:
            self.readers.setdefault(r, []).append(me)
        for w in writes:
            self.last_w[w] = me
            self.readers[w] = []

    def op(self, eng, fn, reads=(), writes=()):
        waits = self._deps(eng, reads, writes)
        self.cnt[eng] += 1
        s = 'c_' + eng
        self.semnames.add(s)
        me = (s, self.cnt[eng])
        if eng == 'pe':
            self.seen[eng][s] = self.cnt[eng]
        self.ops[eng].append((waits, fn, s, 1))
        self._update(me, reads, writes)

    def dma(self, eng, fn, reads=(), writes=()):
        i = self.dma_rr[eng]
        self.dma_rr[eng] = (i + 1) % N_DMA_SEM
        s = 'd_%s_%d' % (eng, i)
        self.semnames.add(s)
        u = self.dma_use.get(s, 0)
        waits = self._deps(eng, reads, writes)
        if u > 0 and self.seen[eng].get(s, 0) < 16 * u:
            waits.append((s, 16 * u))
            self.seen[eng][s] = 16 * u
        self.dma_use[s] = u + 1
        me = (s, 16 * (u + 1))
        self.ops[eng].append((waits, fn, s, 16))
        self._update(me, reads, writes)

    def barrier(self):
        for e in ENGS:
            waits = []
            seen = self.seen[e]
            for s, u in self.dma_use.items():
                if seen.get(s, 0) < 16 * u:
                    waits.append((s, 16 * u))
                    seen[s] = 16 * u
            for o in ENGS:
                if self.cnt[o] > 0 and seen.get('c_' + o, 0) < self.cnt[o]:
                    waits.append(('c_' + o, self.cnt[o]))
                    seen['c_' + o] = self.cnt[o]
            if waits:
                self.ops[e].append((waits, None, None, 0))

    def finish(self):
        eng = 'sp'
        waits = []
        for s, u in self.dma_use.items():
            if self.seen[eng].get(s, 0) < 16 * u:
                waits.append((s, 16 * u))
        for e in ENGS:
            if self.cnt[e] > 0 and e != eng:
                waits.append(('c_' + e, self.cnt[e]))
        self.ops[eng].append((waits, None, None, 0))

    def build(self):
        nc = self.nc
        self.finish()
        sems = {}
        for name in sorted(self.semnames):
            sems[name] = self.st.enter_context(nc.semaphore(name))
        block = self.st.enter_context(nc.Block())
        for eng in ENGS:
            if not self.ops[eng]:
                continue
            deco = getattr(block, ENG_ATTR[eng])

            def body(e, eng=eng):
                for waits, fn, s, inc in self.ops[eng]:
                    for ws, wv in waits:
                        e.wait_ge(sems[ws], wv)
                    if fn is not None:
                        ins = fn(e)
                        ins.then_inc(sems[s], inc)
            deco(body)
        self.st.close()
        return nc

    def n_instr(self):
        return {e: len(self.ops[e]) for e in ENGS}


def _rope_tables(dim):
    half = dim // 2
    q = dim // 4
    t = np.arange(T)
    row = (t // 64).astype(np.float32)
    col = (t % 64).astype(np.float32)
    inv = (10000.0 ** (-np.arange(0, half, 2, dtype=np.float32) / half)).astype(np.float32)
    ar = row[:, None] * inv
    ac = col[:, None] * inv
    cos = np.concatenate([np.cos(ar), np.cos(ar), np.cos(ac), np.cos(ac)], axis=1)
    sin = np.concatenate([-np.sin(ar), np.sin(ar), -np.sin(ac), np.sin(ac)], axis=1)
    return cos.astype(np.float32), sin.astype(np.float32)


def host_consts():
    c = {}
    c['ident'] = np.eye(128, dtype=np.float32)
    i = np.arange(128)
    su = (i[:, None] < i[None, :]).astype(np.float32)
    iu = (i[:, None] <= i[None, :]).astype(np.float32)
    sl = (i[:, None] > i[None, :]).astype(np.float32)
    il = (i[:, None] >= i[None, :]).astype(np.float32)
    m = np.stack([su, iu, sl, il], axis=1)
    c['masks'] = m.reshape(128, 512).astype(np.float32)
    c['tri'] = (-WDEC * m).reshape(128, 512).astype(np.float32)
    cos64, sin64 = _rope_tables(64)
    c['cosA'] = np.tile(cos64, (1, 10)).astype(np.float32)
    c['sinA'] = np.tile(sin64, (1, 10)).astype(np.float32)
    cos256, sin256 = _rope_tables(256)
    c['cosC'] = np.concatenate([cos256, cos256 / 16.0], axis=1).astype(np.float32)
    c['sinC'] = np.concatenate([sin256, sin256 / 16.0], axis=1).astype(np.float32)
    lgf = np.log(1.0 - 2.0 ** (-5.0 - np.arange(4, dtype=np.float64)))
    lgb = lgf[::-1]
    cc = np.arange(3968)[None, :] - 1920 - np.arange(128)[:, None]
    td = np.zeros((4, 128, 3968), np.float32)
    for h in range(4):
        td[h] = np.where(cc >= 0, np.exp(cc * lgf[h]), np.exp(-cc * lgb[h])).astype(np.float32)
    c['retdec'] = td.transpose(1, 0, 2).reshape(128, 4 * 3968).copy()
    c['negcol'] = np.full((128, 1), -WDEC, np.float32)
    return c


CONST_SHAPES = {'ident': [128, 128], 'masks': [128, 512], 'tri': [128, 512], 'cosA': [T, 640], 'sinA': [T, 640],
                'cosC': [T, 512], 'sinC': [T, 512], 'retdec': [128, 4 * 3968], 'negcol': [128, 1]}

PARAM_SHAPES = {
    'pre_gain': [4, 1024], 'post_gain': [4, 1024], 'even_w_in': [2, 1024, 3584], 'even_mu': [2, 1792],
    'even_q_gain': [2, 64], 'even_k_gain': [2, 64], 'even_k_k': [2, 512], 'even_k_a': [2, 512],
    'even_r_k': [2, 512], 'even_w0_f': [2, 512], 'even_w2_f': [2, 64, 512], 'even_a0_f': [2, 512],
    'even_a2_f': [2, 64, 512], 'even_w0_b': [2, 512], 'even_w2_b': [2, 64, 512], 'even_a0_b': [2, 512],
    'even_a2_b': [2, 64, 512], 'even_lnx_g': [2, 512], 'even_lnx_b': [2, 512], 'even_w_out': [2, 1024, 1024],
    'odd_w_in': [2, 1024, 6144], 'odd_gn_g': [2, 2048], 'odd_w_out': [2, 2048, 1024],
}


ARENA = 84 * 1024


def build_program(nseq=4, layers=(0, 1, 2, 3), dbg=False, parts=('a', 'b')):
    nc = bass.Bass("TRN2", target_bir_lowering=False)
    P = Prog(nc)

    def X(eng, meth, reads, writes, **kw):
        P.op(eng, lambda e: getattr(e, meth)(**kw), reads, writes)

    def DMA(eng, out, in_, reads, writes):
        P.dma(eng, lambda e: e.dma_start(out=out, in_=in_), reads, writes)

    x = P.dram('x', [nseq * T, D], F32, 'ExternalInput')
    y = P.dram('y', [nseq * T, D], F32, 'ExternalOutput')
    prm = {k: P.dram(k, shp, F32, 'ExternalInput') for k, shp in PARAM_SHAPES.items()}
    cst = {k: P.dram('c_' + k, shp, F32, 'ExternalInput') for k, shp in CONST_SHAPES.items()}
    proj = P.dram('proj', [T, ODD_IN], F32)
    mixd = P.dram('mixd', [T, 2048], F32)
    ubd = P.dram('ubd', [T, 1792], F32)
    dbgo = P.dram('dbgo', [128, 8 * 512], F32, 'ExternalOutput') if dbg else None

    idb = P.sb('idb', [128, 128], BF16)
    msk = P.sb('msk', [128, 4, 128], BF16)
    tri = P.sb('tri', [128, 4, 128], F32)
    negcol = P.sb('negcol', [128, 1], F32)
    gpre = P.sb('gpre', [128, D], F32)
    gpost = P.sb('gpost', [128, D], F32)
    hx = [P.sb('hx%d' % i, [128, D], F32) for i in range(2)]
    junk = P.sb('junk', [128, 2048], BF16)
    hnb = [P.sb('hnb%d' % i, [128, 2048], BF16) for i in range(2)]
    stt = [P.sb('stt%d' % i, [128, 8], F32) for i in range(2)]
    ev = [P.sb('ev%d' % i, [128, 512], F32) for i in range(3)]
    arena = P.sb('arena', [128, ARENA], BF16)
    ptr = [P.ps('ptr%d' % i, [128, 1024], BF16) for i in range(2)]
    pa = [P.ps('pa%d' % i, [128, 512], F32) for i in range(6)]

    DMA('pool', idb[:], cst['ident'], [], ['idb'])
    DMA('pool', msk[:].rearrange("p a b -> p (a b)"), cst['masks'], [], ['msk'])
    DMA('sp', tri[:].rearrange("p a b -> p (a b)"), cst['tri'], [], ['tri'])
    DMA('sp', negcol[:], cst['negcol'], [], ['negcol'])

    cnt = {'ev': 0, 'ptr': 0, 'ph': 0}

    class Phase:
        def __init__(self):
            P.barrier()
            self.off = 0
            cnt['ph'] += 1
            self.id = cnt['ph']

        def alloc(self, shape, dt):
            n = int(np.prod(shape[1:]))
            sz = n * (2 if dt == F32 else 1)
            self.off = (self.off + 7) // 8 * 8
            v = arena[:, self.off:self.off + sz]
            self.off += sz
            assert self.off <= ARENA, ('arena overflow', self.off)
            if dt == F32:
                v = v.bitcast(F32)
            if len(shape) == 3:
                v = v.rearrange("p (a b) -> p a b", a=shape[1])
            elif len(shape) == 4:
                v = v.rearrange("p (a b c) -> p a b c", a=shape[1], b=shape[2])
            return v

        def key(self, name):
            return 'ph%d_%s' % (self.id, name)

    def next_ev():
        cnt['ev'] = (cnt['ev'] + 1) % 3
        return cnt['ev']

    def next_ptr():
        cnt['ptr'] = (cnt['ptr'] + 1) % 2
        return cnt['ptr']

    def rms_rows(src_reads, src, b, width, eps):
        X('act', 'activation', src_reads, ['junk', 'stt%d' % b], out=junk[:, 0:width], in_=src, func=AF.Square,
          accum_out=stt[b][:, 0:1])
        X('dve', 'tensor_scalar', ['stt%d' % b], ['stt%d' % b], out=stt[b][:, 1:2], in0=stt[b][:, 0:1],
          scalar1=1.0 / width, scalar2=eps, op0=ALU.mult, op1=ALU.add)
        X('act', 'activation', ['stt%d' % b], ['stt%d' % b], out=stt[b][:, 2:3], in_=stt[b][:, 1:2], func=AF.Sqrt)
        X('dve', 'reciprocal', ['stt%d' % b], ['stt%d' % b], out=stt[b][:, 3:4], in_=stt[b][:, 2:3])

    def transpose_tile(src_bf, src_key, nchunks, dst_fn, dst_key):
        for c0 in range(0, nchunks, 4):
            n = min(4, nchunks - c0)
            pb = next_ptr()
            for k in range(n):
                X('pe', 'transpose', [src_key, 'idb'], ['ptr%d' % pb], out=ptr[pb][:, k * 128:(k + 1) * 128],
                  in_=src_bf[:, (c0 + k) * 128:(c0 + k + 1) * 128], identity=idb[:])
            src = ptr[pb][:, 0:n * 128].rearrange("p (k t) -> p k t", k=n)
            if (c0 // 4) % 2 == 0:
                X('act', 'activation', ['ptr%d' % pb], [dst_key], out=dst_fn(c0, n), in_=src, func=AF.Copy)
            else:
                X('dve', 'tensor_copy', ['ptr%d' % pb], [dst_key], out=dst_fn(c0, n), in_=src)

    def rope(eng2, xin, xkey, cos, sin, ckey, tmpA, tmpB, tkey, out_bf, okey, H, dim):
        q = dim // 4
        v5 = lambda a: a.rearrange("p (h x f q) -> p h x f q", h=H, x=2, f=2)
        X('dve', 'tensor_tensor', [xkey, ckey], [tkey + 'A'], out=tmpA, in0=xin, in1=cos, op=ALU.mult)
        for f in range(2):
            X(eng2, 'tensor_tensor', [xkey, ckey], [tkey + 'B'], out=v5(tmpB)[:, :, :, f, :],
              in0=v5(xin)[:, :, :, 1 - f, :], in1=v5(sin)[:, :, :, f, :], op=ALU.mult)
        X('dve', 'tensor_tensor', [tkey + 'A', tkey + 'B'], [okey], out=out_bf, in0=tmpA, in1=tmpB, op=ALU.add)

    def prenorm_inproj(hsrc, layer, w_dram, ncols):
        ph = Phase()
        hnT = ph.alloc([128, 8, T], BF16)
        wt = [ph.alloc([128, 8, 512], BF16) for _ in range(2)]
        K = ph.key
        DMA('sp', gpre[:], prm['pre_gain'][layer:layer + 1, :].partition_broadcast(128), [], ['gpre'])
        for tt in range(NT):
            b = tt % 2
            DMA('sp', hx[b][:], hsrc[tt * 128:(tt + 1) * 128, :], ['h%d' % tt], ['hx%d' % b])
            rms_rows(['hx%d' % b], hx[b][:], b, D, EPS)
            X('dve', 'scalar_tensor_tensor', ['hx%d' % b, 'stt%d' % b, 'gpre'], ['hnb%d' % b], out=hnb[b][:, 0:D],
              in0=hx[b][:], scalar=stt[b][:, 3:4], in1=gpre[:], op0=ALU.mult, op1=ALU.mult)
            transpose_tile(hnb[b], 'hnb%d' % b, 8,
                           lambda c0, n, tt=tt: hnT[:, c0:c0 + n, tt * 128:(tt + 1) * 128], K('hnT%d' % tt))
        for cb in range(ncols // 512):
            wb = cb % 2
            DMA('pool', wt[wb], w_dram[:, cb * 512:(cb + 1) * 512].rearrange("(kc p) n -> p kc n", p=128),
                [], [K('wt%d' % wb)])
            for tt in range(NT):
                pb = (cb * NT + tt) % 2
                for kc in range(8):
                    X('pe', 'matmul', [K('hnT%d' % tt), K('wt%d' % wb)], ['pa%d' % pb], out=pa[pb][:],
                      lhsT=hnT[:, kc, tt * 128:(tt + 1) * 128], rhs=wt[wb][:, kc, :], start=(kc == 0), stop=(kc == 7))
                e = next_ev()
                if tt % 2 == 0:
                    X('act', 'activation', ['pa%d' % pb], ['ev%d' % e], out=ev[e][:], in_=pa[pb][:], func=AF.Copy)
                else:
                    X('dve', 'tensor_copy', ['pa%d' % pb], ['ev%d' % e], out=ev[e][:], in_=pa[pb][:])
                DMA('sp', proj[tt * 128:(tt + 1) * 128, cb * 512:(cb + 1) * 512], ev[e][:], ['ev%d' % e],
                    ['proj%d_%d' % (tt, cb)])

    def out_proj(w_dram, Kdim, hsrc, hdst, layer):
        ph = Phase()
        KC = Kdim // 128
        wout = ph.alloc([128, KC, 1024], BF16)
        xT = [ph.alloc([128, KC, 128], BF16) for _ in range(2)]
        K = ph.key
        DMA('sp', gpost[:], prm['post_gain'][layer:layer + 1, :].partition_broadcast(128), [], ['gpost'])
        for k4 in range(0, KC, 4):
            DMA('pool', wout[:, k4:k4 + 4, :],
                w_dram[k4 * 128:(k4 + 4) * 128, :].rearrange("(kc p) n -> p kc n", p=128), [], [K('wout')])
        for tt in range(NT):
            b = tt % 2
            DMA('pool', hnb[b][:, 0:Kdim], mixd[tt * 128:(tt + 1) * 128, 0:Kdim], ['mixd%d' % tt], ['hnb%d' % b])
            transpose_tile(hnb[b], 'hnb%d' % b, KC, lambda c0, n, b=b: xT[b][:, c0:c0 + n, :], K('xT%d' % b))
            DMA('sp', hx[b][:], hsrc[tt * 128:(tt + 1) * 128, :], ['h%d' % tt], ['hx%d' % b])
            for half in range(2):
                for kc in range(KC):
                    X('pe', 'matmul', [K('xT%d' % b), K('wout')], ['pa%d' % half], out=pa[half][:],
                      lhsT=xT[b][:, kc, :], rhs=wout[:, kc, half * 512:(half + 1) * 512], start=(kc == 0),
                      stop=(kc == KC - 1))
            X('act', 'activation', ['pa0'], ['junk', 'stt%d' % b], out=junk[:, 0:512], in_=pa[0][:], func=AF.Square,
              accum_out=stt[b][:, 4:5])
            X('act', 'activation', ['pa1'], ['junk', 'stt%d' % b], out=junk[:, 512:1024], in_=pa[1][:], func=AF.Square,
              accum_out=stt[b][:, 5:6])
            X('dve', 'tensor_tensor', ['stt%d' % b], ['stt%d' % b], out=stt[b][:, 0:1], in0=stt[b][:, 4:5],
              in1=stt[b][:, 5:6], op=ALU.add)
            X('dve', 'tensor_scalar', ['stt%d' % b], ['stt%d' % b], out=stt[b][:, 1:2], in0=stt[b][:, 0:1],
              scalar1=1.0 / D, scalar2=EPS, op0=ALU.mult, op1=ALU.add)
            X('act', 'activation', ['stt%d' % b], ['stt%d' % b], out=stt[b][:, 2:3], in_=stt[b][:, 1:2], func=AF.Sqrt)
            X('dve', 'reciprocal', ['stt%d' % b], ['stt%d' % b], out=stt[b][:, 3:4], in_=stt[b][:, 2:3])
            for half in range(2):
                e = next_ev()
                X('dve', 'scalar_tensor_tensor', ['pa%d' % half, 'stt%d' % b, 'gpost'], ['ev%d' % e], out=ev[e][:],
                  in0=pa[half][:], scalar=stt[b][:, 3:4], in1=gpost[:, half * 512:(half + 1) * 512], op0=ALU.mult,
                  op1=ALU.mult)
                X('pool', 'tensor_tensor', ['ev%d' % e, 'hx%d' % b], ['ev%d' % e], out=ev[e][:], in0=ev[e][:],
                  in1=hx[b][:, half * 512:(half + 1) * 512], op=ALU.add)
                DMA('sp', hdst[tt * 128:(tt + 1) * 128, half * 512:(half + 1) * 512], ev[e][:], ['ev%d' % e],
                    ['h%d' % tt])

    def odd_mixer(layer):
        j = layer // 2
        ph = Phase()
        K = ph.key
        qkT = ph.alloc([128, 4, T], BF16)
        vS = ph.alloc([128, NT, 512], BF16)
        rdec = ph.alloc([128, 3968], BF16)
        pT = [ph.alloc([128, 512], BF16) for _ in range(2)]
        gng = ph.alloc([128, 2048], F32)
        rq = [ph.alloc([128, 512], F32) for _ in range(2)]
        ct = [ph.alloc([128, 512], F32) for _ in range(2)]
        sn = [ph.alloc([128, 512], F32) for _ in range(2)]
        tA = [ph.alloc([128, 512], F32) for _ in range(2)]
        tB = [ph.alloc([128, 512], F32) for _ in range(2)]
        rb = [ph.alloc([128, 512], BF16) for _ in range(2)]
        ob = [ph.alloc([128, 512], F32) for _ in range(2)]
        gt = [ph.alloc([128, 512], F32) for _ in range(2)]
        bst = [ph.alloc([128, 8], F32) for _ in range(2)]
        DMA('sp', gng, prm['odd_gn_g'][j:j + 1, :].partition_broadcast(128), [], [K('gng')])
        for h in range(4):
            DMA('pool', rdec, cst['retdec'][:, h * 3968:(h + 1) * 3968], [], [K('rdec')])
            for tt in range(NT):
                b = tt % 2
                r0 = slice(tt * 128, (tt + 1) * 128)
                DMA('sp', rq[b][:, 0:256], proj[r0, h * 256:(h + 1) * 256], ['proj%d_%d' % (tt, h // 2)], [K('rq%d' % b)])
                DMA('sp', rq[b][:, 256:512], proj[r0, 1024 + h * 256:1024 + (h + 1) * 256],
                    ['proj%d_%d' % (tt, 2 + h // 2)], [K('rq%d' % b)])
                DMA('sp', ct[b], cst['cosC'][r0, :], [], [K('cs%d' % b)])
                DMA('sp', sn[b], cst['sinC'][r0, :], [], [K('cs%d' % b)])
                rope('pool', rq[b], K('rq%d' % b), ct[b], sn[b], K('cs%d' % b), tA[b], tB[b], K('t%d' % b), rb[b],
                     K('rb%d' % b), 2, 256)
                transpose_tile(rb[b], K('rb%d' % b), 4, lambda c0, n, tt=tt: qkT[:, 0:4, tt * 128:(tt + 1) * 128],
                               K('qkT%d' % tt))
            for t4 in range(0, NT, 4):
                DMA('pool', vS[:, t4:t4 + 4, :],
                    proj[t4 * 128:(t4 + 4) * 128, 2048 + h * 512:2048 + (h + 1) * 512].rearrange("(t p) n -> p t n", p=128),
                    ['proj%d_%d' % (tt, 4 + h) for tt in range(t4, t4 + 4)], [K('vS%d' % tt) for tt in range(t4, t4 + 4)])
            it = 0
            for cn4 in range(4):
                qkeys = [K('qkT%d' % tt) for tt in range(cn4 * 4, cn4 * 4 + 4)]
                for cm in range(NT):
                    s = it % 2
                    it += 1
                    for dc in range(2):
                        X('pe', 'matmul', qkeys + [K('qkT%d' % cm)], ['pa%d' % s], out=pa[s][:],
                          lhsT=qkT[:, 2 + dc, cm * 128:(cm + 1) * 128], rhs=qkT[:, dc, cn4 * 512:(cn4 + 1) * 512],
                          start=(dc == 0), stop=(dc == 1))
                    off = cn4 * 512 - cm * 128 + 1920
                    X('dve', 'tensor_tensor', ['pa%d' % s, K('rdec')], [K('pT%d' % s)], out=pT[s], in0=pa[s][:],
                      in1=rdec[:, off:off + 512], op=ALU.mult)
                    for qs in range(4):
                        X('pe', 'matmul', [K('pT%d' % s), K('vS%d' % cm)], ['pa%d' % (2 + qs)], out=pa[2 + qs][:],
                          lhsT=pT[s][:, qs * 128:(qs + 1) * 128], rhs=vS[:, cm, :], start=(cm == 0), stop=(cm == NT - 1))
                for qs in range(4):
                    tt = cn4 * 4 + qs
                    b = qs % 2
                    r0 = slice(tt * 128, (tt + 1) * 128)
                    X('act', 'activation', ['pa%d' % (2 + qs)], [K('ob%d' % b)], out=ob[b], in_=pa[2 + qs][:], func=AF.Copy)
                    DMA('sp', gt[b], proj[r0, 4096 + h * 512:4096 + (h + 1) * 512], ['proj%d_%d' % (tt, 8 + h)], [K('gt%d' % b)])
                    X('act', 'activation', [K('gt%d' % b)], [K('gt%d' % b)], out=gt[b], in_=gt[b], func=AF.Silu)
                    X('act', 'activation', [K('ob%d' % b)], ['junk', K('bst%d' % b)], out=junk[:, 0:512], in_=ob[b],
                      func=AF.Square, accum_out=bst[b][:, 0:1])
                    X('dve', 'tensor_reduce', [K('ob%d' % b)], [K('bst%d' % b)], out=bst[b][:, 1:2], in_=ob[b], axis=AX.X,
                      op=ALU.add)
                    X('dve', 'tensor_scalar', [K('bst%d' % b)], [K('bst%d' % b)], out=bst[b][:, 2:3], in0=bst[b][:, 1:2],
                      scalar1=1.0 / 512, scalar2=None, op0=ALU.mult)
                    X('dve', 'tensor_tensor', [K('bst%d' % b)], [K('bst%d' % b)], out=bst[b][:, 3:4], in0=bst[b][:, 2:3],
                      in1=bst[b][:, 2:3], op=ALU.mult)
                    X('dve', 'scalar_tensor_tensor', [K('bst%d' % b)], [K('bst%d' % b)], out=bst[b][:, 4:5],
                      in0=bst[b][:, 0:1], scalar=1.0 / 512, in1=bst[b][:, 3:4], op0=ALU.mult, op1=ALU.subtract)
                    X('dve', 'tensor_scalar', [K('bst%d' % b)], [K('bst%d' % b)], out=bst[b][:, 4:5], in0=bst[b][:, 4:5],
                      scalar1=1e-5, scalar2=None, op0=ALU.add)
                    X('act', 'activation', [K('bst%d' % b)], [K('bst%d' % b)], out=bst[b][:, 5:6], in_=bst[b][:, 4:5],
                      func=AF.Sqrt)
                    X('dve', 'reciprocal', [K('bst%d' % b)], [K('bst%d' % b)], out=bst[b][:, 6:7], in_=bst[b][:, 5:6])
                    X('dve', 'tensor_scalar', [K('ob%d' % b), K('bst%d' % b)], [K('ob%d' % b)], out=ob[b], in0=ob[b],
                      scalar1=bst[b][:, 2:3], scalar2=bst[b][:, 6:7], op0=ALU.subtract, op1=ALU.mult)
                    X('pool', 'tensor_tensor', [K('ob%d' % b), K('gng')], [K('ob%d' % b)], out=ob[b], in0=ob[b],
                      in1=gng[:, h * 512:(h + 1) * 512], op=ALU.mult)
                    X('pool', 'tensor_tensor', [K('ob%d' % b), K('gt%d' % b)], [K('ob%d' % b)], out=ob[b], in0=ob[b],
                      in1=gt[b], op=ALU.mult)
                    DMA('sp', mixd[r0, h * 512:(h + 1) * 512], ob[b], [K('ob%d' % b)], ['mixd%d' % tt])


    def bcast_row(ph, key, src_row, width, eng='sp'):
        t = ph.alloc([128, width], F32)
        DMA(eng, t, src_row.partition_broadcast(128), [], [ph.key(key)])
        return t

    def zero_mix(c0, c1):
        X('dve', 'memset', [], ['ev0'], ap=ev[0][:], constant=0.0)
        for tt in range(NT):
            DMA('sp', mixd[tt * 128:(tt + 1) * 128, c0:c1], ev[0][:, 0:c1 - c0], ['ev0'], ['mixd%d' % tt])

    def attn_part(i):
        ph = Phase()
        K = ph.key
        qT6 = ph.alloc([128, 6, T], BF16)
        vA = ph.alloc([128, NT, 2, 66], BF16)
        outA = ph.alloc([128, NT, 512], F32)
        pT = [ph.alloc([128, 512], BF16) for _ in range(2)]
        qg = ph.alloc([128, 640], F32)
        aq = [ph.alloc([128, 640], F32) for _ in range(2)]
        sq = [ph.alloc([128, 640], F32) for _ in range(2)]
        ss = [ph.alloc([128, 32], F32) for _ in range(2)]
        ct = [ph.alloc([128, 640], F32) for _ in range(2)]
        sn = [ph.alloc([128, 640], F32) for _ in range(2)]
        tA = [ph.alloc([128, 640], F32) for _ in range(2)]
        tB = [ph.alloc([128, 640], F32) for _ in range(2)]
        rb = [ph.alloc([128, 768], BF16) for _ in range(2)]
        gt = [ph.alloc([128, 512], F32) for _ in range(2)]
        rs = [ph.alloc([128, 4], F32) for _ in range(4)]
        for h in range(8):
            DMA('sp', qg[:, h * 64:(h + 1) * 64], prm['even_q_gain'][i:i + 1, :].partition_broadcast(128), [], [K('qg')])
        for h in range(2):
            DMA('sp', qg[:, 512 + h * 64:512 + (h + 1) * 64], prm['even_k_gain'][i:i + 1, :].partition_broadcast(128), [],
                [K('qg')])
        X('pool', 'memset', [], [K('vA')], ap=vA.rearrange("p a b c -> p (a b c)"), constant=1.0)
        for t4 in range(0, NT, 4):
            for g in range(2):
                DMA('pool', vA[:, t4:t4 + 4, g, 0:64],
                    proj[t4 * 128:(t4 + 4) * 128, 640 + g * 64:640 + (g + 1) * 64].rearrange("(t p) d -> p t d", p=128),
                    ['proj%d_1' % tt for tt in range(t4, t4 + 4)], [K('vA')])
        v3 = lambda a: a.rearrange("p (h d) -> p h d", d=64)
        for tt in range(NT):
            b = tt % 2
            r0 = slice(tt * 128, (tt + 1) * 128)
            DMA('sp', aq[b], proj[r0, 0:640], ['proj%d_0' % tt, 'proj%d_1' % tt], [K('aq%d' % b)])
            DMA('sp', ct[b], cst['cosA'][r0, :], [], [K('cs%d' % b)])
            DMA('sp', sn[b], cst['sinA'][r0, :], [], [K('cs%d' % b)])
            X('pool', 'tensor_tensor', [K('aq%d' % b)], [K('sq%d' % b)], out=sq[b], in0=aq[b], in1=aq[b], op=ALU.mult)
            X('dve', 'tensor_reduce', [K('sq%d' % b)], [K('ss%d' % b)], out=ss[b][:, 0:10], in_=v3(sq[b]), axis=AX.X, op=ALU.add)
            X('dve', 'tensor_scalar', [K('ss%d' % b)], [K('ss%d' % b)], out=ss[b][:, 10:20], in0=ss[b][:, 0:10],
              scalar1=1.0 / 64, scalar2=EPS, op0=ALU.mult, op1=ALU.add)
            X('act', 'activation', [K('ss%d' % b)], [K('ss%d' % b)], out=ss[b][:, 20:30], in_=ss[b][:, 10:20], func=AF.Sqrt)
            X('dve', 'reciprocal', [K('ss%d' % b)], [K('ss%d' % b)], out=ss[b][:, 0:10], in_=ss[b][:, 20:30])
            X('dve', 'tensor_tensor', [K('aq%d' % b), K('ss%d' % b)], [K('aq%d' % b)], out=v3(aq[b]), in0=v3(aq[b]),
              in1=ss[b][:, 0:10].unsqueeze(2).to_broadcast([128, 10, 64]), op=ALU.mult)
            X('pool', 'tensor_tensor', [K('aq%d' % b), K('qg')], [K('aq%d' % b)], out=aq[b], in0=aq[b], in1=qg, op=ALU.mult)
            rope('pool', aq[b], K('aq%d' % b), ct[b], sn[b], K('cs%d' % b), tA[b], tB[b], K('t%d' % b), rb[b][:, 0:640],
                 K('rb%d' % b), 10, 64)
            X('pool', 'tensor_copy', [K('rb%d' % b)], [K('rb%d' % b)], out=rb[b][:, 640:704], in_=rb[b][:, 576:640])
            X('pool', 'tensor_copy', [K('rb%d' % b)], [K('rb%d' % b)], out=rb[b][:, 704:768], in_=rb[b][:, 512:576])
            transpose_tile(rb[b], K('rb%d' % b), 6, lambda c0, n, tt=tt: qT6[:, c0:c0 + n, tt * 128:(tt + 1) * 128],
                           K('qT%d' % tt))
        it = 0
        for hq in range(8):
            g = hq // 4
            base = (hq % 2) * 64
            pair = hq // 2
            kc = 4 + (1 if g != base // 64 else 0)
            for cn4 in range(4):
                qkeys = [K('qT%d' % tt) for tt in range(cn4 * 4, cn4 * 4 + 4)]
                for cm in range(NT):
                    s_ = it % 2
                    it += 1
                    X('pe', 'matmul', qkeys + [K('qT%d' % cm)], ['pa%d' % s_], out=pa[s_][:],
                      lhsT=qT6[base:base + 64, kc, cm * 128:(cm + 1) * 128],
                      rhs=qT6[base:base + 64, pair, cn4 * 512:(cn4 + 1) * 512], start=True, stop=True)
                    X('act', 'activation', ['pa%d' % s_], [K('pT%d' % s_)], out=pT[s_], in_=pa[s_][:], func=AF.Exp, scale=0.125)
                    for qs in range(4):
                        X('pe', 'matmul', [K('pT%d' % s_), K('vA')], ['pa%d' % (2 + qs)], out=pa[2 + qs][:, 0:65],
                          lhsT=pT[s_][:, qs * 128:(qs + 1) * 128], rhs=vA[:, cm, g, 0:65], start=(cm == 0), stop=(cm == NT - 1))
                for qs in range(4):
                    tt = cn4 * 4 + qs
                    X('dve', 'reciprocal', ['pa%d' % (2 + qs)], [K('rs%d' % qs)], out=rs[qs][:, 0:1], in_=pa[2 + qs][:, 64:65])
                    X('dve', 'tensor_scalar', ['pa%d' % (2 + qs), K('rs%d' % qs)], [K('outA%d' % tt)],
                      out=outA[:, tt, hq * 64:(hq + 1) * 64], in0=pa[2 + qs][:, 0:64], scalar1=rs[qs][:, 0:1], scalar2=None,
                      op0=ALU.mult)
        for tt in range(NT):
            b = tt % 2
            r0 = slice(tt * 128, (tt + 1) * 128)
            DMA('sp', gt[b], proj[r0, 768:1280], ['proj%d_1' % tt, 'proj%d_2' % tt], [K('gt%d' % b)])
            X('act', 'activation', [K('gt%d' % b)], [K('gt%d' % b)], out=gt[b], in_=gt[b], func=AF.Silu)
            X('pool', 'tensor_tensor', [K('gt%d' % b), K('outA%d' % tt)], [K('gt%d' % b)], out=gt[b], in0=gt[b],
              in1=outA[:, tt, :], op=ALU.mult)
            DMA('sp', mixd[r0, 0:512], gt[b], [K('gt%d' % b)], ['mixd%d' % tt])

    def rwkv_part(i):
        ph = Phase()
        K = ph.key
        MU = bcast_row(ph, 'MU', prm['even_mu'][i:i + 1, :], 1792)
        s0 = [ph.alloc([128, 1792], F32) for _ in range(2)]
        sm = [ph.alloc([128, 1792], F32) for _ in range(2)]
        sp_ = [ph.alloc([128, 1792], F32) for _ in range(2)]
        pk = lambda tt: ['proj%d_%d' % (tt, cb) for cb in range(2, 6)]
        for tt in range(NT):
            b = tt % 2
            r0 = tt * 128
            DMA('sp', s0[b], proj[r0:r0 + 128, 1280:3072], pk(tt), [K('s0%d' % b)])
            if tt == 0:
                X('pool', 'memset', [], [K('sm%d' % b)], ap=sm[b], constant=0.0)
                DMA('sp', sm[b][1:128, :], proj[0:127, 1280:3072], pk(tt), [K('sm%d' % b)])
            else:
                DMA('sp', sm[b], proj[r0 - 1:r0 + 127, 1280:3072], pk(tt) + pk(tt - 1), [K('sm%d' % b)])
            if tt == NT - 1:
                X('pool', 'memset', [], [K('sp%d' % b)], ap=sp_[b], constant=0.0)
                DMA('sp', sp_[b][0:127, :], proj[r0 + 1:r0 + 128, 1280:3072], pk(tt), [K('sp%d' % b)])
            else:
                DMA('sp', sp_[b], proj[r0 + 1:r0 + 129, 1280:3072], pk(tt) + pk(tt + 1), [K('sp%d' % b)])
            X('pool', 'tensor_tensor', [K('sm%d' % b), K('sp%d' % b)], [K('sm%d' % b)], out=sm[b], in0=sm[b], in1=sp_[b],
              op=ALU.add)
            X('dve', 'scalar_tensor_tensor', [K('sm%d' % b), K('s0%d' % b)], [K('sm%d' % b)], out=sm[b], in0=sm[b],
              scalar=0.5, in1=s0[b], op0=ALU.mult, op1=ALU.subtract)
            X('pool', 'tensor_tensor', [K('sm%d' % b), K('MU')], [K('sm%d' % b)], out=sm[b], in0=sm[b], in1=MU, op=ALU.mult)
            X('dve', 'tensor_tensor', [K('sm%d' % b), K('s0%d' % b)], [K('sm%d' % b)], out=sm[b], in0=sm[b], in1=s0[b],
              op=ALU.add)
            DMA('sp', ubd[r0:r0 + 128, :], sm[b], [K('sm%d' % b)], ['ub%d' % tt])

        ph = Phase()
        K = ph.key
        Yacc = ph.alloc([128, NT, 512], F32)
        KKB = bcast_row(ph, 'KKB', prm['even_k_k'][i:i + 1, :], 512)
        KAB = bcast_row(ph, 'KAB', prm['even_k_a'][i:i + 1, :], 512)
        RKB = bcast_row(ph, 'RKB', prm['even_r_k'][i:i + 1, :], 512)
        LNG = bcast_row(ph, 'LNG', prm['even_lnx_g'][i:i + 1, :], 512)
        LNB = bcast_row(ph, 'LNB', prm['even_lnx_b'][i:i + 1, :], 512)
        W0 = [bcast_row(ph, 'W0%d' % d, prm['even_w0_' + 'fb'[d]][i:i + 1, :], 512) for d in range(2)]
        A0 = [bcast_row(ph, 'A0%d' % d, prm['even_a0_' + 'fb'[d]][i:i + 1, :], 512) for d in range(2)]
        NEGB = ph.alloc([128, 512], F32)
        X('dve', 'memset', [], [K('NEGB')], ap=NEGB, constant=-1.0)
        mark = ph.off
        W2A2 = [ph.alloc([128, 512], BF16) for _ in range(2)]
        for d in range(2):
            DMA('pool', W2A2[d][0:64, :], prm['even_w2_' + 'fb'[d]][i], [], [K('W2A2%d' % d)])
            DMA('pool', W2A2[d][64:128, :], prm['even_a2_' + 'fb'[d]][i], [], [K('W2A2%d' % d)])
        A_ = lambda shape, dt: [ph.alloc(shape, dt) for _ in range(2)]
        u = A_([128, 1792], F32)
        lo = A_([128, 128], BF16)
        loT = A_([128, 128], BF16)
        sgm = A_([128, 512], F32)
        alp = A_([128, 512], F32)
        Epl = A_([128, 512], F32)
        Emi = A_([128, 512], F32)
        Epr = A_([128, 512], F32)
        kk = A_([128, 512], F32)
        kx = A_([128, 512], F32)
        bb = A_([128, 512], F32)
        tm = A_([128, 512], F32)
        ss8 = A_([128, 32], F32)
        Kt = A_([128, 512], BF16)
        Bt = A_([128, 512], BF16)
        Vt = A_([128, 512], BF16)
        ARs = A_([128, 2, 512], BF16)
        ARt = A_([128, 4, 2, 128], BF16)
        KBt = A_([128, 4, 2, 128], BF16)
        PC = A_([128, 4], F32)
        SC = A_([128, 8, 512], BF16)
        TI = A_([128, 8, 128], BF16)
        L0 = [ph.alloc([128, 128], BF16) for _ in range(4)]
        Wk = [[ph.alloc([128, 384], BF16) for _ in range(2)] for _ in range(4)]
        Xb = A_([128, 4, 128], BF16)
        Ub = A_([128, 4, 128], BF16)
        H32 = A_([128, 4, 128], F32)
        Hbf = A_([128, 4, 128], BF16)
        t32 = A_([128, 128], F32)
        mk4 = A_([128, 512], BF16)
        for d in range(2):
            for rep in range(2):
                X('pool', 'tensor_copy', ['msk'], [K('mk4%d' % d)], out=mk4[d][:, rep * 256:(rep + 1) * 256],
                  in_=msk[:, 2 * d:2 * d + 2, :].rearrange("p a b -> p (a b)"))
            X('pool', 'memset', [], [K('H32%d' % d)], ap=H32[d].rearrange("p a b -> p (a b)"), constant=0.0)
            X('pool', 'memset', [], [K('Hbf%d' % d)], ap=Hbf[d].rearrange("p a b -> p (a b)"), constant=0.0)
        X('pool', 'memset', [], [K('Yacc%d' % tt) for tt in range(NT)], ap=Yacc.rearrange("p a b -> p (a b)"), constant=0.0)
        v8 = lambda a: a.rearrange("p (h d) -> p h d", d=64)

        PS = float(os.environ.get('PREP_STAGE', '9'))

        def prep(d, tt):
            kd = lambda n: K('%s%d' % (n, d))
            r0 = tt * 128
            DMA('sp', u[d], ubd[r0:r0 + 128, :], ['ub%d' % tt], [kd('u')])
            r_ = u[d][:, 0:512]
            kb_ = u[d][:, 512:1024]
            vb_ = u[d][:, 1024:1536]
            X('act', 'activation', [kd('u')], [kd('lo')], out=lo[d][:, 0:64], in_=u[d][:, 1536 + d * 64:1600 + d * 64],
              func=AF.Tanh)
            X('dve', 'tensor_copy', [kd('u')], [kd('lo')], out=lo[d][:, 64:128], in_=u[d][:, 1664 + d * 64:1728 + d * 64])
            pb = next_ptr()
            X('pe', 'transpose', [kd('lo'), 'idb'], ['ptr%d' % pb], out=ptr[pb][:, 0:128], in_=lo[d], identity=idb[:])
            X('act', 'activation', ['ptr%d' % pb], [kd('loT')], out=loT[d], in_=ptr[pb][:, 0:128], func=AF.Copy)
            X('pe', 'matmul', [kd('loT'), K('W2A2%d' % d)], ['pa0'], out=pa[0][:], lhsT=loT[d][0:64, :],
              rhs=W2A2[d][0:64, :], start=True, stop=True)
            X('pe', 'matmul', [kd('loT'), K('W2A2%d' % d)], ['pa1'], out=pa[1][:], lhsT=loT[d][64:128, :],
              rhs=W2A2[d][64:128, :], start=True, stop=True)
            X('dve', 'tensor_tensor', ['pa0', K('W0%d' % d)], [kd('sgm')], out=sgm[d], in0=pa[0][:], in1=W0[d], op=ALU.add)
            X('act', 'activation', [kd('sgm')], [kd('sgm')], out=sgm[d], in_=sgm[d], func=AF.Sigmoid)
            X('dve', 'tensor_tensor', ['pa1', K('A0%d' % d)], [kd('alp')], out=alp[d], in0=pa[1][:], in1=A0[d], op=ALU.add)
            X('act', 'activation', [kd('alp')], [kd('alp')], out=alp[d], in_=alp[d], func=AF.Sigmoid)
            if PS < 2:
                return
            X('pe', 'matmul', [kd('sgm'), 'tri'], ['pa0'], out=pa[0][:], lhsT=tri[:, 2 * d + 1, :], rhs=sgm[d], start=True,
              stop=True)
            X('pe', 'matmul', [kd('sgm'), 'tri'], ['pa1'], out=pa[1][:], lhsT=tri[:, 2 * d, :], rhs=sgm[d], start=True,
              stop=True)
            X('act', 'activation', ['pa0'], [kd('Epl')], out=Epl[d], in_=pa[0][:], func=AF.Exp)
            X('act', 'activation', ['pa0'], [kd('Emi')], out=Emi[d], in_=pa[0][:], func=AF.Exp, scale=-1.0)
            X('act', 'activation', ['pa1'], [kd('Epr')], out=Epr[d], in_=pa[1][:], func=AF.Exp)
            if PS < 3:
                return
            for p_ in range(4):
                X('pe', 'matmul', [kd('sgm'), 'negcol'], ['pa0'], out=pa[0][:, p_:p_ + 1],
                  lhsT=sgm[d][:, p_ * 128:(p_ + 1) * 128], rhs=negcol[:, 0:1], start=True, stop=True)
            X('act', 'activation', ['pa0'], [kd('PC')], out=PC[d], in_=pa[0][:, 0:4], func=AF.Exp)
            if PS < 4:
                return
            X('pool', 'tensor_tensor', [kd('u'), K('KKB')], [kd('kk')], out=kk[d], in0=kb_, in1=KKB, op=ALU.mult)
            X('pool', 'tensor_tensor', [kd('kk')], [kd('tm')], out=tm[d], in0=kk[d], in1=kk[d], op=ALU.mult)
            if PS < 4.2:
                return
            X('dve', 'tensor_reduce', [kd('tm')], [kd('ss8')], out=ss8[d][:, 0:8], in_=v8(tm[d]), axis=AX.X, op=ALU.add)
            X('act', 'activation', [kd('ss8')], [kd('ss8')], out=ss8[d][:, 8:16], in_=ss8[d][:, 0:8], func=AF.Sqrt)
            X('dve', 'tensor_scalar', [kd('ss8')], [kd('ss8')], out=ss8[d][:, 8:16], in0=ss8[d][:, 8:16], scalar1=1e-12,
              scalar2=None, op0=ALU.max)
            X('dve', 'reciprocal', [kd('ss8')], [kd('ss8')], out=ss8[d][:, 16:24], in_=ss8[d][:, 8:16])
            if PS < 4.4:
                return
            X('pool', 'tensor_tensor', [kd('kk'), kd('ss8')], [kd('kk')], out=v8(kk[d]), in0=v8(kk[d]),
              in1=ss8[d][:, 16:24].unsqueeze(2).to_broadcast([128, 8, 64]), op=ALU.mult)
            if PS < 4.6:
                return
            X('dve', 'tensor_tensor', [kd('alp'), K('KAB')], [kd('kx')], out=kx[d], in0=alp[d], in1=KAB, op=ALU.mult)
            X('dve', 'tensor_tensor', [kd('kx'), K('KAB')], [kd('kx')], out=kx[d], in0=kx[d], in1=KAB, op=ALU.subtract)
            X('dve', 'tensor_tensor', [kd('kx'), kd('u')], [kd('kx')], out=kx[d], in0=kx[d], in1=kb_, op=ALU.mult)
            X('dve', 'tensor_tensor', [kd('kx'), kd('u')], [kd('kx')], out=kx[d], in0=kx[d], in1=kb_, op=ALU.add)
            X('pool', 'tensor_tensor', [kd('kk'), kd('alp')], [kd('bb')], out=bb[d], in0=kk[d], in1=alp[d], op=ALU.mult)
            if PS < 5:
                return
            v4 = lambda a: a.rearrange("p (a c) -> p a c", a=4)
            X('dve', 'tensor_tensor', [kd('kk'), kd('Epr')], [kd('tm')], out=tm[d], in0=kk[d], in1=Epr[d], op=ALU.mult)
            X('dve', 'tensor_tensor', [kd('tm'), K('NEGB')], [kd('ARs')], out=ARs[d][:, 0, :], in0=tm[d], in1=NEGB,
              op=ALU.mult)
            X('dve', 'tensor_tensor', [kd('u'), kd('Epl')], [kd('ARs')], out=ARs[d][:, 1, :], in0=r_, in1=Epl[d],
              op=ALU.mult)
            X('dve', 'tensor_tensor', [kd('kx'), kd('Emi')], [kd('Kt')], out=Kt[d], in0=kx[d], in1=Emi[d], op=ALU.mult)
            X('dve', 'tensor_tensor', [kd('bb'), kd('Emi')], [kd('Bt')], out=Bt[d], in0=bb[d], in1=Emi[d], op=ALU.mult)
            X('dve', 'tensor_copy', [kd('u')], [kd('Vt')], out=Vt[d], in_=vb_)
            if dbg and os.environ.get('DBG_LIST'):
                lst = [tuple(int(v) for v in it.split(':')) for it in os.environ['DBG_LIST'].split(',')]
                if (d, tt) in lst:
                    qi = lst.index((d, tt))
                    DMA('sp', dbgo[:, qi * 512:(qi + 1) * 512], Emi[d], [kd('Emi')], ['dbgo%d' % qi])
            if PS < 6:
                return
            for p_ in range(4):
                pb = next_ptr()
                X('pe', 'transpose', [kd('ARs'), 'idb'], ['ptr%d' % pb], out=ptr[pb][:, 0:128], in_=ARs[d][:, 0, p_ * 128:(p_ + 1) * 128],
                  identity=idb[:])
                X('pe', 'transpose', [kd('ARs'), 'idb'], ['ptr%d' % pb], out=ptr[pb][:, 128:256], in_=ARs[d][:, 1, p_ * 128:(p_ + 1) * 128],
                  identity=idb[:])
                X('act', 'activation', ['ptr%d' % pb], [kd('ARt')], out=ARt[d][:, p_, :, :].rearrange("p x c -> p (x c)"),
                  in_=ptr[pb][:, 0:256], func=AF.Copy)
                pb = next_ptr()
                X('pe', 'transpose', [kd('Kt'), 'idb'], ['ptr%d' % pb], out=ptr[pb][:, 0:128],
                  in_=Kt[d][:, p_ * 128:(p_ + 1) * 128], identity=idb[:])
                X('pe', 'transpose', [kd('Bt'), 'idb'], ['ptr%d' % pb], out=ptr[pb][:, 128:256],
                  in_=Bt[d][:, p_ * 128:(p_ + 1) * 128], identity=idb[:])
                X('act', 'activation', ['ptr%d' % pb], [kd('KTt'), kd('BTt')],
                  out=KBt[d][:, p_, :, :].rearrange("p x c -> p (x c)"), in_=ptr[pb][:, 0:256], func=AF.Copy)

        def head_gen(d, h, slot):
            kd = lambda n: K('%s%d' % (n, d))
            par = h % 2
            base = par * 64
            p_ = h // 2
            z = 2 + slot
            ev_eng = 'act' if slot < 2 else 'dve'

            def evac(reads, writes, out, in_):
                if ev_eng == 'act':
                    X('act', 'activation', reads, writes, out=out, in_=in_, func=AF.Copy)
                else:
                    X('dve', 'tensor_copy', reads, writes, out=out, in_=in_)
            arf = ARt[d][base:base + 64, p_, :, :].rearrange("p x c -> p (x c)")
            aT = ARt[d][base:base + 64, p_, 0, :]
            X('pe', 'matmul', [kd('BTt'), kd('ARt')], ['pa0'], out=pa[0][:, 0:256], lhsT=KBt[d][base:base + 64, p_, 1, :],
              rhs=arf, start=True, stop=True)
            X('pe', 'matmul', [kd('KTt'), kd('ARt')], ['pa0'], out=pa[0][:, 256:512], lhsT=KBt[d][base:base + 64, p_, 0, :],
              rhs=arf, start=True, stop=True)
            X('pe', 'matmul', [kd('BTt'), kd('ARt')], ['pa1'], out=pa[1][:, 0:128], lhsT=aT,
              rhs=KBt[d][base:base + 64, p_, 1, :], start=True, stop=True)
            sck = kd('SC%d_' % h)
            X('dve', 'tensor_tensor', ['pa0', K('mk4%d' % d)], [sck], out=SC[d][:, h, :], in0=pa[0][:], in1=mk4[d],
              op=ALU.mult)
            l0k = K('L0%d' % slot)
            X('dve', 'tensor_tensor', ['pa1', 'msk'], [l0k], out=L0[slot], in0=pa[1][:, 0:128], in1=msk[:, 2 - 2 * d, :],
              op=ALU.mult)
            yield
            N0 = SC[d][:, h, 0:128]
            wk = [K('Wk%d_%d' % (slot, q)) for q in range(2)]
            W = Wk[slot]
            X('pe', 'matmul', [sck, 'idb'], ['pa%d' % z], out=pa[z][:, 0:128], lhsT=idb[:], rhs=N0, start=True, stop=False)
            X('pe', 'matmul', [sck, 'idb'], ['pa%d' % z], out=pa[z][:, 0:128], lhsT=idb[:], rhs=idb[:], start=False, stop=True)
            X('pe', 'matmul', [l0k, sck], ['pa%d' % z], out=pa[z][:, 128:256], lhsT=L0[slot], rhs=N0, start=True, stop=True)
            X('pe', 'matmul', [l0k, sck], ['pa%d' % z], out=pa[z][:, 256:384], lhsT=N0, rhs=L0[slot], start=True, stop=True)
            evac(['pa%d' % z], [wk[0]], W[0][:, 0:384], pa[z][:, 0:384])
            yield
            cur = 0
            for k_ in range(1, 6):
                a_, b_ = cur, 1 - cur
                rk = [wk[a_], 'idb']
                X('pe', 'matmul', rk, ['pa%d' % z], out=pa[z][:, 0:128], lhsT=idb[:], rhs=W[a_][:, 0:128], start=True, stop=False)
                X('pe', 'matmul', rk, ['pa%d' % z], out=pa[z][:, 0:128], lhsT=W[a_][:, 256:384], rhs=W[a_][:, 0:128],
                  start=False, stop=True)
                X('pe', 'matmul', rk, ['pa%d' % z], out=pa[z][:, 128:256], lhsT=W[a_][:, 256:384], rhs=W[a_][:, 128:256],
                  start=True, stop=True)
                X('pe', 'matmul', rk, ['pa%d' % z], out=pa[z][:, 256:384], lhsT=W[a_][:, 128:256], rhs=W[a_][:, 256:384],
                  start=True, stop=True)
                evac(['pa%d' % z], [wk[b_]], W[b_][:, 0:384], pa[z][:, 0:384])
                cur = b_
                yield
            rk = [wk[cur], 'idb']
            X('pe', 'matmul', rk, ['pa%d' % z], out=pa[z][:, 0:128], lhsT=idb[:], rhs=W[cur][:, 0:128], start=True, stop=False)
            X('pe', 'matmul', rk, ['pa%d' % z], out=pa[z][:, 0:128], lhsT=W[cur][:, 256:384], rhs=W[cur][:, 0:128],
              start=False, stop=True)
            evac(['pa%d' % z], [kd('TI%d_' % h)], TI[d][:, h, :], pa[z][:, 0:128])
            yield

        def run_heads():
            combos = [(d, h) for h in range(8) for d in range(2)]
            for g0 in range(0, 16, 4):
                alive = [head_gen(d, h, slot) for slot, (d, h) in enumerate(combos[g0:g0 + 4])]
                while alive:
                    nxt = []
                    for g in alive:
                        try:
                            next(g)
                            nxt.append(g)
                        except StopIteration:
                            pass
                    alive = nxt

        def chain(d, tt):
            kd = lambda n: K('%s%d' % (n, d))
            for p_ in range(4):
                hk = kd('H%d_' % p_)
                for hh in range(2):
                    h = 2 * p_ + hh
                    base = hh * 64
                    cs = slice(hh * 64, hh * 64 + 64)
                    X('pe', 'matmul', [kd('SC%d_' % h), kd('Vt')], ['pa0'], out=pa[0][:, cs], lhsT=SC[d][:, h, 256:384],
                      rhs=Vt[d][:, h * 64:(h + 1) * 64], start=True, stop=False)
                    X('pe', 'matmul', [kd('ARt'), hk + 'b'], ['pa0'], out=pa[0][:, cs], lhsT=ARt[d][base:base + 64, p_, 0, :],
                      rhs=Hbf[d][base:base + 64, p_, base:base + 64], start=False, stop=True)
                X('act', 'activation', ['pa0'], [kd('Xb%d_' % p_)], out=Xb[d][:, p_, :], in_=pa[0][:, 0:128], func=AF.Copy)
                for hh in range(2):
                    h = 2 * p_ + hh
                    cs = slice(hh * 64, hh * 64 + 64)
                    X('pe', 'matmul', [kd('TI%d_' % h), kd('Xb%d_' % p_)], ['pa1'], out=pa[1][:, cs], lhsT=TI[d][:, h, :],
                      rhs=Xb[d][:, p_, cs], start=True, stop=True)
                X('dve', 'tensor_copy', ['pa1'], [kd('Ub%d_' % p_)], out=Ub[d][:, p_, :], in_=pa[1][:, 0:128])
                for hh in range(2):
                    h = 2 * p_ + hh
                    base = hh * 64
                    cs = slice(hh * 64, hh * 64 + 64)
                    ys = slice(128 + hh * 64, 128 + hh * 64 + 64)
                    X('pe', 'matmul', [kd('ARt'), hk + 'b'], ['pa0'], out=pa[0][:, ys], lhsT=ARt[d][base:base + 64, p_, 1, :],
                      rhs=Hbf[d][base:base + 64, p_, base:base + 64], start=True, stop=False)
                    X('pe', 'matmul', [kd('SC%d_' % h), kd('Vt')], ['pa0'], out=pa[0][:, ys], lhsT=SC[d][:, h, 384:512],
                      rhs=Vt[d][:, h * 64:(h + 1) * 64], start=False, stop=False)
                    X('pe', 'matmul', [kd('SC%d_' % h), kd('Ub%d_' % p_)], ['pa0'], out=pa[0][:, ys], lhsT=SC[d][:, h, 128:256],
                      rhs=Ub[d][:, p_, cs], start=False, stop=True)
                X('dve', 'tensor_tensor', ['pa0', K('Yacc%d' % tt)], [K('Yacc%d' % tt)], out=Yacc[:, tt, p_ * 128:(p_ + 1) * 128],
                  in0=pa[0][:, 128:256], in1=Yacc[:, tt, p_ * 128:(p_ + 1) * 128], op=ALU.add)
                ps_ = slice(p_ * 128, (p_ + 1) * 128)
                X('pe', 'matmul', [kd('Kt'), kd('Vt')], ['pa1'], out=pa[1][:, 128:256], lhsT=Kt[d][:, ps_], rhs=Vt[d][:, ps_],
                  start=True, stop=False)
                X('pe', 'matmul', [kd('Bt'), kd('Ub%d_' % p_)], ['pa1'], out=pa[1][:, 128:256], lhsT=Bt[d][:, ps_],
                  rhs=Ub[d][:, p_, :], start=False, stop=True)
                X('dve', 'tensor_tensor', ['pa1', hk + 'f'], [kd('t32')], out=t32[d], in0=pa[1][:, 128:256], in1=H32[d][:, p_, :],
                  op=ALU.add)
                X('dve', 'tensor_scalar', [kd('t32'), kd('PC')], [hk + 'f'], out=H32[d][:, p_, :], in0=t32[d],
                  scalar1=PC[d][:, p_:p_ + 1], scalar2=None, op0=ALU.mult)
                X('act', 'activation', [kd('t32'), kd('PC')], [hk + 'b'], out=Hbf[d][:, p_, :], in_=t32[d], func=AF.Copy,
                  scale=PC[d][:, p_:p_ + 1])

        RW = int(os.environ.get('RW_STAGE', '9'))
        for c in range(NT):
            if RW >= 2:
                prep(0, c)
                prep(1, NT - 1 - c)
            if RW >= 3:
                run_heads()
            if RW >= 4:
                chain(0, c)
                chain(1, NT - 1 - c)

        P.barrier()
        ph.off = mark
        uc = A_([128, 1536], F32)
        gtb = A_([128, 512], F32)
        yv = A_([128, 512], F32)
        sqv = A_([128, 512], F32)
        st8 = A_([128, 64], F32)
        for tt in range(NT):
            b = tt % 2
            r0 = tt * 128
            kb2 = lambda n: K('%s%d' % (n, b))
            DMA('sp', uc[b], ubd[r0:r0 + 128, 0:1536], ['ub%d' % tt], [kb2('uc')])
            DMA('sp', gtb[b], proj[r0:r0 + 128, 3072:3584], ['proj%d_6' % tt], [kb2('gtb')])
            X('act', 'activation', [kb2('gtb')], [kb2('gtb')], out=gtb[b], in_=gtb[b], func=AF.Silu)
            yt = Yacc[:, tt, :]
            S = st8[b]
            X('dve', 'tensor_reduce', [K('Yacc%d' % tt)], [kb2('st8')], out=S[:, 0:8], in_=v8(yt), axis=AX.X, op=ALU.add)
            X('pool', 'tensor_tensor', [K('Yacc%d' % tt)], [kb2('sqv')], out=sqv[b], in0=yt, in1=yt, op=ALU.mult)
            X('dve', 'tensor_reduce', [kb2('sqv')], [kb2('st8')], out=S[:, 8:16], in_=v8(sqv[b]), axis=AX.X, op=ALU.add)
            X('dve', 'tensor_scalar', [kb2('st8')], [kb2('st8')], out=S[:, 16:24], in0=S[:, 0:8], scalar1=1.0 / 64,
              scalar2=None, op0=ALU.mult)
            X('dve', 'tensor_tensor', [kb2('st8')], [kb2('st8')], out=S[:, 24:32], in0=S[:, 16:24], in1=S[:, 16:24],
              op=ALU.mult)
            X('dve', 'scalar_tensor_tensor', [kb2('st8')], [kb2('st8')], out=S[:, 32:40], in0=S[:, 8:16], scalar=1.0 / 64,
              in1=S[:, 24:32], op0=ALU.mult, op1=ALU.subtract)
            X('dve', 'tensor_scalar', [kb2('st8')], [kb2('st8')], out=S[:, 32:40], in0=S[:, 32:40], scalar1=64e-5,
              scalar2=None, op0=ALU.add)
            X('act', 'activation', [kb2('st8')], [kb2('st8')], out=S[:, 40:48], in_=S[:, 32:40], func=AF.Sqrt)
            X('dve', 'reciprocal', [kb2('st8')], [kb2('st8')], out=S[:, 48:56], in_=S[:, 40:48])
            X('dve', 'tensor_tensor', [K('Yacc%d' % tt), kb2('st8')], [kb2('yv')], out=v8(yv[b]), in0=v8(yt),
              in1=S[:, 16:24].unsqueeze(2).to_broadcast([128, 8, 64]), op=ALU.subtract)
            X('pool', 'tensor_tensor', [kb2('yv'), kb2('st8')], [kb2('yv')], out=v8(yv[b]), in0=v8(yv[b]),
              in1=S[:, 48:56].unsqueeze(2).to_broadcast([128, 8, 64]), op=ALU.mult)
            X('pool', 'tensor_tensor', [kb2('yv'), K('LNG')], [kb2('yv')], out=yv[b], in0=yv[b], in1=LNG, op=ALU.mult)
            X('pool', 'tensor_tensor', [kb2('yv'), K('LNB')], [kb2('yv')], out=yv[b], in0=yv[b], in1=LNB, op=ALU.add)
            X('dve', 'tensor_tensor', [kb2('uc')], [kb2('sqv')], out=sqv[b], in0=uc[b][:, 0:512], in1=uc[b][:, 512:1024],
              op=ALU.mult)
            X('pool', 'tensor_tensor', [kb2('sqv'), K('RKB')], [kb2('sqv')], out=sqv[b], in0=sqv[b], in1=RKB, op=ALU.mult)
            X('dve', 'tensor_reduce', [kb2('sqv')], [kb2('st8')], out=S[:, 56:64], in_=v8(sqv[b]), axis=AX.X, op=ALU.add)
            X('dve', 'tensor_tensor', [kb2('uc'), kb2('st8')], [kb2('sqv')], out=v8(sqv[b]), in0=v8(uc[b][:, 1024:1536]),
              in1=S[:, 56:64].unsqueeze(2).to_broadcast([128, 8, 64]), op=ALU.mult)
            X('pool', 'tensor_tensor', [kb2('yv'), kb2('sqv')], [kb2('yv')], out=yv[b], in0=yv[b], in1=sqv[b], op=ALU.add)
            X('pool', 'tensor_tensor', [kb2('yv'), kb2('gtb')], [kb2('yv')], out=yv[b], in0=yv[b], in1=gtb[b], op=ALU.mult)
            DMA('sp', mixd[r0:r0 + 128, 512:1024], yv[b], [kb2('yv')], ['mixd%d' % tt])

    def even_mixer(layer):
        i = layer // 2
        if 'a' in parts:
            attn_part(i)
        else:
            zero_mix(0, 512)
        if 'b' in parts:
            rwkv_part(i)
        else:
            zero_mix(512, 1024)

    EVEN_HOOK = globals().get('_even_mixer_builder')
    for s in range(nseq):
        xs = x[s * T:(s + 1) * T, :]
        ys = y[s * T:(s + 1) * T, :]
        first = True
        for layer in layers:
            hsrc = xs if first else ys
            j = layer // 2
            if layer % 2 == 1:
                prenorm_inproj(hsrc, layer, prm['odd_w_in'][j], ODD_IN)
                odd_mixer(layer)
                out_proj(prm['odd_w_out'][j], 2048, hsrc, ys, layer)
            else:
                prenorm_inproj(hsrc, layer, prm['even_w_in'][j], EVEN_IN)
                even_mixer(layer)
                out_proj(prm['even_w_out'][j], 1024, hsrc, ys, layer)
            first = False
    if dbg:
        print('instr counts', P.n_instr())
    return P.build()


def make_in_maps(inputs, ncores=8, nseq=4):
    consts = host_consts()
    x = np.ascontiguousarray(inputs['x'], dtype=np.float32)
    maps = []
    for c in range(ncores):
        m = {'x': x[c * nseq:(c + 1) * nseq].reshape(nseq * T, D)}
        for k, shp in PARAM_SHAPES.items():
            m[k] = np.ascontiguousarray(inputs[k], dtype=np.float32).reshape(shp)
        m.update({'c_' + k: v for k, v in consts.items()})
        maps.append(m)
    return maps


def kernel(**inputs):
    nc = build_program()
    maps = make_in_maps(inputs)
    res = run_bass_kernel_spmd(nc, maps, core_ids=list(range(8)))
    out = np.concatenate([np.asarray(r['y']).reshape(4, T, D) for r in res.results], axis=0)
    return out.astype(np.float32)
```

```python
import os
import numpy as np
from contextlib import ExitStack
import concourse.bass as bass
import concourse.mybir as mybir
from concourse.bass_utils import run_bass_kernel_spmd

F32 = mybir.dt.float32
BF16 = mybir.dt.bfloat16
AF = mybir.ActivationFunctionType
ALU = mybir.AluOpType
AX = mybir.AxisListType

ENGS = ['pe', 'dve', 'act', 'pool', 'sp']
ENG_ATTR = {'pe': 'tensor', 'dve': 'vector', 'act': 'scalar', 'pool': 'gpsimd', 'sp': 'sync'}
N_DMA_SEM = 12

T = 2048
D = 1024
NT = T // 128
EVEN_IN = 3584
ODD_IN = 6144
EPS = 1e-6
WDEC = float(np.exp(-0.5))


class Prog:
    def __init__(self, nc):
        self.nc = nc
        self.st = ExitStack()
        self.ops = {e: [] for e in ENGS}
        self.cnt = {e: 0 for e in ENGS}
        self.seen = {e: {} for e in ENGS}
        self.last_w = {}
        self.readers = {}
        self.dma_rr = {e: 0 for e in ENGS}
        self.dma_use = {}
        self.semnames = set()

    def sb(self, name, shape, dt):
        return self.st.enter_context(self.nc.sbuf_tensor(name, list(shape), dt))

    def ps(self, name, shape, dt):
        return self.st.enter_context(self.nc.psum_tensor(name, list(shape), dt))

    def dram(self, name, shape, dt, kind=None):
        if kind is None:
            return self.nc.dram_tensor(name, list(shape), dt).ap()
        return self.nc.dram_tensor(name, list(shape), dt, kind=kind).ap()

    def _deps(self, eng, reads, writes):
        deps = []
        for r in reads:
            w = self.last_w.get(r)
            if w is not None:
                deps.append(w)
        for w_ in writes:
            w = self.last_w.get(w_)
            if w is not None:
                deps.append(w)
            deps.extend(self.readers.get(w_, ()))
        waits = []
        seen = self.seen[eng]
        best = {}
        for s, v in deps:
            if seen.get(s, 0) >= v:
                continue
            if best.get(s, 0) < v:
                best[s] = v
        for s, v in best.items():
            seen[s] = v
            waits.append((s, v))
        return waits

    def _update(self, me, reads, writes):
        for r in reads:
            self.readers.setdefault(r, []).append(me)
        for w in writes:
            self.last_w[w] = me
            self.readers[w] = []

    def op(self, eng, fn, reads=(), writes=()):
        waits = self._deps(eng, reads, writes)
        self.cnt[eng] += 1
        s = 'c_' + eng
        self.semnames.add(s)
        me = (s, self.cnt[eng])
        if eng == 'pe':
            self.seen[eng][s] = self.cnt[eng]
        self.ops[eng].append((waits, fn, s, 1))
        self._update(me, reads, writes)

    def dma(self, eng, fn, reads=(), writes=()):
        i = self.dma_rr[eng]
        self.dma_rr[eng] = (i + 1) % N_DMA_SEM
        s = 'd_%s_%d' % (eng, i)
        self.semnames.add(s)
        u = self.dma_use.get(s, 0)
        waits = self._deps(eng, reads, writes)
        if u > 0 and self.seen[eng].get(s, 0) < 16 * u:
            waits.append((s, 16 * u))
            self.seen[eng][s] = 16 * u
        self.dma_use[s] = u + 1
        me = (s, 16 * (u + 1))
        self.ops[eng].append((waits, fn, s, 16))
        self._update(me, reads, writes)

    def barrier(self):
        for e in ENGS:
            waits = []
            seen = self.seen[e]
            for s, u in self.dma_use.items():
                if seen.get(s, 0) < 16 * u:
                    waits.append((s, 16 * u))
                    seen[s] = 16 * u
            for o in ENGS:
                if self.cnt[o] > 0 and seen.get('c_' + o, 0) < self.cnt[o]:
                    waits.append(('c_' + o, self.cnt[o]))
                    seen['c_' + o] = self.cnt[o]
            if waits:
                self.ops[e].append((waits, None, None, 0))

    def finish(self):
        eng = 'sp'
        waits = []
        for s, u in self.dma_use.items():
            if self.seen[eng].get(s, 0) < 16 * u:
                waits.append((s, 16 * u))
        for e in ENGS:
            if self.cnt[e] > 0 and e != eng:
                waits.append(('c_' + e, self.cnt[e]))
        self.ops[eng].append((waits, None, None, 0))

    def build(self):
        nc = self.nc
        self.finish()
        sems = {}
        for name in sorted(self.semnames):
            sems[name] = self.st.enter_context(nc.semaphore(name))
        block = self.st.enter_context(nc.Block())
        for eng in ENGS:
            if not self.ops[eng]:
                continue
            deco = getattr(block, ENG_ATTR[eng])

            def body(e, eng=eng):
                for waits, fn, s, inc in self.ops[eng]:
                    for ws, wv in waits:
                        e.wait_ge(sems[ws], wv)
                    if fn is not None:
                        ins = fn(e)
                        ins.then_inc(sems[s], inc)
            deco(body)
        self.st.close()
        return nc

    def n_instr(self):
        return {e: len(self.ops[e]) for e in ENGS}


def _rope_tables(dim):
    half = dim // 2
    q = dim // 4
    t = np.arange(T)
    row = (t // 64).astype(np.float32)
    col = (t % 64).astype(np.float32)
    inv = (10000.0 ** (-np.arange(0, half, 2, dtype=np.float32) / half)).astype(np.float32)
    ar = row[:, None] * inv
    ac = col[:, None] * inv
    cos = np.concatenate([np.cos(ar), np.cos(ar), np.cos(ac), np.cos(ac)], axis=1)
    sin = np.concatenate([-np.sin(ar), np.sin(ar), -np.sin(ac), np.sin(ac)], axis=1)
    return cos.astype(np.float32), sin.astype(np.float32)


def host_consts():
    c = {}
    c['ident'] = np.eye(128, dtype=np.float32)
    i = np.arange(128)
    su = (i[:, None] < i[None, :]).astype(np.float32)
    iu = (i[:, None] <= i[None, :]).astype(np.float32)
    sl = (i[:, None] > i[None, :]).astype(np.float32)
    il = (i[:, None] >= i[None, :]).astype(np.float32)
    m = np.stack([su, iu, sl, il], axis=1)
    c['masks'] = m.reshape(128, 512).astype(np.float32)
    c['tri'] = (-WDEC * m).reshape(128, 512).astype(np.float32)
    cos64, sin64 = _rope_tables(64)
    c['cosA'] = np.tile(cos64, (1, 10)).astype(np.float32)
    c['sinA'] = np.tile(sin64, (1, 10)).astype(np.float32)
    cos256, sin256 = _rope_tables(256)
    c['cosC'] = np.concatenate([cos256, cos256 / 16.0], axis=1).astype(np.float32)
    c['sinC'] = np.concatenate([sin256, sin256 / 16.0], axis=1).astype(np.float32)
    lgf = np.log(1.0 - 2.0 ** (-5.0 - np.arange(4, dtype=np.float64)))
    lgb = lgf[::-1]
    cc = np.arange(3968)[None, :] - 1920 - np.arange(128)[:, None]
    td = np.zeros((4, 128, 3968), np.float32)
    for h in range(4):
        td[h] = np.where(cc >= 0, np.exp(cc * lgf[h]), np.exp(-cc * lgb[h])).astype(np.float32)
    c['retdec'] = td.transpose(1, 0, 2).reshape(128, 4 * 3968).copy()
    c['negcol'] = np.full((128, 1), -WDEC, np.float32)
    return c


CONST_SHAPES = {'ident': [128, 128], 'masks': [128, 512], 'tri': [128, 512], 'cosA': [T, 640], 'sinA': [T, 640],
                'cosC': [T, 512], 'sinC': [T, 512], 'retdec': [128, 4 * 3968], 'negcol': [128, 1]}

PARAM_SHAPES = {
    'pre_gain': [4, 1024], 'post_gain': [4, 1024], 'even_w_in': [2, 1024, 3584], 'even_mu': [2, 1792],
    'even_q_gain': [2, 64], 'even_k_gain': [2, 64], 'even_k_k': [2, 512], 'even_k_a': [2, 512],
    'even_r_k': [2, 512], 'even_w0_f': [2, 512], 'even_w2_f': [2, 64, 512], 'even_a0_f': [2, 512],
    'even_a2_f': [2, 64, 512], 'even_w0_b': [2, 512], 'even_w2_b': [2, 64, 512], 'even_a0_b': [2, 512],
    'even_a2_b': [2, 64, 512], 'even_lnx_g': [2, 512], 'even_lnx_b': [2, 512], 'even_w_out': [2, 1024, 1024],
    'odd_w_in': [2, 1024, 6144], 'odd_gn_g': [2, 2048], 'odd_w_out': [2, 2048, 1024],
}


ARENA = 84 * 1024


def build_program(nseq=4, layers=(0, 1, 2, 3), dbg=False, parts=('a', 'b')):
    nc = bass.Bass("TRN2", target_bir_lowering=False)
    P = Prog(nc)

    def X(eng, meth, reads, writes, **kw):
        P.op(eng, lambda e: getattr(e, meth)(**kw), reads, writes)

    def DMA(eng, out, in_, reads, writes):
        P.dma(eng, lambda e: e.dma_start(out=out, in_=in_), reads, writes)

    x = P.dram('x', [nseq * T, D], F32, 'ExternalInput')
    y = P.dram('y', [nseq * T, D], F32, 'ExternalOutput')
    prm = {k: P.dram(k, shp, F32, 'ExternalInput') for k, shp in PARAM_SHAPES.items()}
    cst = {k: P.dram('c_' + k, shp, F32, 'ExternalInput') for k, shp in CONST_SHAPES.items()}
    proj = P.dram('proj', [T, ODD_IN], F32)
    mixd = P.dram('mixd', [T, 2048], F32)
    ubd = P.dram('ubd', [T, 1792], F32)
    dbgo = P.dram('dbgo', [128, 8 * 512], F32, 'ExternalOutput') if dbg else None

    idb = P.sb('idb', [128, 128], BF16)
    msk = P.sb('msk', [128, 4, 128], BF16)
    tri = P.sb('tri', [128, 4, 128], F32)
    negcol = P.sb('negcol', [128, 1], F32)
    gpre = P.sb('gpre', [128, D], F32)
    gpost = P.sb('gpost', [128, D], F32)
    hx = [P.sb('hx%d' % i, [128, D], F32) for i in range(2)]
    junk = P.sb('junk', [128, 2048], BF16)
    hnb = [P.sb('hnb%d' % i, [128, 2048], BF16) for i in range(2)]
    stt = [P.sb('stt%d' % i, [128, 8], F32) for i in range(2)]
    ev = [P.sb('ev%d' % i, [128, 512], F32) for i in range(3)]
    arena = P.sb('arena', [128, ARENA], BF16)
    ptr = [P.ps('ptr%d' % i, [128, 1024], BF16) for i in range(2)]
    pa = [P.ps('pa%d' % i, [128, 512], F32) for i in range(6)]

    DMA('pool', idb[:], cst['ident'], [], ['idb'])
    DMA('pool', msk[:].rearrange("p a b -> p (a b)"), cst['masks'], [], ['msk'])
    DMA('sp', tri[:].rearrange("p a b -> p (a b)"), cst['tri'], [], ['tri'])
    DMA('sp', negcol[:], cst['negcol'], [], ['negcol'])

    cnt = {'ev': 0, 'ptr': 0, 'ph': 0}

    class Phase:
        def __init__(self):
            P.barrier()
            self.off = 0
            cnt['ph'] += 1
            self.id = cnt['ph']

        def alloc(self, shape, dt):
            n = int(np.prod(shape[1:]))
            sz = n * (2 if dt == F32 else 1)
            self.off = (self.off + 7) // 8 * 8
            v = arena[:, self.off:self.off + sz]
            self.off += sz
            assert self.off <= ARENA, ('arena overflow', self.off)
            if dt == F32:
                v = v.bitcast(F32)
            if len(shape) == 3:
                v = v.rearrange("p (a b) -> p a b", a=shape[1])
            elif len(shape) == 4:
                v = v.rearrange("p (a b c) -> p a b c", a=shape[1], b=shape[2])
            return v

        def key(self, name):
            return 'ph%d_%s' % (self.id, name)

    def next_ev():
        cnt['ev'] = (cnt['ev'] + 1) % 3
        return cnt['ev']

    def next_ptr():
        cnt['ptr'] = (cnt['ptr'] + 1) % 2
        return cnt['ptr']

    def rms_rows(src_reads, src, b, width, eps):
        X('act', 'activation', src_reads, ['junk', 'stt%d' % b], out=junk[:, 0:width], in_=src, func=AF.Square,
          accum_out=stt[b][:, 0:1])
        X('dve', 'tensor_scalar', ['stt%d' % b], ['stt%d' % b], out=stt[b][:, 1:2], in0=stt[b][:, 0:1],
          scalar1=1.0 / width, scalar2=eps, op0=ALU.mult, op1=ALU.add)
        X('act', 'activation', ['stt%d' % b], ['stt%d' % b], out=stt[b][:, 2:3], in_=stt[b][:, 1:2], func=AF.Sqrt)
        X('dve', 'reciprocal', ['stt%d' % b], ['stt%d' % b], out=stt[b][:, 3:4], in_=stt[b][:, 2:3])

    def transpose_tile(src_bf, src_key, nchunks, dst_fn, dst_key):
        for c0 in range(0, nchunks, 4):
            n = min(4, nchunks - c0)
            pb = next_ptr()
            for k in range(n):
                X('pe', 'transpose', [src_key, 'idb'], ['ptr%d' % pb], out=ptr[pb][:, k * 128:(k + 1) * 128],
                  in_=src_bf[:, (c0 + k) * 128:(c0 + k + 1) * 128], identity=idb[:])
            src = ptr[pb][:, 0:n * 128].rearrange("p (k t) -> p k t", k=n)
            if (c0 // 4) % 2 == 0:
                X('act', 'activation', ['ptr%d' % pb], [dst_key], out=dst_fn(c0, n), in_=src, func=AF.Copy)
            else:
                X('dve', 'tensor_copy', ['ptr%d' % pb], [dst_key], out=dst_fn(c0, n), in_=src)

    def rope(eng2, xin, xkey, cos, sin, ckey, tmpA, tmpB, tkey, out_bf, okey, H, dim):
        q = dim // 4
        v5 = lambda a: a.rearrange("p (h x f q) -> p h x f q", h=H, x=2, f=2)
        X('dve', 'tensor_tensor', [xkey, ckey], [tkey + 'A'], out=tmpA, in0=xin, in1=cos, op=ALU.mult)
        for f in range(2):
            X(eng2, 'tensor_tensor', [xkey, ckey], [tkey + 'B'], out=v5(tmpB)[:, :, :, f, :],
              in0=v5(xin)[:, :, :, 1 - f, :], in1=v5(sin)[:, :, :, f, :], op=ALU.mult)
        X('dve', 'tensor_tensor', [tkey + 'A', tkey + 'B'], [okey], out=out_bf, in0=tmpA, in1=tmpB, op=ALU.add)

    def prenorm_inproj(hsrc, layer, w_dram, ncols):
        ph = Phase()
        hnT = ph.alloc([128, 8, T], BF16)
        wt = [ph.alloc([128, 8, 512], BF16) for _ in range(2)]
        K = ph.key
        DMA('sp', gpre[:], prm['pre_gain'][layer:layer + 1, :].partition_broadcast(128), [], ['gpre'])
        for tt in range(NT):
            b = tt % 2
            DMA('sp', hx[b][:], hsrc[tt * 128:(tt + 1) * 128, :], ['h%d' % tt], ['hx%d' % b])
            rms_rows(['hx%d' % b], hx[b][:], b, D, EPS)
            X('dve', 'scalar_tensor_tensor', ['hx%d' % b, 'stt%d' % b, 'gpre'], ['hnb%d' % b], out=hnb[b][:, 0:D],
              in0=hx[b][:], scalar=stt[b][:, 3:4], in1=gpre[:], op0=ALU.mult, op1=ALU.mult)
            transpose_tile(hnb[b], 'hnb%d' % b, 8,
                           lambda c0, n, tt=tt: hnT[:, c0:c0 + n, tt * 128:(tt + 1) * 128], K('hnT%d' % tt))
        for cb in range(ncols // 512):
            wb = cb % 2
            DMA('pool', wt[wb], w_dram[:, cb * 512:(cb + 1) * 512].rearrange("(kc p) n -> p kc n", p=128),
                [], [K('wt%d' % wb)])
            for tt in range(NT):
                pb = (cb * NT + tt) % 2
                for kc in range(8):
                    X('pe', 'matmul', [K('hnT%d' % tt), K('wt%d' % wb)], ['pa%d' % pb], out=pa[pb][:],
                      lhsT=hnT[:, kc, tt * 128:(tt + 1) * 128], rhs=wt[wb][:, kc, :], start=(kc == 0), stop=(kc == 7))
                e = next_ev()
                if tt % 2 == 0:
                    X('act', 'activation', ['pa%d' % pb], ['ev%d' % e], out=ev[e][:], in_=pa[pb][:], func=AF.Copy)
                else:
                    X('dve', 'tensor_copy', ['pa%d' % pb], ['ev%d' % e], out=ev[e][:], in_=pa[pb][:])
                DMA('sp', proj[tt * 128:(tt + 1) * 128, cb * 512:(cb + 1) * 512], ev[e][:], ['ev%d' % e],
                    ['proj%d_%d' % (tt, cb)])

    def out_proj(w_dram, Kdim, hsrc, hdst, layer):
        ph = Phase()
        KC = Kdim // 128
        wout = ph.alloc([128, KC, 1024], BF16)
        xT = [ph.alloc([128, KC, 128], BF16) for _ in range(2)]
        K = ph.key
        DMA('sp', gpost[:], prm['post_gain'][layer:layer + 1, :].partition_broadcast(128), [], ['gpost'])
        for k4 in range(0, KC, 4):
            DMA('pool', wout[:, k4:k4 + 4, :],
                w_dram[k4 * 128:(k4 + 4) * 128, :].rearrange("(kc p) n -> p kc n", p=128), [], [K('wout')])
        for tt in range(NT):
            b = tt % 2
            DMA('pool', hnb[b][:, 0:Kdim], mixd[tt * 128:(tt + 1) * 128, 0:Kdim], ['mixd%d' % tt], ['hnb%d' % b])
            transpose_tile(hnb[b], 'hnb%d' % b, KC, lambda c0, n, b=b: xT[b][:, c0:c0 + n, :], K('xT%d' % b))
            DMA('sp', hx[b][:], hsrc[tt * 128:(tt + 1) * 128, :], ['h%d' % tt], ['hx%d' % b])
            for half in range(2):
                for kc in range(KC):
                    X('pe', 'matmul', [K('xT%d' % b), K('wout')], ['pa%d' % half], out=pa[half][:],
                      lhsT=xT[b][:, kc, :], rhs=wout[:, kc, half * 512:(half + 1) * 512], start=(kc == 0),
                      stop=(kc == KC - 1))
            X('act', 'activation', ['pa0'], ['junk', 'stt%d' % b], out=junk[:, 0:512], in_=pa[0][:], func=AF.Square,
              accum_out=stt[b][:, 4:5])
            X('act', 'activation', ['pa1'], ['junk', 'stt%d' % b], out=junk[:, 512:1024], in_=pa[1][:], func=AF.Square,
              accum_out=stt[b][:, 5:6])
            X('dve', 'tensor_tensor', ['stt%d' % b], ['stt%d' % b], out=stt[b][:, 0:1], in0=stt[b][:, 4:5],
              in1=stt[b][:, 5:6], op=ALU.add)
            X('dve', 'tensor_scalar', ['stt%d' % b], ['stt%d' % b], out=stt[b][:, 1:2], in0=stt[b][:, 0:1],
              scalar1=1.0 / D, scalar2=EPS, op0=ALU.mult, op1=ALU.add)
            X('act', 'activation', ['stt%d' % b], ['stt%d' % b], out=stt[b][:, 2:3], in_=stt[b][:, 1:2], func=AF.Sqrt)
            X('dve', 'reciprocal', ['stt%d' % b], ['stt%d' % b], out=stt[b][:, 3:4], in_=stt[b][:, 2:3])
            for half in range(2):
                e = next_ev()
                X('dve', 'scalar_tensor_tensor', ['pa%d' % half, 'stt%d' % b, 'gpost'], ['ev%d' % e], out=ev[e][:],
                  in0=pa[half][:], scalar=stt[b][:, 3:4], in1=gpost[:, half * 512:(half + 1) * 512], op0=ALU.mult,
                  op1=ALU.mult)
                X('pool', 'tensor_tensor', ['ev%d' % e, 'hx%d' % b], ['ev%d' % e], out=ev[e][:], in0=ev[e][:],
                  in1=hx[b][:, half * 512:(half + 1) * 512], op=ALU.add)
                DMA('sp', hdst[tt * 128:(tt + 1) * 128, half * 512:(half + 1) * 512], ev[e][:], ['ev%d' % e],
                    ['h%d' % tt])

    def odd_mixer(layer):
        j = layer // 2
        ph = Phase()
        K = ph.key
        qkT = ph.alloc([128, 4, T], BF16)
        vS = ph.alloc([128, NT, 512], BF16)
        rdec = ph.alloc([128, 3968], BF16)
        pT = [ph.alloc([128, 512], BF16) for _ in range(2)]
        gng = ph.alloc([128, 2048], F32)
        rq = [ph.alloc([128, 512], F32) for _ in range(2)]
        ct = [ph.alloc([128, 512], F32) for _ in range(2)]
        sn = [ph.alloc([128, 512], F32) for _ in range(2)]
        tA = [ph.alloc([128, 512], F32) for _ in range(2)]
        tB = [ph.alloc([128, 512], F32) for _ in range(2)]
        rb = [ph.alloc([128, 512], BF16) for _ in range(2)]
        ob = [ph.alloc([128, 512], F32) for _ in range(2)]
        gt = [ph.alloc([128, 512], F32) for _ in range(2)]
        bst = [ph.alloc([128, 8], F32) for _ in range(2)]
        DMA('sp', gng, prm['odd_gn_g'][j:j + 1, :].partition_broadcast(128), [], [K('gng')])
        for h in range(4):
            DMA('pool', rdec, cst['retdec'][:, h * 3968:(h + 1) * 3968], [], [K('rdec')])
            for tt in range(NT):
                b = tt % 2
                r0 = slice(tt * 128, (tt + 1) * 128)
                DMA('sp', rq[b][:, 0:256], proj[r0, h * 256:(h + 1) * 256], ['proj%d_%d' % (tt, h // 2)], [K('rq%d' % b)])
                DMA('sp', rq[b][:, 256:512], proj[r0, 1024 + h * 256:1024 + (h + 1) * 256],
                    ['proj%d_%d' % (tt, 2 + h // 2)], [K('rq%d' % b)])
                DMA('sp', ct[b], cst['cosC'][r0, :], [], [K('cs%d' % b)])
                DMA('sp', sn[b], cst['sinC'][r0, :], [], [K('cs%d' % b)])
                rope('pool', rq[b], K('rq%d' % b), ct[b], sn[b], K('cs%d' % b), tA[b], tB[b], K('t%d' % b), rb[b],
                     K('rb%d' % b), 2, 256)
                transpose_tile(rb[b], K('rb%d' % b), 4, lambda c0, n, tt=tt: qkT[:, 0:4, tt * 128:(tt + 1) * 128],
                               K('qkT%d' % tt))
            for t4 in range(0, NT, 4):
                DMA('pool', vS[:, t4:t4 + 4, :],
                    proj[t4 * 128:(t4 + 4) * 128, 2048 + h * 512:2048 + (h + 1) * 512].rearrange("(t p) n -> p t n", p=128),
                    ['proj%d_%d' % (tt, 4 + h) for tt in range(t4, t4 + 4)], [K('vS%d' % tt) for tt in range(t4, t4 + 4)])
            it = 0
            for cn4 in range(4):
                qkeys = [K('qkT%d' % tt) for tt in range(cn4 * 4, cn4 * 4 + 4)]
                for cm in range(NT):
                    s = it % 2
                    it += 1
                    for dc in range(2):
                        X('pe', 'matmul', qkeys + [K('qkT%d' % cm)], ['pa%d' % s], out=pa[s][:],
                          lhsT=qkT[:, 2 + dc, cm * 128:(cm + 1) * 128], rhs=qkT[:, dc, cn4 * 512:(cn4 + 1) * 512],
                          start=(dc == 0), stop=(dc == 1))
                    off = cn4 * 512 - cm * 128 + 1920
                    X('dve', 'tensor_tensor', ['pa%d' % s, K('rdec')], [K('pT%d' % s)], out=pT[s], in0=pa[s][:],
                      in1=rdec[:, off:off + 512], op=ALU.mult)
                    for qs in range(4):
                        X('pe', 'matmul', [K('pT%d' % s), K('vS%d' % cm)], ['pa%d' % (2 + qs)], out=pa[2 + qs][:],
                          lhsT=pT[s][:, qs * 128:(qs + 1) * 128], rhs=vS[:, cm, :], start=(cm == 0), stop=(cm == NT - 1))
                for qs in range(4):
                    tt = cn4 * 4 + qs
                    b = qs % 2
                    r0 = slice(tt * 128, (tt + 1) * 128)
                    X('act', 'activation', ['pa%d' % (2 + qs)], [K('ob%d' % b)], out=ob[b], in_=pa[2 + qs][:], func=AF.Copy)
                    DMA('sp', gt[b], proj[r0, 4096 + h * 512:4096 + (h + 1) * 512], ['proj%d_%d' % (tt, 8 + h)], [K('gt%d' % b)])
                    X('act', 'activation', [K('gt%d' % b)], [K('gt%d' % b)], out=gt[b], in_=gt[b], func=AF.Silu)
                    X('act', 'activation', [K('ob%d' % b)], ['junk', K('bst%d' % b)], out=junk[:, 0:512], in_=ob[b],
                      func=AF.Square, accum_out=bst[b][:, 0:1])
                    X('dve', 'tensor_reduce', [K('ob%d' % b)], [K('bst%d' % b)], out=bst[b][:, 1:2], in_=ob[b], axis=AX.X,
                      op=ALU.add)
                    X('dve', 'tensor_scalar', [K('bst%d' % b)], [K('bst%d' % b)], out=bst[b][:, 2:3], in0=bst[b][:, 1:2],
                      scalar1=1.0 / 512, scalar2=None, op0=ALU.mult)
                    X('dve', 'tensor_tensor', [K('bst%d' % b)], [K('bst%d' % b)], out=bst[b][:, 3:4], in0=bst[b][:, 2:3],
                      in1=bst[b][:, 2:3], op=ALU.mult)
                    X('dve', 'scalar_tensor_tensor', [K('bst%d' % b)], [K('bst%d' % b)], out=bst[b][:, 4:5],
                      in0=bst[b][:, 0:1], scalar=1.0 / 512, in1=bst[b][:, 3:4], op0=ALU.mult, op1=ALU.subtract)
                    X('dve', 'tensor_scalar', [K('bst%d' % b)], [K('bst%d' % b)], out=bst[b][:, 4:5], in0=bst[b][:, 4:5],
                      scalar1=1e-5, scalar2=None, op0=ALU.add)
                    X('act', 'activation', [K('bst%d' % b)], [K('bst%d' % b)], out=bst[b][:, 5:6], in_=bst[b][:, 4:5],
                      func=AF.Sqrt)
                    X('dve', 'reciprocal', [K('bst%d' % b)], [K('bst%d' % b)], out=bst[b][:, 6:7], in_=bst[b][:, 5:6])
                    X('dve', 'tensor_scalar', [K('ob%d' % b), K('bst%d' % b)], [K('ob%d' % b)], out=ob[b], in0=ob[b],
                      scalar1=bst[b][:, 2:3], scalar2=bst[b][:, 6:7], op0=ALU.subtract, op1=ALU.mult)
                    X('pool', 'tensor_tensor', [K('ob%d' % b), K('gng')], [K('ob%d' % b)], out=ob[b], in0=ob[b],
                      in1=gng[:, h * 512:(h + 1) * 512], op=ALU.mult)
                    X('pool', 'tensor_tensor', [K('ob%d' % b), K('gt%d' % b)], [K('ob%d' % b)], out=ob[b], in0=ob[b],
                      in1=gt[b], op=ALU.mult)
                    DMA('sp', mixd[r0, h * 512:(h + 1) * 512], ob[b], [K('ob%d' % b)], ['mixd%d' % tt])


    def bcast_row(ph, key, src_row, width, eng='sp'):
        t = ph.alloc([128, width], F32)
        DMA(eng, t, src_row.partition_broadcast(128), [], [ph.key(key)])
        return t

    def zero_mix(c0, c1):
        X('dve', 'memset', [], ['ev0'], ap=ev[0][:], constant=0.0)
        for tt in range(NT):
            DMA('sp', mixd[tt * 128:(tt + 1) * 128, c0:c1], ev[0][:, 0:c1 - c0], ['ev0'], ['mixd%d' % tt])

    def attn_part(i):
        ph = Phase()
        K = ph.key
        qT6 = ph.alloc([128, 6, T], BF16)
        vA = ph.alloc([128, NT, 2, 66], BF16)
        outA = ph.alloc([128, NT, 512], F32)
        pT = [ph.alloc([128, 512], BF16) for _ in range(2)]
        qg = ph.alloc([128, 640], F32)
        aq = [ph.alloc([128, 640], F32) for _ in range(2)]
        sq = [ph.alloc([128, 640], F32) for _ in range(2)]
        ss = [ph.alloc([128, 32], F32) for _ in range(2)]
        ct = [ph.alloc([128, 640], F32) for _ in range(2)]
        sn = [ph.alloc([128, 640], F32) for _ in range(2)]
        tA = [ph.alloc([128, 640], F32) for _ in range(2)]
        tB = [ph.alloc([128, 640], F32) for _ in range(2)]
        rb = [ph.alloc([128, 768], BF16) for _ in range(2)]
        gt = [ph.alloc([128, 512], F32) for _ in range(2)]
        rs = [ph.alloc([128, 4], F32) for _ in range(4)]
        for h in range(8):
            DMA('sp', qg[:, h * 64:(h + 1) * 64], prm['even_q_gain'][i:i + 1, :].partition_broadcast(128), [], [K('qg')])
        for h in range(2):
            DMA('sp', qg[:, 512 + h * 64:512 + (h + 1) * 64], prm['even_k_gain'][i:i + 1, :].partition_broadcast(128), [],
                [K('qg')])
        X('pool', 'memset', [], [K('vA')], ap=vA.rearrange("p a b c -> p (a b c)"), constant=1.0)
        for t4 in range(0, NT, 4):
            for g in range(2):
                DMA('pool', vA[:, t4:t4 + 4, g, 0:64],
                    proj[t4 * 128:(t4 + 4) * 128, 640 + g * 64:640 + (g + 1) * 64].rearrange("(t p) d -> p t d", p=128),
                    ['proj%d_1' % tt for tt in range(t4, t4 + 4)], [K('vA')])
        v3 = lambda a: a.rearrange("p (h d) -> p h d", d=64)
        for tt in range(NT):
            b = tt % 2
            r0 = slice(tt * 128, (tt + 1) * 128)
            DMA('sp', aq[b], proj[r0, 0:640], ['proj%d_0' % tt, 'proj%d_1' % tt], [K('aq%d' % b)])
            DMA('sp', ct[b], cst['cosA'][r0, :], [], [K('cs%d' % b)])
            DMA('sp', sn[b], cst['sinA'][r0, :], [], [K('cs%d' % b)])
            X('pool', 'tensor_tensor', [K('aq%d' % b)], [K('sq%d' % b)], out=sq[b], in0=aq[b], in1=aq[b], op=ALU.mult)
            X('dve', 'tensor_reduce', [K('sq%d' % b)], [K('ss%d' % b)], out=ss[b][:, 0:10], in_=v3(sq[b]), axis=AX.X, op=ALU.add)
            X('dve', 'tensor_scalar', [K('ss%d' % b)], [K('ss%d' % b)], out=ss[b][:, 10:20], in0=ss[b][:, 0:10],
              scalar1=1.0 / 64, scalar2=EPS, op0=ALU.mult, op1=ALU.add)
            X('act', 'activation', [K('ss%d' % b)], [K('ss%d' % b)], out=ss[b][:, 20:30], in_=ss[b][:, 10:20], func=AF.Sqrt)
            X('dve', 'reciprocal', [K('ss%d' % b)], [K('ss%d' % b)], out=ss[b][:, 0:10], in_=ss[b][:, 20:30])
            X('dve', 'tensor_tensor', [K('aq%d' % b), K('ss%d' % b)], [K('aq%d' % b)], out=v3(aq[b]), in0=v3(aq[b]),
              in1=ss[b][:, 0:10].unsqueeze(2).to_broadcast([128, 10, 64]), op=ALU.mult)
            X('pool', 'tensor_tensor', [K('aq%d' % b), K('qg')], [K('aq%d' % b)], out=aq[b], in0=aq[b], in1=qg, op=ALU.mult)
            rope('pool', aq[b], K('aq%d' % b), ct[b], sn[b], K('cs%d' % b), tA[b], tB[b], K('t%d' % b), rb[b][:, 0:640],
                 K('rb%d' % b), 10, 64)
            X('pool', 'tensor_copy', [K('rb%d' % b)], [K('rb%d' % b)], out=rb[b][:, 640:704], in_=rb[b][:, 576:640])
            X('pool', 'tensor_copy', [K('rb%d' % b)], [K('rb%d' % b)], out=rb[b][:, 704:768], in_=rb[b][:, 512:576])
            transpose_tile(rb[b], K('rb%d' % b), 6, lambda c0, n, tt=tt: qT6[:, c0:c0 + n, tt * 128:(tt + 1) * 128],
                           K('qT%d' % tt))
        it = 0
        for hq in range(8):
            g = hq // 4
            base = (hq % 2) * 64
            pair = hq // 2
            kc = 4 + (1 if g != base // 64 else 0)
            for cn4 in range(4):
                qkeys = [K('qT%d' % tt) for tt in range(cn4 * 4, cn4 * 4 + 4)]
                for cm in range(NT):
                    s_ = it % 2
                    it += 1
                    X('pe', 'matmul', qkeys + [K('qT%d' % cm)], ['pa%d' % s_], out=pa[s_][:],
                      lhsT=qT6[base:base + 64, kc, cm * 128:(cm + 1) * 128],
                      rhs=qT6[base:base + 64, pair, cn4 * 512:(cn4 + 1) * 512], start=True, stop=True)
                    X('act', 'activation', ['pa%d' % s_], [K('pT%d' % s_)], out=pT[s_], in_=pa[s_][:], func=AF.Exp, scale=0.125)
                    for qs in range(4):
                        X('pe', 'matmul', [K('pT%d' % s_), K('vA')], ['pa%d' % (2 + qs)], out=pa[2 + qs][:, 0:65],
                          lhsT=pT[s_][:, qs * 128:(qs + 1) * 128], rhs=vA[:, cm, g, 0:65], start=(cm == 0), stop=(cm == NT - 1))
                for qs in range(4):
                    tt = cn4 * 4 + qs
                    X('dve', 'reciprocal', ['pa%d' % (2 + qs)], [K('rs%d' % qs)], out=rs[qs][:, 0:1], in_=pa[2 + qs][:, 64:65])
                    X('dve', 'tensor_scalar', ['pa%d' % (2 + qs), K('rs%d' % qs)], [K('outA%d' % tt)],
                      out=outA[:, tt, hq * 64:(hq + 1) * 64], in0=pa[2 + qs][:, 0:64], scalar1=rs[qs][:, 0:1], scalar2=None,
                      op0=ALU.mult)
        for tt in range(NT):
            b = tt % 2
            r0 = slice(tt * 128, (tt + 1) * 128)
            DMA('sp', gt[b], proj[r0, 768:1280], ['proj%d_1' % tt, 'proj%d_2' % tt], [K('gt%d' % b)])
            X('act', 'activation', [K('gt%d' % b)], [K('gt%d' % b)], out=gt[b], in_=gt[b], func=AF.Silu)
            X('pool', 'tensor_tensor', [K('gt%d' % b), K('outA%d' % tt)], [K('gt%d' % b)], out=gt[b], in0=gt[b],
              in1=outA[:, tt, :], op=ALU.mult)
            DMA('sp', mixd[r0, 0:512], gt[b], [K('gt%d' % b)], ['mixd%d' % tt])

    def rwkv_part(i):
        ph = Phase()
        K = ph.key
        MU = bcast_row(ph, 'MU', prm['even_mu'][i:i + 1, :], 1792)
        s0 = [ph.alloc([128, 1792], F32) for _ in range(2)]
        sm = [ph.alloc([128, 1792], F32) for _ in range(2)]
        sp_ = [ph.alloc([128, 1792], F32) for _ in range(2)]
        pk = lambda tt: ['proj%d_%d' % (tt, cb) for cb in range(2, 6)]
        for tt in range(NT):
            b = tt % 2
            r0 = tt * 128
            DMA('sp', s0[b], proj[r0:r0 + 128, 1280:3072], pk(tt), [K('s0%d' % b)])
            if tt == 0:
                X('pool', 'memset', [], [K('sm%d' % b)], ap=sm[b], constant=0.0)
                DMA('sp', sm[b][1:128, :], proj[0:127, 1280:3072], pk(tt), [K('sm%d' % b)])
            else:
                DMA('sp', sm[b], proj[r0 - 1:r0 + 127, 1280:3072], pk(tt) + pk(tt - 1), [K('sm%d' % b)])
            if tt == NT - 1:
                X('pool', 'memset', [], [K('sp%d' % b)], ap=sp_[b], constant=0.0)
                DMA('sp', sp_[b][0:127, :], proj[r0 + 1:r0 + 128, 1280:3072], pk(tt), [K('sp%d' % b)])
            else:
                DMA('sp', sp_[b], proj[r0 + 1:r0 + 129, 1280:3072], pk(tt) + pk(tt + 1), [K('sp%d' % b)])
            X('pool', 'tensor_tensor', [K('sm%d' % b), K('sp%d' % b)], [K('sm%d' % b)], out=sm[b], in0=sm[b], in1=sp_[b],
              op=ALU.add)
            X('dve', 'scalar_tensor_tensor', [K('sm%d' % b), K('s0%d' % b)], [K('sm%d' % b)], out=sm[b], in0=sm[b],
              scalar=0.5, in1=s0[b], op0=ALU.mult, op1=ALU.subtract)
            X('pool', 'tensor_tensor', [K('sm%d' % b), K('MU')], [K('sm%d' % b)], out=sm[b], in0=sm[b], in1=MU, op=ALU.mult)
            X('dve', 'tensor_tensor', [K('sm%d' % b), K('s0%d' % b)], [K('sm%d' % b)], out=sm[b], in0=sm[b], in1=s0[b],
              op=ALU.add)
            DMA('sp', ubd[r0:r0 + 128, :], sm[b], [K('sm%d' % b)], ['ub%d' % tt])

        ph = Phase()
        K = ph.key
        Yacc = ph.alloc([128, NT, 512], F32)
        KKB = bcast_row(ph, 'KKB', prm['even_k_k'][i:i + 1, :], 512)
        KAB = bcast_row(ph, 'KAB', prm['even_k_a'][i:i + 1, :], 512)
        RKB = bcast_row(ph, 'RKB', prm['even_r_k'][i:i + 1, :], 512)
        LNG = bcast_row(ph, 'LNG', prm['even_lnx_g'][i:i + 1, :], 512)
        LNB = bcast_row(ph, 'LNB', prm['even_lnx_b'][i:i + 1, :], 512)
        W0 = [bcast_row(ph, 'W0%d' % d, prm['even_w0_' + 'fb'[d]][i:i + 1, :], 512) for d in range(2)]
        A0 = [bcast_row(ph, 'A0%d' % d, prm['even_a0_' + 'fb'[d]][i:i + 1, :], 512) for d in range(2)]
        NEGB = ph.alloc([128, 512], F32)
        X('dve', 'memset', [], [K('NEGB')], ap=NEGB, constant=-1.0)
        mark = ph.off
        W2A2 = [ph.alloc([128, 512], BF16) for _ in range(2)]
        for d in range(2):
            DMA('pool', W2A2[d][0:64, :], prm['even_w2_' + 'fb'[d]][i], [], [K('W2A2%d' % d)])
            DMA('pool', W2A2[d][64:128, :], prm['even_a2_' + 'fb'[d]][i], [], [K('W2A2%d' % d)])
        A_ = lambda shape, dt: [ph.alloc(shape, dt) for _ in range(2)]
        u = A_([128, 1792], F32)
        lo = A_([128, 128], BF16)
        loT = A_([128, 128], BF16)
        sgm = A_([128, 512], F32)
        alp = A_([128, 512], F32)
        Epl = A_([128, 512], F32)
        Emi = A_([128, 512], F32)
        Epr = A_([128, 512], F32)
        kk = A_([128, 512], F32)
        kx = A_([128, 512], F32)
        bb = A_([128, 512], F32)
        tm = A_([128, 512], F32)
        ss8 = A_([128, 32], F32)
        Kt = A_([128, 512], BF16)
        Bt = A_([128, 512], BF16)
        Vt = A_([128, 512], BF16)
        ARs = A_([128, 2, 512], BF16)
        ARt = A_([128, 4, 2, 128], BF16)
        KBt = A_([128, 4, 2, 128], BF16)
        PC = A_([128, 4], F32)
        SC = A_([128, 8, 512], BF16)
        TI = A_([128, 8, 128], BF16)
        L0 = [ph.alloc([128, 128], BF16) for _ in range(4)]
        Wk = [[ph.alloc([128, 384], BF16) for _ in range(2)] for _ in range(4)]
        Xb = A_([128, 4, 128], BF16)
        Ub = A_([128, 4, 128], BF16)
        H32 = A_([128, 4, 128], F32)
        Hbf = A_([128, 4, 128], BF16)
        t32 = A_([128, 128], F32)
        mk4 = A_([128, 512], BF16)
        for d in range(2):
            for rep in range(2):
                X('pool', 'tensor_copy', ['msk'], [K('mk4%d' % d)], out=mk4[d][:, rep * 256:(rep + 1) * 256],
                  in_=msk[:, 2 * d:2 * d + 2, :].rearrange("p a b -> p (a b)"))
            X('pool', 'memset', [], [K('H32%d' % d)], ap=H32[d].rearrange("p a b -> p (a b)"), constant=0.0)
            X('pool', 'memset', [], [K('Hbf%d' % d)], ap=Hbf[d].rearrange("p a b -> p (a b)"), constant=0.0)
        X('pool', 'memset', [], [K('Yacc%d' % tt) for tt in range(NT)], ap=Yacc.rearrange("p a b -> p (a b)"), constant=0.0)
        v8 = lambda a: a.rearrange("p (h d) -> p h d", d=64)

        PS = float(os.environ.get('PREP_STAGE', '9'))

        def prep(d, tt):
            kd = lambda n: K('%s%d' % (n, d))
            r0 = tt * 128
            DMA('sp', u[d], ubd[r0:r0 + 128, :], ['ub%d' % tt], [kd('u')])
            r_ = u[d][:, 0:512]
            kb_ = u[d][:, 512:1024]
            vb_ = u[d][:, 1024:1536]
            X('act', 'activation', [kd('u')], [kd('lo')], out=lo[d][:, 0:64], in_=u[d][:, 1536 + d * 64:1600 + d * 64],
              func=AF.Tanh)
            X('dve', 'tensor_copy', [kd('u')], [kd('lo')], out=lo[d][:, 64:128], in_=u[d][:, 1664 + d * 64:1728 + d * 64])
            pb = next_ptr()
            X('pe', 'transpose', [kd('lo'), 'idb'], ['ptr%d' % pb], out=ptr[pb][:, 0:128], in_=lo[d], identity=idb[:])
            X('act', 'activation', ['ptr%d' % pb], [kd('loT')], out=loT[d], in_=ptr[pb][:, 0:128], func=AF.Copy)
            X('pe', 'matmul', [kd('loT'), K('W2A2%d' % d)], ['pa0'], out=pa[0][:], lhsT=loT[d][0:64, :],
              rhs=W2A2[d][0:64, :], start=True, stop=True)
            X('pe', 'matmul', [kd('loT'), K('W2A2%d' % d)], ['pa1'], out=pa[1][:], lhsT=loT[d][64:128, :],
              rhs=W2A2[d][64:128, :], start=True, stop=True)
            X('dve', 'tensor_tensor', ['pa0', K('W0%d' % d)], [kd('sgm')], out=sgm[d], in0=pa[0][:], in1=W0[d], op=ALU.add)
            X('act', 'activation', [kd('sgm')], [kd('sgm')], out=sgm[d], in_=sgm[d], func=AF.Sigmoid)
            X('dve', 'tensor_tensor', ['pa1', K('A0%d' % d)], [kd('alp')], out=alp[d], in0=pa[1][:], in1=A0[d], op=ALU.add)
            X('act', 'activation', [kd('alp')], [kd('alp')], out=alp[d], in_=alp[d], func=AF.Sigmoid)
            if PS < 2:
                return
            X('pe', 'matmul', [kd('sgm'), 'tri'], ['pa0'], out=pa[0][:], lhsT=tri[:, 2 * d + 1, :], rhs=sgm[d], start=True,
              stop=True)
            X('pe', 'matmul', [kd('sgm'), 'tri'], ['pa1'], out=pa[1][:], lhsT=tri[:, 2 * d, :], rhs=sgm[d], start=True,
              stop=True)
            X('act', 'activation', ['pa0'], [kd('Epl')], out=Epl[d], in_=pa[0][:], func=AF.Exp)
            X('act', 'activation', ['pa0'], [kd('Emi')], out=Emi[d], in_=pa[0][:], func=AF.Exp, scale=-1.0)
            X('act', 'activation', ['pa1'], [kd('Epr')], out=Epr[d], in_=pa[1][:], func=AF.Exp)
            if PS < 3:
                return
            for p_ in range(4):
                X('pe', 'matmul', [kd('sgm'), 'negcol'], ['pa0'], out=pa[0][:, p_:p_ + 1],
                  lhsT=sgm[d][:, p_ * 128:(p_ + 1) * 128], rhs=negcol[:, 0:1], start=True, stop=True)
            X('act', 'activation', ['pa0'], [kd('PC')], out=PC[d], in_=pa[0][:, 0:4], func=AF.Exp)
            if PS < 4:
                return
            X('pool', 'tensor_tensor', [kd('u'), K('KKB')], [kd('kk')], out=kk[d], in0=kb_, in1=KKB, op=ALU.mult)
            X('pool', 'tensor_tensor', [kd('kk')], [kd('tm')], out=tm[d], in0=kk[d], in1=kk[d], op=ALU.mult)
            if PS < 4.2:
                return
            X('dve', 'tensor_reduce', [kd('tm')], [kd('ss8')], out=ss8[d][:, 0:8], in_=v8(tm[d]), axis=AX.X, op=ALU.add)
            X('act', 'activation', [kd('ss8')], [kd('ss8')], out=ss8[d][:, 8:16], in_=ss8[d][:, 0:8], func=AF.Sqrt)
            X('dve', 'tensor_scalar', [kd('ss8')], [kd('ss8')], out=ss8[d][:, 8:16], in0=ss8[d][:, 8:16], scalar1=1e-12,
              scalar2=None, op0=ALU.max)
            X('dve', 'reciprocal', [kd('ss8')], [kd('ss8')], out=ss8[d][:, 16:24], in_=ss8[d][:, 8:16])
            if PS < 4.4:
                return
            X('pool', 'tensor_tensor', [kd('kk'), kd('ss8')], [kd('kk')], out=v8(kk[d]), in0=v8(kk[d]),
              in1=ss8[d][:, 16:24].unsqueeze(2).to_broadcast([128, 8, 64]), op=ALU.mult)
            if PS < 4.6:
                return
            X('dve', 'tensor_tensor', [kd('alp'), K('KAB')], [kd('kx')], out=kx[d], in0=alp[d], in1=KAB, op=ALU.mult)
            X('dve', 'tensor_tensor', [kd('kx'), K('KAB')], [kd('kx')], out=kx[d], in0=kx[d], in1=KAB, op=ALU.subtract)
            X('dve', 'tensor_tensor', [kd('kx'), kd('u')], [kd('kx')], out=kx[d], in0=kx[d], in1=kb_, op=ALU.mult)
            X('dve', 'tensor_tensor', [kd('kx'), kd('u')], [kd('kx')], out=kx[d], in0=kx[d], in1=kb_, op=ALU.add)
            X('pool', 'tensor_tensor', [kd('kk'), kd('alp')], [kd('bb')], out=bb[d], in0=kk[d], in1=alp[d], op=ALU.mult)
            if PS < 5:
                return
            v4 = lambda a: a.rearrange("p (a c) -> p a c", a=4)
            X('dve', 'tensor_tensor', [kd('kk'), kd('Epr')], [kd('tm')], out=tm[d], in0=kk[d], in1=Epr[d], op=ALU.mult)
            X('dve', 'tensor_tensor', [kd('tm'), K('NEGB')], [kd('ARs')], out=ARs[d][:, 0, :], in0=tm[d], in1=NEGB,
              op=ALU.mult)
            X('dve', 'tensor_tensor', [kd('u'), kd('Epl')], [kd('ARs')], out=ARs[d][:, 1, :], in0=r_, in1=Epl[d],
              op=ALU.mult)
            X('dve', 'tensor_tensor', [kd('kx'), kd('Emi')], [kd('Kt')], out=Kt[d], in0=kx[d], in1=Emi[d], op=ALU.mult)
            X('dve', 'tensor_tensor', [kd('bb'), kd('Emi')], [kd('Bt')], out=Bt[d], in0=bb[d], in1=Emi[d], op=ALU.mult)
            X('dve', 'tensor_copy', [kd('u')], [kd('Vt')], out=Vt[d], in_=vb_)
            if dbg and os.environ.get('DBG_LIST'):
                lst = [tuple(int(v) for v in it.split(':')) for it in os.environ['DBG_LIST'].split(',')]
                if (d, tt) in lst:
                    qi = lst.index((d, tt))
                    DMA('sp', dbgo[:, qi * 512:(qi + 1) * 512], Emi[d], [kd('Emi')], ['dbgo%d' % qi])
            if PS < 6:
                return
            for p_ in range(4):
                pb = next_ptr()
                X('pe', 'transpose', [kd('ARs'), 'idb'], ['ptr%d' % pb], out=ptr[pb][:, 0:128], in_=ARs[d][:, 0, p_ * 128:(p_ + 1) * 128],
                  identity=idb[:])
                X('pe', 'transpose', [kd('ARs'), 'idb'], ['ptr%d' % pb], out=ptr[pb][:, 128:256], in_=ARs[d][:, 1, p_ * 128:(p_ + 1) * 128],
                  identity=idb[:])
                X('act', 'activation', ['ptr%d' % pb], [kd('ARt')], out=ARt[d][:, p_, :, :].rearrange("p x c -> p (x c)"),
                  in_=ptr[pb][:, 0:256], func=AF.Copy)
                pb = next_ptr()
                X('pe', 'transpose', [kd('Kt'), 'idb'], ['ptr%d' % pb], out=ptr[pb][:, 0:128],
                  in_=Kt[d][:, p_ * 128:(p_ + 1) * 128], identity=idb[:])
                X('pe', 'transpose', [kd('Bt'), 'idb'], ['ptr%d' % pb], out=ptr[pb][:, 128:256],
                  in_=Bt[d][:, p_ * 128:(p_ + 1) * 128], identity=idb[:])
                X('act', 'activation', ['ptr%d' % pb], [kd('KTt'), kd('BTt')],
                  out=KBt[d][:, p_, :, :].rearrange("p x c -> p (x c)"), in_=ptr[pb][:, 0:256], func=AF.Copy)

        def head_gen(d, h, slot):
            kd = lambda n: K('%s%d' % (n, d))
            par = h % 2
            base = par * 64
            p_ = h // 2
            z = 2 + slot
            ev_eng = 'act' if slot < 2 else 'dve'

            def evac(reads, writes, out, in_):
                if ev_eng == 'act':
                    X('act', 'activation', reads, writes, out=out, in_=in_, func=AF.Copy)
                else:
                    X('dve', 'tensor_copy', reads, writes, out=out, in_=in_)
            arf = ARt[d][base:base + 64, p_, :, :].rearrange("p x c -> p (x c)")
            aT = ARt[d][base:base + 64, p_, 0, :]
            X('pe', 'matmul', [kd('BTt'), kd('ARt')], ['pa0'], out=pa[0][:, 0:256], lhsT=KBt[d][base:base + 64, p_, 1, :],
              rhs=arf, start=True, stop=True)
            X('pe', 'matmul', [kd('KTt'), kd('ARt')], ['pa0'], out=pa[0][:, 256:512], lhsT=KBt[d][base:base + 64, p_, 0, :],
              rhs=arf, start=True, stop=True)
            X('pe', 'matmul', [kd('BTt'), kd('ARt')], ['pa1'], out=pa[1][:, 0:128], lhsT=aT,
              rhs=KBt[d][base:base + 64, p_, 1, :], start=True, stop=True)
            sck = kd('SC%d_' % h)
            X('dve', 'tensor_tensor', ['pa0', K('mk4%d' % d)], [sck], out=SC[d][:, h, :], in0=pa[0][:], in1=mk4[d],
              op=ALU.mult)
            l0k = K('L0%d' % slot)
            X('dve', 'tensor_tensor', ['pa1', 'msk'], [l0k], out=L0[slot], in0=pa[1][:, 0:128], in1=msk[:, 2 - 2 * d, :],
              op=ALU.mult)
            yield
            N0 = SC[d][:, h, 0:128]
            wk = [K('Wk%d_%d' % (slot, q)) for q in range(2)]
            W = Wk[slot]
            X('pe', 'matmul', [sck, 'idb'], ['pa%d' % z], out=pa[z][:, 0:128], lhsT=idb[:], rhs=N0, start=True, stop=False)
            X('pe', 'matmul', [sck, 'idb'], ['pa%d' % z], out=pa[z][:, 0:128], lhsT=idb[:], rhs=idb[:], start=False, stop=True)
            X('pe', 'matmul', [l0k, sck], ['pa%d' % z], out=pa[z][:, 128:256], lhsT=L0[slot], rhs=N0, start=True, stop=True)
            X('pe', 'matmul', [l0k, sck], ['pa%d' % z], out=pa[z][:, 256:384], lhsT=N0, rhs=L0[slot], start=True, stop=True)
            evac(['pa%d' % z], [wk[0]], W[0][:, 0:384], pa[z][:, 0:384])
            yield
            cur = 0
            for k_ in range(1, 6):
                a_, b_ = cur, 1 - cur
                rk = [wk[a_], 'idb']
                X('pe', 'matmul', rk, ['pa%d' % z], out=pa[z][:, 0:128], lhsT=idb[:], rhs=W[a_][:, 0:128], start=True, stop=False)
                X('pe', 'matmul', rk, ['pa%d' % z], out=pa[z][:, 0:128], lhsT=W[a_][:, 256:384], rhs=W[a_][:, 0:128],
                  start=False, stop=True)
                X('pe', 'matmul', rk, ['pa%d' % z], out=pa[z][:, 128:256], lhsT=W[a_][:, 256:384], rhs=W[a_][:, 128:256],
                  start=True, stop=True)
                X('pe', 'matmul', rk, ['pa%d' % z], out=pa[z][:, 256:384], lhsT=W[a_][:, 128:256], rhs=W[a_][:, 256:384],
                  start=True, stop=True)
                evac(['pa%d' % z], [wk[b_]], W[b_][:, 0:384], pa[z][:, 0:384])
                cur = b_
                yield
            rk = [wk[cur], 'idb']
            X('pe', 'matmul', rk, ['pa%d' % z], out=pa[z][:, 0:128], lhsT=idb[:], rhs=W[cur][:, 0:128], start=True, stop=False)
            X('pe', 'matmul', rk, ['pa%d' % z], out=pa[z][:, 0:128], lhsT=W[cur][:, 256:384], rhs=W[cur][:, 0:128],
              start=False, stop=True)
            evac(['pa%d' % z], [kd('TI%d_' % h)], TI[d][:, h, :], pa[z][:, 0:128])
            yield

        def run_heads():
            combos = [(d, h) for h in range(8) for d in range(2)]
            for g0 in range(0, 16, 4):
                alive = [head_gen(d, h, slot) for slot, (d, h) in enumerate(combos[g0:g0 + 4])]
                while alive:
                    nxt = []
                    for g in alive:
                        try:
                            next(g)
                            nxt.append(g)
                        except StopIteration:
                            pass
                    alive = nxt

        def chain_gen(d, tt, bx, bu):
            kd = lambda n: K('%s%d' % (n, d))
            for p_ in range(4):
                hk = kd('H%d_' % p_)
                for hh in range(2):
                    h = 2 * p_ + hh
                    base = hh * 64
                    cs = slice(hh * 64, hh * 64 + 64)
                    X('pe', 'matmul', [kd('SC%d_' % h), kd('Vt')], ['pa%d' % bx], out=pa[bx][:, cs], lhsT=SC[d][:, h, 256:384],
                      rhs=Vt[d][:, h * 64:(h + 1) * 64], start=True, stop=False)
                    X('pe', 'matmul', [kd('ARt'), hk + 'b'], ['pa%d' % bx], out=pa[bx][:, cs], lhsT=ARt[d][base:base + 64, p_, 0, :],
                      rhs=Hbf[d][base:base + 64, p_, base:base + 64], start=False, stop=True)
                X('act', 'activation', ['pa%d' % bx], [kd('Xb%d_' % p_)], out=Xb[d][:, p_, :], in_=pa[bx][:, 0:128], func=AF.Copy)
                yield
                for hh in range(2):
                    h = 2 * p_ + hh
                    cs = slice(hh * 64, hh * 64 + 64)
                    X('pe', 'matmul', [kd('TI%d_' % h), kd('Xb%d_' % p_)], ['pa%d' % bu], out=pa[bu][:, cs], lhsT=TI[d][:, h, :],
                      rhs=Xb[d][:, p_, cs], start=True, stop=True)
                X('dve', 'tensor_copy', ['pa%d' % bu], [kd('Ub%d_' % p_)], out=Ub[d][:, p_, :], in_=pa[bu][:, 0:128])
                yield
                for hh in range(2):
                    h = 2 * p_ + hh
                    base = hh * 64
                    cs = slice(hh * 64, hh * 64 + 64)
                    ys = slice(128 + hh * 64, 128 + hh * 64 + 64)
                    X('pe', 'matmul', [kd('ARt'), hk + 'b'], ['pa%d' % bx], out=pa[bx][:, ys], lhsT=ARt[d][base:base + 64, p_, 1, :],
                      rhs=Hbf[d][base:base + 64, p_, base:base + 64], start=True, stop=False)
                    X('pe', 'matmul', [kd('SC%d_' % h), kd('Vt')], ['pa%d' % bx], out=pa[bx][:, ys], lhsT=SC[d][:, h, 384:512],
                      rhs=Vt[d][:, h * 64:(h + 1) * 64], start=False, stop=False)
                    X('pe', 'matmul', [kd('SC%d_' % h), kd('Ub%d_' % p_)], ['pa%d' % bx], out=pa[bx][:, ys], lhsT=SC[d][:, h, 128:256],
                      rhs=Ub[d][:, p_, cs], start=False, stop=True)
                X('dve', 'tensor_tensor', ['pa%d' % bx, K('Yacc%d' % tt)], [K('Yacc%d' % tt)], out=Yacc[:, tt, p_ * 128:(p_ + 1) * 128],
                  in0=pa[bx][:, 128:256], in1=Yacc[:, tt, p_ * 128:(p_ + 1) * 128], op=ALU.add)
                ps_ = slice(p_ * 128, (p_ + 1) * 128)
                X('pe', 'matmul', [kd('Kt'), kd('Vt')], ['pa%d' % bu], out=pa[bu][:, 128:256], lhsT=Kt[d][:, ps_], rhs=Vt[d][:, ps_],
                  start=True, stop=False)
                X('pe', 'matmul', [kd('Bt'), kd('Ub%d_' % p_)], ['pa%d' % bu], out=pa[bu][:, 128:256], lhsT=Bt[d][:, ps_],
                  rhs=Ub[d][:, p_, :], start=False, stop=True)
                X('dve', 'tensor_tensor', ['pa%d' % bu, hk + 'f'], [kd('t32')], out=t32[d], in0=pa[bu][:, 128:256], in1=H32[d][:, p_, :],
                  op=ALU.add)
                X('dve', 'tensor_scalar', [kd('t32'), kd('PC')], [hk + 'f'], out=H32[d][:, p_, :], in0=t32[d],
                  scalar1=PC[d][:, p_:p_ + 1], scalar2=None, op0=ALU.mult)
                X('act', 'activation', [kd('t32'), kd('PC')], [hk + 'b'], out=Hbf[d][:, p_, :], in_=t32[d], func=AF.Copy,
                  scale=PC[d][:, p_:p_ + 1])
                yield

        def run_chains(c):
            alive = [chain_gen(0, c, 2, 3), chain_gen(1, NT - 1 - c, 4, 5)]
            while alive:
                nxt = []
                for g in alive:
                    try:
                        next(g)
                        nxt.append(g)
                    except StopIteration:
                        pass
                alive = nxt

        RW = int(os.environ.get('RW_STAGE', '9'))
        for c in range(NT):
            if RW >= 2:
                prep(0, c)
                prep(1, NT - 1 - c)
            if RW >= 3:
                run_heads()
            if RW >= 4:
                run_chains(c)

        P.barrier()
        ph.off = mark
        uc = A_([128, 1536], F32)
        gtb = A_([128, 512], F32)
        yv = A_([128, 512], F32)
        sqv = A_([128, 512], F32)
        st8 = A_([128, 64], F32)
        for tt in range(NT):
            b = tt % 2
            r0 = tt * 128
            kb2 = lambda n: K('%s%d' % (n, b))
            DMA('sp', uc[b], ubd[r0:r0 + 128, 0:1536], ['ub%d' % tt], [kb2('uc')])
            DMA('sp', gtb[b], proj[r0:r0 + 128, 3072:3584], ['proj%d_6' % tt], [kb2('gtb')])
            X('act', 'activation', [kb2('gtb')], [kb2('gtb')], out=gtb[b], in_=gtb[b], func=AF.Silu)
            yt = Yacc[:, tt, :]
            S = st8[b]
            X('dve', 'tensor_reduce', [K('Yacc%d' % tt)], [kb2('st8')], out=S[:, 0:8], in_=v8(yt), axis=AX.X, op=ALU.add)
            X('pool', 'tensor_tensor', [K('Yacc%d' % tt)], [kb2('sqv')], out=sqv[b], in0=yt, in1=yt, op=ALU.mult)
            X('dve', 'tensor_reduce', [kb2('sqv')], [kb2('st8')], out=S[:, 8:16], in_=v8(sqv[b]), axis=AX.X, op=ALU.add)
            X('dve', 'tensor_scalar', [kb2('st8')], [kb2('st8')], out=S[:, 16:24], in0=S[:, 0:8], scalar1=1.0 / 64,
              scalar2=None, op0=ALU.mult)
            X('dve', 'tensor_tensor', [kb2('st8')], [kb2('st8')], out=S[:, 24:32], in0=S[:, 16:24], in1=S[:, 16:24],
              op=ALU.mult)
            X('dve', 'scalar_tensor_tensor', [kb2('st8')], [kb2('st8')], out=S[:, 32:40], in0=S[:, 8:16], scalar=1.0 / 64,
              in1=S[:, 24:32], op0=ALU.mult, op1=ALU.subtract)
            X('dve', 'tensor_scalar', [kb2('st8')], [kb2('st8')], out=S[:, 32:40], in0=S[:, 32:40], scalar1=64e-5,
              scalar2=None, op0=ALU.add)
            X('act', 'activation', [kb2('st8')], [kb2('st8')], out=S[:, 40:48], in_=S[:, 32:40], func=AF.Sqrt)
            X('dve', 'reciprocal', [kb2('st8')], [kb2('st8')], out=S[:, 48:56], in_=S[:, 40:48])
            X('dve', 'tensor_tensor', [K('Yacc%d' % tt), kb2('st8')], [kb2('yv')], out=v8(yv[b]), in0=v8(yt),
              in1=S[:, 16:24].unsqueeze(2).to_broadcast([128, 8, 64]), op=ALU.subtract)
            X('pool', 'tensor_tensor', [kb2('yv'), kb2('st8')], [kb2('yv')], out=v8(yv[b]), in0=v8(yv[b]),
              in1=S[:, 48:56].unsqueeze(2).to_broadcast([128, 8, 64]), op=ALU.mult)
            X('pool', 'tensor_tensor', [kb2('yv'), K('LNG')], [kb2('yv')], out=yv[b], in0=yv[b], in1=LNG, op=ALU.mult)
            X('pool', 'tensor_tensor', [kb2('yv'), K('LNB')], [kb2('yv')], out=yv[b], in0=yv[b], in1=LNB, op=ALU.add)
            X('dve', 'tensor_tensor', [kb2('uc')], [kb2('sqv')], out=sqv[b], in0=uc[b][:, 0:512], in1=uc[b][:, 512:1024],
              op=ALU.mult)
            X('pool', 'tensor_tensor', [kb2('sqv'), K('RKB')], [kb2('sqv')], out=sqv[b], in0=sqv[b], in1=RKB, op=ALU.mult)
            X('dve', 'tensor_reduce', [kb2('sqv')], [kb2('st8')], out=S[:, 56:64], in_=v8(sqv[b]), axis=AX.X, op=ALU.add)
            X('dve', 'tensor_tensor', [kb2('uc'), kb2('st8')], [kb2('sqv')], out=v8(sqv[b]), in0=v8(uc[b][:, 1024:1536]),
              in1=S[:, 56:64].unsqueeze(2).to_broadcast([128, 8, 64]), op=ALU.mult)
            X('pool', 'tensor_tensor', [kb2('yv'), kb2('sqv')], [kb2('yv')], out=yv[b], in0=yv[b], in1=sqv[b], op=ALU.add)
            X('pool', 'tensor_tensor', [kb2('yv'), kb2('gtb')], [kb2('yv')], out=yv[b], in0=yv[b], in1=gtb[b], op=ALU.mult)
            DMA('sp', mixd[r0:r0 + 128, 512:1024], yv[b], [kb2('yv')], ['mixd%d' % tt])

    def even_mixer(layer):
        i = layer // 2
        if 'a' in parts:
            attn_part(i)
        else:
            zero_mix(0, 512)
        if 'b' in parts:
            rwkv_part(i)
        else:
            zero_mix(512, 1024)

    EVEN_HOOK = globals().get('_even_mixer_builder')
    for s in range(nseq):
        xs = x[s * T:(s + 1) * T, :]
        ys = y[s * T:(s + 1) * T, :]
        first = True
        for layer in layers:
            hsrc = xs if first else ys
            j = layer // 2
            if layer % 2 == 1:
                prenorm_inproj(hsrc, layer, prm['odd_w_in'][j], ODD_IN)
                odd_mixer(layer)
                out_proj(prm['odd_w_out'][j], 2048, hsrc, ys, layer)
            else:
                prenorm_inproj(hsrc, layer, prm['even_w_in'][j], EVEN_IN)
                even_mixer(layer)
                out_proj(prm['even_w_out'][j], 1024, hsrc, ys, layer)
            first = False
    if dbg:
        print('instr counts', P.n_instr())
    return P.build()


def make_in_maps(inputs, ncores=8, nseq=4):
    consts = host_consts()
    x = np.ascontiguousarray(inputs['x'], dtype=np.float32)
    maps = []
    for c in range(ncores):
        m = {'x': x[c * nseq:(c + 1) * nseq].reshape(nseq * T, D)}
        for k, shp in PARAM_SHAPES.items():
            m[k] = np.ascontiguousarray(inputs[k], dtype=np.float32).reshape(shp)
        m.update({'c_' + k: v for k, v in consts.items()})
        maps.append(m)
    return maps


def kernel(**inputs):
    nc = build_program()
    maps = make_in_maps(inputs)
    res = run_bass_kernel_spmd(nc, maps, core_ids=list(range(8)))
    out = np.concatenate([np.asarray(r['y']).reshape(4, T, D) for r in res.results], axis=0)
    return out.astype(np.float32)
```

```python
import os
import numpy as np
from contextlib import ExitStack
import concourse.bass as bass
import concourse.mybir as mybir
from concourse.bass_utils import run_bass_kernel_spmd

F32 = mybir.dt.float32
BF16 = mybir.dt.bfloat16
AF = mybir.ActivationFunctionType
ALU = mybir.AluOpType
AX = mybir.AxisListType

ENGS = ['pe', 'dve', 'act', 'pool', 'sp']
ENG_ATTR = {'pe': 'tensor', 'dve': 'vector', 'act': 'scalar', 'pool': 'gpsimd', 'sp': 'sync'}
N_DMA_SEM = 12

T = 2048
D = 1024
NT = T // 128
EVEN_IN = 3584
ODD_IN = 6144
EPS = 1e-6
WDEC = float(np.exp(-0.5))


class Prog:
    def __init__(self, nc):
        self.nc = nc
        self.st = ExitStack()
        self.ops = {e: [] for e in ENGS}
        self.cnt = {e: 0 for e in ENGS}
        self.seen = {e: {} for e in ENGS}
        self.last_w = {}
        self.readers = {}
        self.dma_rr = {e: 0 for e in ENGS}
        self.dma_use = {}
        self.semnames = set()

    def sb(self, name, shape, dt):
        return self.st.enter_context(self.nc.sbuf_tensor(name, list(shape), dt))

    def ps(self, name, shape, dt):
        return self.st.enter_context(self.nc.psum_tensor(name, list(shape), dt))

    def dram(self, name, shape, dt, kind=None):
        if kind is None:
            return self.nc.dram_tensor(name, list(shape), dt).ap()
        return self.nc.dram_tensor(name, list(shape), dt, kind=kind).ap()

    def _deps(self, eng, reads, writes):
        deps = []
        for r in reads:
            w = self.last_w.get(r)
            if w is not None:
                deps.append(w)
        for w_ in writes:
            w = self.last_w.get(w_)
            if w is not None:
                deps.append(w)
            deps.extend(self.readers.get(w_, ()))
        waits = []
        seen = self.seen[eng]
        best = {}
        for s, v in deps:
            if seen.get(s, 0) >= v:
                continue
            if best.get(s, 0) < v:
                best[s] = v
        for s, v in best.items():
            seen[s] = v
            waits.append((s, v))
        return waits

    def _update(self, me, reads, writes):
        for r in reads:
            self.readers.setdefault(r, []).append(me)
        for w in writes:
            self.last_w[w] = me
            self.readers[w] = []

    def op(self, eng, fn, reads=(), writes=()):
        waits = self._deps(eng, reads, writes)
        self.cnt[eng] += 1
        s = 'c_' + eng
        self.semnames.add(s)
        me = (s, self.cnt[eng])
        if eng == 'pe':
            self.seen[eng][s] = self.cnt[eng]
        self.ops[eng].append((waits, fn, s, 1))
        self._update(me, reads, writes)

    def dma(self, eng, fn, reads=(), writes=()):
        i = self.dma_rr[eng]
        self.dma_rr[eng] = (i + 1) % N_DMA_SEM
        s = 'd_%s_%d' % (eng, i)
        self.semnames.add(s)
        u = self.dma_use.get(s, 0)
        waits = self._deps(eng, reads, writes)
        if u > 0 and self.seen[eng].get(s, 0) < 16 * u:
            waits.append((s, 16 * u))
            self.seen[eng][s] = 16 * u
        self.dma_use[s] = u + 1
        me = (s, 16 * (u + 1))
        self.ops[eng].append((waits, fn, s, 16))
        self._update(me, reads, writes)

    def barrier(self):
        for e in ENGS:
            waits = []
            seen = self.seen[e]
            for s, u in self.dma_use.items():
                if seen.get(s, 0) < 16 * u:
                    waits.append((s, 16 * u))
                    seen[s] = 16 * u
            for o in ENGS:
                if self.cnt[o] > 0 and seen.get('c_' + o, 0) < self.cnt[o]:
                    waits.append(('c_' + o, self.cnt[o]))
                    seen['c_' + o] = self.cnt[o]
            if waits:
                self.ops[e].append((waits, None, None, 0))

    def finish(self):
        eng = 'sp'
        waits = []
        for s, u in self.dma_use.items():
            if self.seen[eng].get(s, 0) < 16 * u:
                waits.append((s, 16 * u))
        for e in ENGS:
            if self.cnt[e] > 0 and e != eng:
                waits.append(('c_' + e, self.cnt[e]))
        self.ops[eng].append((waits, None, None, 0))

    def build(self):
        nc = self.nc
        self.finish()
        sems = {}
        for name in sorted(self.semnames):
            sems[name] = self.st.enter_context(nc.semaphore(name))
        block = self.st.enter_context(nc.Block())
        for eng in ENGS:
            if not self.ops[eng]:
                continue
            deco = getattr(block, ENG_ATTR[eng])

            def body(e, eng=eng):
                for waits, fn, s, inc in self.ops[eng]:
                    for ws, wv in waits:
                        e.wait_ge(sems[ws], wv)
                    if fn is not None:
                        ins = fn(e)
                        ins.then_inc(sems[s], inc)
            deco(body)
        self.st.close()
        return nc

    def n_instr(self):
        return {e: len(self.ops[e]) for e in ENGS}


def _rope_tables(dim):
    half = dim // 2
    q = dim // 4
    t = np.arange(T)
    row = (t // 64).astype(np.float32)
    col = (t % 64).astype(np.float32)
    inv = (10000.0 ** (-np.arange(0, half, 2, dtype=np.float32) / half)).astype(np.float32)
    ar = row[:, None] * inv
    ac = col[:, None] * inv
    cos = np.concatenate([np.cos(ar), np.cos(ar), np.cos(ac), np.cos(ac)], axis=1)
    sin = np.concatenate([-np.sin(ar), np.sin(ar), -np.sin(ac), np.sin(ac)], axis=1)
    return cos.astype(np.float32), sin.astype(np.float32)


def host_consts():
    c = {}
    c['ident'] = np.eye(128, dtype=np.float32)
    i = np.arange(128)
    su = (i[:, None] < i[None, :]).astype(np.float32)
    iu = (i[:, None] <= i[None, :]).astype(np.float32)
    sl = (i[:, None] > i[None, :]).astype(np.float32)
    il = (i[:, None] >= i[None, :]).astype(np.float32)
    m = np.stack([su, iu, sl, il], axis=1)
    c['masks'] = m.reshape(128, 512).astype(np.float32)
    c['tri'] = (-WDEC * m).reshape(128, 512).astype(np.float32)
    cos64, sin64 = _rope_tables(64)
    c['cosA'] = np.tile(cos64, (1, 10)).astype(np.float32)
    c['sinA'] = np.tile(sin64, (1, 10)).astype(np.float32)
    cos256, sin256 = _rope_tables(256)
    c['cosC'] = np.concatenate([cos256, cos256 / 16.0], axis=1).astype(np.float32)
    c['sinC'] = np.concatenate([sin256, sin256 / 16.0], axis=1).astype(np.float32)
    lgf = np.log(1.0 - 2.0 ** (-5.0 - np.arange(4, dtype=np.float64)))
    lgb = lgf[::-1]
    cc = np.arange(3968)[None, :] - 1920 - np.arange(128)[:, None]
    td = np.zeros((4, 128, 3968), np.float32)
    for h in range(4):
        td[h] = np.where(cc >= 0, np.exp(cc * lgf[h]), np.exp(-cc * lgb[h])).astype(np.float32)
    c['retdec'] = td.transpose(1, 0, 2).reshape(128, 4 * 3968).copy()
    c['negcol'] = np.full((128, 1), -WDEC, np.float32)
    return c


CONST_SHAPES = {'ident': [128, 128], 'masks': [128, 512], 'tri': [128, 512], 'cosA': [T, 640], 'sinA': [T, 640],
                'cosC': [T, 512], 'sinC': [T, 512], 'retdec': [128, 4 * 3968], 'negcol': [128, 1]}

PARAM_SHAPES = {
    'pre_gain': [4, 1024], 'post_gain': [4, 1024], 'even_w_in': [2, 1024, 3584], 'even_mu': [2, 1792],
    'even_q_gain': [2, 64], 'even_k_gain': [2, 64], 'even_k_k': [2, 512], 'even_k_a': [2, 512],
    'even_r_k': [2, 512], 'even_w0_f': [2, 512], 'even_w2_f': [2, 64, 512], 'even_a0_f': [2, 512],
    'even_a2_f': [2, 64, 512], 'even_w0_b': [2, 512], 'even_w2_b': [2, 64, 512], 'even_a0_b': [2, 512],
    'even_a2_b': [2, 64, 512], 'even_lnx_g': [2, 512], 'even_lnx_b': [2, 512], 'even_w_out': [2, 1024, 1024],
    'odd_w_in': [2, 1024, 6144], 'odd_gn_g': [2, 2048], 'odd_w_out': [2, 2048, 1024],
}


ARENA = 84 * 1024


def build_program(nseq=4, layers=(0, 1, 2, 3), dbg=False, parts=('a', 'b')):
    nc = bass.Bass("TRN2", target_bir_lowering=False)
    P = Prog(nc)

    def X(eng, meth, reads, writes, **kw):
        P.op(eng, lambda e: getattr(e, meth)(**kw), reads, writes)

    def DMA(eng, out, in_, reads, writes):
        P.dma(eng, lambda e: e.dma_start(out=out, in_=in_), reads, writes)

    x = P.dram('x', [nseq * T, D], F32, 'ExternalInput')
    y = P.dram('y', [nseq * T, D], F32, 'ExternalOutput')
    prm = {k: P.dram(k, shp, F32, 'ExternalInput') for k, shp in PARAM_SHAPES.items()}
    cst = {k: P.dram('c_' + k, shp, F32, 'ExternalInput') for k, shp in CONST_SHAPES.items()}
    proj = P.dram('proj', [T, ODD_IN], F32)
    mixd = P.dram('mixd', [T, 2048], F32)
    ubd = P.dram('ubd', [T, 1792], F32)
    dbgo = P.dram('dbgo', [128, 8 * 512], F32, 'ExternalOutput') if dbg else None

    idb = P.sb('idb', [128, 128], BF16)
    msk = P.sb('msk', [128, 4, 128], BF16)
    tri = P.sb('tri', [128, 4, 128], F32)
    negcol = P.sb('negcol', [128, 1], F32)
    gpre = P.sb('gpre', [128, D], F32)
    gpost = P.sb('gpost', [128, D], F32)
    hx = [P.sb('hx%d' % i, [128, D], F32) for i in range(2)]
    junk = P.sb('junk', [128, 2048], BF16)
    hnb = [P.sb('hnb%d' % i, [128, 2048], BF16) for i in range(2)]
    stt = [P.sb('stt%d' % i, [128, 8], F32) for i in range(2)]
    ev = [P.sb('ev%d' % i, [128, 512], F32) for i in range(3)]
    arena = P.sb('arena', [128, ARENA], BF16)
    ptr = [P.ps('ptr%d' % i, [128, 1024], BF16) for i in range(2)]
    pa = [P.ps('pa%d' % i, [128, 512], F32) for i in range(6)]

    DMA('pool', idb[:], cst['ident'], [], ['idb'])
    DMA('pool', msk[:].rearrange("p a b -> p (a b)"), cst['masks'], [], ['msk'])
    DMA('sp', tri[:].rearrange("p a b -> p (a b)"), cst['tri'], [], ['tri'])
    DMA('sp', negcol[:], cst['negcol'], [], ['negcol'])

    cnt = {'ev': 0, 'ptr': 0, 'ph': 0}

    class Phase:
        def __init__(self):
            P.barrier()
            self.off = 0
            cnt['ph'] += 1
            self.id = cnt['ph']

        def alloc(self, shape, dt):
            n = int(np.prod(shape[1:]))
            sz = n * (2 if dt == F32 else 1)
            self.off = (self.off + 7) // 8 * 8
            v = arena[:, self.off:self.off + sz]
            self.off += sz
            assert self.off <= ARENA, ('arena overflow', self.off)
            if dt == F32:
                v = v.bitcast(F32)
            if len(shape) == 3:
                v = v.rearrange("p (a b) -> p a b", a=shape[1])
            elif len(shape) == 4:
                v = v.rearrange("p (a b c) -> p a b c", a=shape[1], b=shape[2])
            return v

        def key(self, name):
            return 'ph%d_%s' % (self.id, name)

    def next_ev():
        cnt['ev'] = (cnt['ev'] + 1) % 3
        return cnt['ev']

    def next_ptr():
        cnt['ptr'] = (cnt['ptr'] + 1) % 2
        return cnt['ptr']

    def rms_rows(src_reads, src, b, width, eps):
        X('act', 'activation', src_reads, ['junk', 'stt%d' % b], out=junk[:, 0:width], in_=src, func=AF.Square,
          accum_out=stt[b][:, 0:1])
        X('dve', 'tensor_scalar', ['stt%d' % b], ['stt%d' % b], out=stt[b][:, 1:2], in0=stt[b][:, 0:1],
          scalar1=1.0 / width, scalar2=eps, op0=ALU.mult, op1=ALU.add)
        X('act', 'activation', ['stt%d' % b], ['stt%d' % b], out=stt[b][:, 2:3], in_=stt[b][:, 1:2], func=AF.Sqrt)
        X('dve', 'reciprocal', ['stt%d' % b], ['stt%d' % b], out=stt[b][:, 3:4], in_=stt[b][:, 2:3])

    def transpose_tile(src_bf, src_key, nchunks, dst_fn, dst_key):
        for c0 in range(0, nchunks, 4):
            n = min(4, nchunks - c0)
            pb = next_ptr()
            for k in range(n):
                X('pe', 'transpose', [src_key, 'idb'], ['ptr%d' % pb], out=ptr[pb][:, k * 128:(k + 1) * 128],
                  in_=src_bf[:, (c0 + k) * 128:(c0 + k + 1) * 128], identity=idb[:])
            src = ptr[pb][:, 0:n * 128].rearrange("p (k t) -> p k t", k=n)
            if (c0 // 4) % 2 == 0:
                X('act', 'activation', ['ptr%d' % pb], [dst_key], out=dst_fn(c0, n), in_=src, func=AF.Copy)
            else:
                X('dve', 'tensor_copy', ['ptr%d' % pb], [dst_key], out=dst_fn(c0, n), in_=src)

    def rope(eng2, xin, xkey, cos, sin, ckey, tmpA, tmpB, tkey, out_bf, okey, H, dim):
        q = dim // 4
        v5 = lambda a: a.rearrange("p (h x f q) -> p h x f q", h=H, x=2, f=2)
        X('dve', 'tensor_tensor', [xkey, ckey], [tkey + 'A'], out=tmpA, in0=xin, in1=cos, op=ALU.mult)
        for f in range(2):
            X(eng2, 'tensor_tensor', [xkey, ckey], [tkey + 'B'], out=v5(tmpB)[:, :, :, f, :],
              in0=v5(xin)[:, :, :, 1 - f, :], in1=v5(sin)[:, :, :, f, :], op=ALU.mult)
        X('dve', 'tensor_tensor', [tkey + 'A', tkey + 'B'], [okey], out=out_bf, in0=tmpA, in1=tmpB, op=ALU.add)

    def prenorm_inproj(hsrc, layer, w_dram, ncols):
        ph = Phase()
        hnT = ph.alloc([128, 8, T], BF16)
        wt = [ph.alloc([128, 8, 512], BF16) for _ in range(2)]
        K = ph.key
        DMA('sp', gpre[:], prm['pre_gain'][layer:layer + 1, :].partition_broadcast(128), [], ['gpre'])
        for tt in range(NT):
            b = tt % 2
            DMA('sp', hx[b][:], hsrc[tt * 128:(tt + 1) * 128, :], ['h%d' % tt], ['hx%d' % b])
            rms_rows(['hx%d' % b], hx[b][:], b, D, EPS)
            X('dve', 'scalar_tensor_tensor', ['hx%d' % b, 'stt%d' % b, 'gpre'], ['hnb%d' % b], out=hnb[b][:, 0:D],
              in0=hx[b][:], scalar=stt[b][:, 3:4], in1=gpre[:], op0=ALU.mult, op1=ALU.mult)
            transpose_tile(hnb[b], 'hnb%d' % b, 8,
                           lambda c0, n, tt=tt: hnT[:, c0:c0 + n, tt * 128:(tt + 1) * 128], K('hnT%d' % tt))
        for cb in range(ncols // 512):
            wb = cb % 2
            DMA('pool', wt[wb], w_dram[:, cb * 512:(cb + 1) * 512].rearrange("(kc p) n -> p kc n", p=128),
                [], [K('wt%d' % wb)])
            for tt in range(NT):
                pb = (cb * NT + tt) % 2
                for kc in range(8):
                    X('pe', 'matmul', [K('hnT%d' % tt), K('wt%d' % wb)], ['pa%d' % pb], out=pa[pb][:],
                      lhsT=hnT[:, kc, tt * 128:(tt + 1) * 128], rhs=wt[wb][:, kc, :], start=(kc == 0), stop=(kc == 7))
                e = next_ev()
                if tt % 2 == 0:
                    X('act', 'activation', ['pa%d' % pb], ['ev%d' % e], out=ev[e][:], in_=pa[pb][:], func=AF.Copy)
                else:
                    X('dve', 'tensor_copy', ['pa%d' % pb], ['ev%d' % e], out=ev[e][:], in_=pa[pb][:])
                DMA('sp', proj[tt * 128:(tt + 1) * 128, cb * 512:(cb + 1) * 512], ev[e][:], ['ev%d' % e],
                    ['proj%d_%d' % (tt, cb)])

    def out_proj(w_dram, Kdim, hsrc, hdst, layer):
        ph = Phase()
        KC = Kdim // 128
        wout = ph.alloc([128, KC, 1024], BF16)
        xT = [ph.alloc([128, KC, 128], BF16) for _ in range(2)]
        K = ph.key
        DMA('sp', gpost[:], prm['post_gain'][layer:layer + 1, :].partition_broadcast(128), [], ['gpost'])
        for k4 in range(0, KC, 4):
            DMA('pool', wout[:, k4:k4 + 4, :],
                w_dram[k4 * 128:(k4 + 4) * 128, :].rearrange("(kc p) n -> p kc n", p=128), [], [K('wout')])
        for tt in range(NT):
            b = tt % 2
            DMA('pool', hnb[b][:, 0:Kdim], mixd[tt * 128:(tt + 1) * 128, 0:Kdim], ['mixd%d' % tt], ['hnb%d' % b])
            transpose_tile(hnb[b], 'hnb%d' % b, KC, lambda c0, n, b=b: xT[b][:, c0:c0 + n, :], K('xT%d' % b))
            DMA('sp', hx[b][:], hsrc[tt * 128:(tt + 1) * 128, :], ['h%d' % tt], ['hx%d' % b])
            for half in range(2):
                for kc in range(KC):
                    X('pe', 'matmul', [K('xT%d' % b), K('wout')], ['pa%d' % half], out=pa[half][:],
                      lhsT=xT[b][:, kc, :], rhs=wout[:, kc, half * 512:(half + 1) * 512], start=(kc == 0),
                      stop=(kc == KC - 1))
            X('act', 'activation', ['pa0'], ['junk', 'stt%d' % b], out=junk[:, 0:512], in_=pa[0][:], func=AF.Square,
              accum_out=stt[b][:, 4:5])
            X('act', 'activation', ['pa1'], ['junk', 'stt%d' % b], out=junk[:, 512:1024], in_=pa[1][:], func=AF.Square,
              accum_out=stt[b][:, 5:6])
            X('dve', 'tensor_tensor', ['stt%d' % b], ['stt%d' % b], out=stt[b][:, 0:1], in0=stt[b][:, 4:5],
              in1=stt[b][:, 5:6], op=ALU.add)
            X('dve', 'tensor_scalar', ['stt%d' % b], ['stt%d' % b], out=stt[b][:, 1:2], in0=stt[b][:, 0:1],
              scalar1=1.0 / D, scalar2=EPS, op0=ALU.mult, op1=ALU.add)
            X('act', 'activation', ['stt%d' % b], ['stt%d' % b], out=stt[b][:, 2:3], in_=stt[b][:, 1:2], func=AF.Sqrt)
            X('dve', 'reciprocal', ['stt%d' % b], ['stt%d' % b], out=stt[b][:, 3:4], in_=stt[b][:, 2:3])
            for half in range(2):
                e = next_ev()
                X('dve', 'scalar_tensor_tensor', ['pa%d' % half, 'stt%d' % b, 'gpost'], ['ev%d' % e], out=ev[e][:],
                  in0=pa[half][:], scalar=stt[b][:, 3:4], in1=gpost[:, half * 512:(half + 1) * 512], op0=ALU.mult,
                  op1=ALU.mult)
                X('pool', 'tensor_tensor', ['ev%d' % e, 'hx%d' % b], ['ev%d' % e], out=ev[e][:], in0=ev[e][:],
                  in1=hx[b][:, half * 512:(half + 1) * 512], op=ALU.add)
                DMA('sp', hdst[tt * 128:(tt + 1) * 128, half * 512:(half + 1) * 512], ev[e][:], ['ev%d' % e],
                    ['h%d' % tt])

    def odd_mixer(layer):
        j = layer // 2
        ph = Phase()
        K = ph.key
        qkT = ph.alloc([128, 4, T], BF16)
        vS = ph.alloc([128, NT, 512], BF16)
        rdec = ph.alloc([128, 3968], BF16)
        pT = [ph.alloc([128, 512], BF16) for _ in range(2)]
        gng = ph.alloc([128, 2048], F32)
        rq = [ph.alloc([128, 512], F32) for _ in range(2)]
        ct = [ph.alloc([128, 512], F32) for _ in range(2)]
        sn = [ph.alloc([128, 512], F32) for _ in range(2)]
        tA = [ph.alloc([128, 512], F32) for _ in range(2)]
        tB = [ph.alloc([128, 512], F32) for _ in range(2)]
        rb = [ph.alloc([128, 512], BF16) for _ in range(2)]
        ob = [ph.alloc([128, 512], F32) for _ in range(2)]
        gt = [ph.alloc([128, 512], F32) for _ in range(2)]
        bst = [ph.alloc([128, 8], F32) for _ in range(2)]
        DMA('sp', gng, prm['odd_gn_g'][j:j + 1, :].partition_broadcast(128), [], [K('gng')])
        for h in range(4):
            DMA('pool', rdec, cst['retdec'][:, h * 3968:(h + 1) * 3968], [], [K('rdec')])
            for tt in range(NT):
                b = tt % 2
                r0 = slice(tt * 128, (tt + 1) * 128)
                DMA('sp', rq[b][:, 0:256], proj[r0, h * 256:(h + 1) * 256], ['proj%d_%d' % (tt, h // 2)], [K('rq%d' % b)])
                DMA('sp', rq[b][:, 256:512], proj[r0, 1024 + h * 256:1024 + (h + 1) * 256],
                    ['proj%d_%d' % (tt, 2 + h // 2)], [K('rq%d' % b)])
                DMA('sp', ct[b], cst['cosC'][r0, :], [], [K('cs%d' % b)])
                DMA('sp', sn[b], cst['sinC'][r0, :], [], [K('cs%d' % b)])
                rope('pool', rq[b], K('rq%d' % b), ct[b], sn[b], K('cs%d' % b), tA[b], tB[b], K('t%d' % b), rb[b],
                     K('rb%d' % b), 2, 256)
                transpose_tile(rb[b], K('rb%d' % b), 4, lambda c0, n, tt=tt: qkT[:, 0:4, tt * 128:(tt + 1) * 128],
                               K('qkT%d' % tt))
            for t4 in range(0, NT, 4):
                DMA('pool', vS[:, t4:t4 + 4, :],
                    proj[t4 * 128:(t4 + 4) * 128, 2048 + h * 512:2048 + (h + 1) * 512].rearrange("(t p) n -> p t n", p=128),
                    ['proj%d_%d' % (tt, 4 + h) for tt in range(t4, t4 + 4)], [K('vS%d' % tt) for tt in range(t4, t4 + 4)])
            iters = [(cn4, cm) for cn4 in range(4) for cm in range(NT)]

            def r_scores(i):
                cn4, cm = iters[i]
                s = i % 2
                qkeys = [K('qkT%d' % tt) for tt in range(cn4 * 4, cn4 * 4 + 4)]
                for dc in range(2):
                    X('pe', 'matmul', qkeys + [K('qkT%d' % cm)], ['pa%d' % s], out=pa[s][:],
                      lhsT=qkT[:, 2 + dc, cm * 128:(cm + 1) * 128], rhs=qkT[:, dc, cn4 * 512:(cn4 + 1) * 512],
                      start=(dc == 0), stop=(dc == 1))

            def r_rest(i):
                cn4, cm = iters[i]
                s = i % 2
                off = cn4 * 512 - cm * 128 + 1920
                X('dve', 'tensor_tensor', ['pa%d' % s, K('rdec')], [K('pT%d' % s)], out=pT[s], in0=pa[s][:],
                  in1=rdec[:, off:off + 512], op=ALU.mult)
                for qs in range(4):
                    X('pe', 'matmul', [K('pT%d' % s), K('vS%d' % cm)], ['pa%d' % (2 + qs)], out=pa[2 + qs][:],
                      lhsT=pT[s][:, qs * 128:(qs + 1) * 128], rhs=vS[:, cm, :], start=(cm == 0), stop=(cm == NT - 1))

            r_scores(0)
            for i_ in range(len(iters)):
                cn4, cm = iters[i_]
                if i_ + 1 < len(iters):
                    r_scores(i_ + 1)
                r_rest(i_)
                if cm != NT - 1:
                    continue
                for qs in range(4):
                    tt = cn4 * 4 + qs
                    b = qs % 2
                    r0 = slice(tt * 128, (tt + 1) * 128)
                    X('act', 'activation', ['pa%d' % (2 + qs)], [K('ob%d' % b)], out=ob[b], in_=pa[2 + qs][:], func=AF.Copy)
                    DMA('sp', gt[b], proj[r0, 4096 + h * 512:4096 + (h + 1) * 512], ['proj%d_%d' % (tt, 8 + h)], [K('gt%d' % b)])
                    X('act', 'activation', [K('gt%d' % b)], [K('gt%d' % b)], out=gt[b], in_=gt[b], func=AF.Silu)
                    X('act', 'activation', [K('ob%d' % b)], ['junk', K('bst%d' % b)], out=junk[:, 0:512], in_=ob[b],
                      func=AF.Square, accum_out=bst[b][:, 0:1])
                    X('dve', 'tensor_reduce', [K('ob%d' % b)], [K('bst%d' % b)], out=bst[b][:, 1:2], in_=ob[b], axis=AX.X,
                      op=ALU.add)
                    X('dve', 'tensor_scalar', [K('bst%d' % b)], [K('bst%d' % b)], out=bst[b][:, 2:3], in0=bst[b][:, 1:2],
                      scalar1=1.0 / 512, scalar2=None, op0=ALU.mult)
                    X('dve', 'tensor_tensor', [K('bst%d' % b)], [K('bst%d' % b)], out=bst[b][:, 3:4], in0=bst[b][:, 2:3],
                      in1=bst[b][:, 2:3], op=ALU.mult)
                    X('dve', 'scalar_tensor_tensor', [K('bst%d' % b)], [K('bst%d' % b)], out=bst[b][:, 4:5],
                      in0=bst[b][:, 0:1], scalar=1.0 / 512, in1=bst[b][:, 3:4], op0=ALU.mult, op1=ALU.subtract)
                    X('dve', 'tensor_scalar', [K('bst%d' % b)], [K('bst%d' % b)], out=bst[b][:, 4:5], in0=bst[b][:, 4:5],
                      scalar1=1e-5, scalar2=None, op0=ALU.add)
                    X('act', 'activation', [K('bst%d' % b)], [K('bst%d' % b)], out=bst[b][:, 5:6], in_=bst[b][:, 4:5],
                      func=AF.Sqrt)
                    X('dve', 'reciprocal', [K('bst%d' % b)], [K('bst%d' % b)], out=bst[b][:, 6:7], in_=bst[b][:, 5:6])
                    X('dve', 'tensor_scalar', [K('ob%d' % b), K('bst%d' % b)], [K('ob%d' % b)], out=ob[b], in0=ob[b],
                      scalar1=bst[b][:, 2:3], scalar2=bst[b][:, 6:7], op0=ALU.subtract, op1=ALU.mult)
                    X('pool', 'tensor_tensor', [K('ob%d' % b), K('gng')], [K('ob%d' % b)], out=ob[b], in0=ob[b],
                      in1=gng[:, h * 512:(h + 1) * 512], op=ALU.mult)
                    X('pool', 'tensor_tensor', [K('ob%d' % b), K('gt%d' % b)], [K('ob%d' % b)], out=ob[b], in0=ob[b],
                      in1=gt[b], op=ALU.mult)
                    DMA('sp', mixd[r0, h * 512:(h + 1) * 512], ob[b], [K('ob%d' % b)], ['mixd%d' % tt])


    def bcast_row(ph, key, src_row, width, eng='sp'):
        t = ph.alloc([128, width], F32)
        DMA(eng, t, src_row.partition_broadcast(128), [], [ph.key(key)])
        return t

    def zero_mix(c0, c1):
        X('dve', 'memset', [], ['ev0'], ap=ev[0][:], constant=0.0)
        for tt in range(NT):
            DMA('sp', mixd[tt * 128:(tt + 1) * 128, c0:c1], ev[0][:, 0:c1 - c0], ['ev0'], ['mixd%d' % tt])

    def attn_part(i):
        ph = Phase()
        K = ph.key
        qT6 = ph.alloc([128, 6, T], BF16)
        vA = ph.alloc([128, NT, 2, 66], BF16)
        outA = ph.alloc([128, NT, 512], F32)
        pT = [ph.alloc([128, 512], BF16) for _ in range(2)]
        qg = ph.alloc([128, 640], F32)
        aq = [ph.alloc([128, 640], F32) for _ in range(2)]
        sq = [ph.alloc([128, 640], F32) for _ in range(2)]
        ss = [ph.alloc([128, 32], F32) for _ in range(2)]
        ct = [ph.alloc([128, 640], F32) for _ in range(2)]
        sn = [ph.alloc([128, 640], F32) for _ in range(2)]
        tA = [ph.alloc([128, 640], F32) for _ in range(2)]
        tB = [ph.alloc([128, 640], F32) for _ in range(2)]
        rb = [ph.alloc([128, 768], BF16) for _ in range(2)]
        gt = [ph.alloc([128, 512], F32) for _ in range(2)]
        rs = [ph.alloc([128, 4], F32) for _ in range(4)]
        for h in range(8):
            DMA('sp', qg[:, h * 64:(h + 1) * 64], prm['even_q_gain'][i:i + 1, :].partition_broadcast(128), [], [K('qg')])
        for h in range(2):
            DMA('sp', qg[:, 512 + h * 64:512 + (h + 1) * 64], prm['even_k_gain'][i:i + 1, :].partition_broadcast(128), [],
                [K('qg')])
        X('pool', 'memset', [], [K('vA')], ap=vA.rearrange("p a b c -> p (a b c)"), constant=1.0)
        for t4 in range(0, NT, 4):
            for g in range(2):
                DMA('pool', vA[:, t4:t4 + 4, g, 0:64],
                    proj[t4 * 128:(t4 + 4) * 128, 640 + g * 64:640 + (g + 1) * 64].rearrange("(t p) d -> p t d", p=128),
                    ['proj%d_1' % tt for tt in range(t4, t4 + 4)], [K('vA')])
        v3 = lambda a: a.rearrange("p (h d) -> p h d", d=64)
        for tt in range(NT):
            b = tt % 2
            r0 = slice(tt * 128, (tt + 1) * 128)
            DMA('sp', aq[b], proj[r0, 0:640], ['proj%d_0' % tt, 'proj%d_1' % tt], [K('aq%d' % b)])
            DMA('sp', ct[b], cst['cosA'][r0, :], [], [K('cs%d' % b)])
            DMA('sp', sn[b], cst['sinA'][r0, :], [], [K('cs%d' % b)])
            X('pool', 'tensor_tensor', [K('aq%d' % b)], [K('sq%d' % b)], out=sq[b], in0=aq[b], in1=aq[b], op=ALU.mult)
            X('dve', 'tensor_reduce', [K('sq%d' % b)], [K('ss%d' % b)], out=ss[b][:, 0:10], in_=v3(sq[b]), axis=AX.X, op=ALU.add)
            X('dve', 'tensor_scalar', [K('ss%d' % b)], [K('ss%d' % b)], out=ss[b][:, 10:20], in0=ss[b][:, 0:10],
              scalar1=1.0 / 64, scalar2=EPS, op0=ALU.mult, op1=ALU.add)
            X('act', 'activation', [K('ss%d' % b)], [K('ss%d' % b)], out=ss[b][:, 20:30], in_=ss[b][:, 10:20], func=AF.Sqrt)
            X('dve', 'reciprocal', [K('ss%d' % b)], [K('ss%d' % b)], out=ss[b][:, 0:10], in_=ss[b][:, 20:30])
            X('dve', 'tensor_tensor', [K('aq%d' % b), K('ss%d' % b)], [K('aq%d' % b)], out=v3(aq[b]), in0=v3(aq[b]),
              in1=ss[b][:, 0:10].unsqueeze(2).to_broadcast([128, 10, 64]), op=ALU.mult)
            X('pool', 'tensor_tensor', [K('aq%d' % b), K('qg')], [K('aq%d' % b)], out=aq[b], in0=aq[b], in1=qg, op=ALU.mult)
            rope('pool', aq[b], K('aq%d' % b), ct[b], sn[b], K('cs%d' % b), tA[b], tB[b], K('t%d' % b), rb[b][:, 0:640],
                 K('rb%d' % b), 10, 64)
            X('pool', 'tensor_copy', [K('rb%d' % b)], [K('rb%d' % b)], out=rb[b][:, 640:704], in_=rb[b][:, 576:640])
            X('pool', 'tensor_copy', [K('rb%d' % b)], [K('rb%d' % b)], out=rb[b][:, 704:768], in_=rb[b][:, 512:576])
            transpose_tile(rb[b], K('rb%d' % b), 6, lambda c0, n, tt=tt: qT6[:, c0:c0 + n, tt * 128:(tt + 1) * 128],
                           K('qT%d' % tt))
        aiters = [(hq, cn4, cm) for hq in range(8) for cn4 in range(4) for cm in range(NT)]

        def a_par(hq):
            g = hq // 4
            base = (hq % 2) * 64
            pair = hq // 2
            kc = 4 + (1 if g != base // 64 else 0)
            return g, base, pair, kc

        def a_scores(i):
            hq, cn4, cm = aiters[i]
            g, base, pair, kc = a_par(hq)
            s_ = i % 2
            qkeys = [K('qT%d' % tt) for tt in range(cn4 * 4, cn4 * 4 + 4)]
            X('pe', 'matmul', qkeys + [K('qT%d' % cm)], ['pa%d' % s_], out=pa[s_][:],
              lhsT=qT6[base:base + 64, kc, cm * 128:(cm + 1) * 128],
              rhs=qT6[base:base + 64, pair, cn4 * 512:(cn4 + 1) * 512], start=True, stop=True)

        def a_rest(i):
            hq, cn4, cm = aiters[i]
            g, base, pair, kc = a_par(hq)
            s_ = i % 2
            X('act', 'activation', ['pa%d' % s_], [K('pT%d' % s_)], out=pT[s_], in_=pa[s_][:], func=AF.Exp, scale=0.125)
            for qs in range(4):
                X('pe', 'matmul', [K('pT%d' % s_), K('vA')], ['pa%d' % (2 + qs)], out=pa[2 + qs][:, 0:65],
                  lhsT=pT[s_][:, qs * 128:(qs + 1) * 128], rhs=vA[:, cm, g, 0:65], start=(cm == 0), stop=(cm == NT - 1))

        a_scores(0)
        for i_ in range(len(aiters)):
            hq, cn4, cm = aiters[i_]
            if i_ + 1 < len(aiters):
                a_scores(i_ + 1)
            a_rest(i_)
            if cm == NT - 1:
                for qs in range(4):
                    tt = cn4 * 4 + qs
                    X('dve', 'reciprocal', ['pa%d' % (2 + qs)], [K('rs%d' % qs)], out=rs[qs][:, 0:1], in_=pa[2 + qs][:, 64:65])
                    X('dve', 'tensor_scalar', ['pa%d' % (2 + qs), K('rs%d' % qs)], [K('outA%d' % tt)],
                      out=outA[:, tt, hq * 64:(hq + 1) * 64], in0=pa[2 + qs][:, 0:64], scalar1=rs[qs][:, 0:1], scalar2=None,
                      op0=ALU.mult)
        for tt in range(NT):
            b = tt % 2
            r0 = slice(tt * 128, (tt + 1) * 128)
            DMA('sp', gt[b], proj[r0, 768:1280], ['proj%d_1' % tt, 'proj%d_2' % tt], [K('gt%d' % b)])
            X('act', 'activation', [K('gt%d' % b)], [K('gt%d' % b)], out=gt[b], in_=gt[b], func=AF.Silu)
            X('pool', 'tensor_tensor', [K('gt%d' % b), K('outA%d' % tt)], [K('gt%d' % b)], out=gt[b], in0=gt[b],
              in1=outA[:, tt, :], op=ALU.mult)
            DMA('sp', mixd[r0, 0:512], gt[b], [K('gt%d' % b)], ['mixd%d' % tt])

    def rwkv_part(i):
        ph = Phase()
        K = ph.key
        MU = bcast_row(ph, 'MU', prm['even_mu'][i:i + 1, :], 1792)
        s0 = [ph.alloc([128, 1792], F32) for _ in range(2)]
        sm = [ph.alloc([128, 1792], F32) for _ in range(2)]
        sp_ = [ph.alloc([128, 1792], F32) for _ in range(2)]
        pk = lambda tt: ['proj%d_%d' % (tt, cb) for cb in range(2, 6)]
        for tt in range(NT):
            b = tt % 2
            r0 = tt * 128
            DMA('sp', s0[b], proj[r0:r0 + 128, 1280:3072], pk(tt), [K('s0%d' % b)])
            if tt == 0:
                X('pool', 'memset', [], [K('sm%d' % b)], ap=sm[b], constant=0.0)
                DMA('sp', sm[b][1:128, :], proj[0:127, 1280:3072], pk(tt), [K('sm%d' % b)])
            else:
                DMA('sp', sm[b], proj[r0 - 1:r0 + 127, 1280:3072], pk(tt) + pk(tt - 1), [K('sm%d' % b)])
            if tt == NT - 1:
                X('pool', 'memset', [], [K('sp%d' % b)], ap=sp_[b], constant=0.0)
                DMA('sp', sp_[b][0:127, :], proj[r0 + 1:r0 + 128, 1280:3072], pk(tt), [K('sp%d' % b)])
            else:
                DMA('sp', sp_[b], proj[r0 + 1:r0 + 129, 1280:3072], pk(tt) + pk(tt + 1), [K('sp%d' % b)])
            X('pool', 'tensor_tensor', [K('sm%d' % b), K('sp%d' % b)], [K('sm%d' % b)], out=sm[b], in0=sm[b], in1=sp_[b],
              op=ALU.add)
            X('dve', 'scalar_tensor_tensor', [K('sm%d' % b), K('s0%d' % b)], [K('sm%d' % b)], out=sm[b], in0=sm[b],
              scalar=0.5, in1=s0[b], op0=ALU.mult, op1=ALU.subtract)
            X('pool', 'tensor_tensor', [K('sm%d' % b), K('MU')], [K('sm%d' % b)], out=sm[b], in0=sm[b], in1=MU, op=ALU.mult)
            X('dve', 'tensor_tensor', [K('sm%d' % b), K('s0%d' % b)], [K('sm%d' % b)], out=sm[b], in0=sm[b], in1=s0[b],
              op=ALU.add)
            DMA('sp', ubd[r0:r0 + 128, :], sm[b], [K('sm%d' % b)], ['ub%d' % tt])

        ph = Phase()
        K = ph.key
        Yacc = ph.alloc([128, NT, 512], F32)
        KKB = bcast_row(ph, 'KKB', prm['even_k_k'][i:i + 1, :], 512)
        KAB = bcast_row(ph, 'KAB', prm['even_k_a'][i:i + 1, :], 512)
        RKB = bcast_row(ph, 'RKB', prm['even_r_k'][i:i + 1, :], 512)
        LNG = bcast_row(ph, 'LNG', prm['even_lnx_g'][i:i + 1, :], 512)
        LNB = bcast_row(ph, 'LNB', prm['even_lnx_b'][i:i + 1, :], 512)
        W0 = [bcast_row(ph, 'W0%d' % d, prm['even_w0_' + 'fb'[d]][i:i + 1, :], 512) for d in range(2)]
        A0 = [bcast_row(ph, 'A0%d' % d, prm['even_a0_' + 'fb'[d]][i:i + 1, :], 512) for d in range(2)]
        NEGB = ph.alloc([128, 512], F32)
        X('dve', 'memset', [], [K('NEGB')], ap=NEGB, constant=-1.0)
        mark = ph.off
        W2A2 = [ph.alloc([128, 512], BF16) for _ in range(2)]
        for d in range(2):
            DMA('pool', W2A2[d][0:64, :], prm['even_w2_' + 'fb'[d]][i], [], [K('W2A2%d' % d)])
            DMA('pool', W2A2[d][64:128, :], prm['even_a2_' + 'fb'[d]][i], [], [K('W2A2%d' % d)])
        A_ = lambda shape, dt: [ph.alloc(shape, dt) for _ in range(2)]
        u = A_([128, 1792], F32)
        lo = A_([128, 128], BF16)
        loT = A_([128, 128], BF16)
        sgm = A_([128, 512], F32)
        alp = A_([128, 512], F32)
        Epl = A_([128, 512], F32)
        Emi = A_([128, 512], F32)
        Epr = A_([128, 512], F32)
        kk = A_([128, 512], F32)
        kx = A_([128, 512], F32)
        bb = A_([128, 512], F32)
        tm = A_([128, 512], F32)
        ss8 = A_([128, 32], F32)
        Kt = A_([128, 512], BF16)
        Bt = A_([128, 512], BF16)
        Vt = A_([128, 512], BF16)
        ARs = A_([128, 2, 512], BF16)
        ARt = A_([128, 4, 2, 128], BF16)
        KBt = A_([128, 4, 2, 128], BF16)
        PC = A_([128, 4], F32)
        SC = A_([128, 8, 512], BF16)
        TI = A_([128, 8, 128], BF16)
        L0 = [ph.alloc([128, 128], BF16) for _ in range(4)]
        Wk = [[ph.alloc([128, 384], BF16) for _ in range(2)] for _ in range(4)]
        Xb = A_([128, 4, 128], BF16)
        Ub = A_([128, 4, 128], BF16)
        H32 = A_([128, 4, 128], F32)
        Hbf = A_([128, 4, 128], BF16)
        t32 = A_([128, 128], F32)
        mk4 = A_([128, 512], BF16)
        for d in range(2):
            for rep in range(2):
                X('pool', 'tensor_copy', ['msk'], [K('mk4%d' % d)], out=mk4[d][:, rep * 256:(rep + 1) * 256],
                  in_=msk[:, 2 * d:2 * d + 2, :].rearrange("p a b -> p (a b)"))
            X('pool', 'memset', [], [K('H32%d' % d)], ap=H32[d].rearrange("p a b -> p (a b)"), constant=0.0)
            X('pool', 'memset', [], [K('Hbf%d' % d)], ap=Hbf[d].rearrange("p a b -> p (a b)"), constant=0.0)
        X('pool', 'memset', [], [K('Yacc%d' % tt) for tt in range(NT)], ap=Yacc.rearrange("p a b -> p (a b)"), constant=0.0)
        v8 = lambda a: a.rearrange("p (h d) -> p h d", d=64)

        PS = float(os.environ.get('PREP_STAGE', '9'))

        def prep(d, tt):
            kd = lambda n: K('%s%d' % (n, d))
            r0 = tt * 128
            DMA('sp', u[d], ubd[r0:r0 + 128, :], ['ub%d' % tt], [kd('u')])
            r_ = u[d][:, 0:512]
            kb_ = u[d][:, 512:1024]
            vb_ = u[d][:, 1024:1536]
            X('act', 'activation', [kd('u')], [kd('lo')], out=lo[d][:, 0:64], in_=u[d][:, 1536 + d * 64:1600 + d * 64],
              func=AF.Tanh)
            X('dve', 'tensor_copy', [kd('u')], [kd('lo')], out=lo[d][:, 64:128], in_=u[d][:, 1664 + d * 64:1728 + d * 64])
            pb = next_ptr()
            X('pe', 'transpose', [kd('lo'), 'idb'], ['ptr%d' % pb], out=ptr[pb][:, 0:128], in_=lo[d], identity=idb[:])
            X('act', 'activation', ['ptr%d' % pb], [kd('loT')], out=loT[d], in_=ptr[pb][:, 0:128], func=AF.Copy)
            X('pe', 'matmul', [kd('loT'), K('W2A2%d' % d)], ['pa0'], out=pa[0][:], lhsT=loT[d][0:64, :],
              rhs=W2A2[d][0:64, :], start=True, stop=True)
            X('pe', 'matmul', [kd('loT'), K('W2A2%d' % d)], ['pa1'], out=pa[1][:], lhsT=loT[d][64:128, :],
              rhs=W2A2[d][64:128, :], start=True, stop=True)
            X('dve', 'tensor_tensor', ['pa0', K('W0%d' % d)], [kd('sgm')], out=sgm[d], in0=pa[0][:], in1=W0[d], op=ALU.add)
            X('act', 'activation', [kd('sgm')], [kd('sgm')], out=sgm[d], in_=sgm[d], func=AF.Sigmoid)
            X('dve', 'tensor_tensor', ['pa1', K('A0%d' % d)], [kd('alp')], out=alp[d], in0=pa[1][:], in1=A0[d], op=ALU.add)
            X('act', 'activation', [kd('alp')], [kd('alp')], out=alp[d], in_=alp[d], func=AF.Sigmoid)
            if PS < 2:
                return
            X('pe', 'matmul', [kd('sgm'), 'tri'], ['pa0'], out=pa[0][:], lhsT=tri[:, 2 * d + 1, :], rhs=sgm[d], start=True,
              stop=True)
            X('pe', 'matmul', [kd('sgm'), 'tri'], ['pa1'], out=pa[1][:], lhsT=tri[:, 2 * d, :], rhs=sgm[d], start=True,
              stop=True)
            X('act', 'activation', ['pa0'], [kd('Epl')], out=Epl[d], in_=pa[0][:], func=AF.Exp)
            X('act', 'activation', ['pa0'], [kd('Emi')], out=Emi[d], in_=pa[0][:], func=AF.Exp, scale=-1.0)
            X('act', 'activation', ['pa1'], [kd('Epr')], out=Epr[d], in_=pa[1][:], func=AF.Exp)
            if PS < 3:
                return
            for p_ in range(4):
                X('pe', 'matmul', [kd('sgm'), 'negcol'], ['pa0'], out=pa[0][:, p_:p_ + 1],
                  lhsT=sgm[d][:, p_ * 128:(p_ + 1) * 128], rhs=negcol[:, 0:1], start=True, stop=True)
            X('act', 'activation', ['pa0'], [kd('PC')], out=PC[d], in_=pa[0][:, 0:4], func=AF.Exp)
            if PS < 4:
                return
            X('pool', 'tensor_tensor', [kd('u'), K('KKB')], [kd('kk')], out=kk[d], in0=kb_, in1=KKB, op=ALU.mult)
            X('pool', 'tensor_tensor', [kd('kk')], [kd('tm')], out=tm[d], in0=kk[d], in1=kk[d], op=ALU.mult)
            if PS < 4.2:
                return
            X('dve', 'tensor_reduce', [kd('tm')], [kd('ss8')], out=ss8[d][:, 0:8], in_=v8(tm[d]), axis=AX.X, op=ALU.add)
            X('act', 'activation', [kd('ss8')], [kd('ss8')], out=ss8[d][:, 8:16], in_=ss8[d][:, 0:8], func=AF.Sqrt)
            X('dve', 'tensor_scalar', [kd('ss8')], [kd('ss8')], out=ss8[d][:, 8:16], in0=ss8[d][:, 8:16], scalar1=1e-12,
              scalar2=None, op0=ALU.max)
            X('dve', 'reciprocal', [kd('ss8')], [kd('ss8')], out=ss8[d][:, 16:24], in_=ss8[d][:, 8:16])
            if PS < 4.4:
                return
            X('pool', 'tensor_tensor', [kd('kk'), kd('ss8')], [kd('kk')], out=v8(kk[d]), in0=v8(kk[d]),
              in1=ss8[d][:, 16:24].unsqueeze(2).to_broadcast([128, 8, 64]), op=ALU.mult)
            if PS < 4.6:
                return
            X('dve', 'tensor_tensor', [kd('alp'), K('KAB')], [kd('kx')], out=kx[d], in0=alp[d], in1=KAB, op=ALU.mult)
            X('dve', 'tensor_tensor', [kd('kx'), K('KAB')], [kd('kx')], out=kx[d], in0=kx[d], in1=KAB, op=ALU.subtract)
            X('dve', 'tensor_tensor', [kd('kx'), kd('u')], [kd('kx')], out=kx[d], in0=kx[d], in1=kb_, op=ALU.mult)
            X('dve', 'tensor_tensor', [kd('kx'), kd('u')], [kd('kx')], out=kx[d], in0=kx[d], in1=kb_, op=ALU.add)
            X('pool', 'tensor_tensor', [kd('kk'), kd('alp')], [kd('bb')], out=bb[d], in0=kk[d], in1=alp[d], op=ALU.mult)
            if PS < 5:
                return
            v4 = lambda a: a.rearrange("p (a c) -> p a c", a=4)
            X('dve', 'tensor_tensor', [kd('kk'), kd('Epr')], [kd('tm')], out=tm[d], in0=kk[d], in1=Epr[d], op=ALU.mult)
            X('dve', 'tensor_tensor', [kd('tm'), K('NEGB')], [kd('ARs')], out=ARs[d][:, 0, :], in0=tm[d], in1=NEGB,
              op=ALU.mult)
            X('dve', 'tensor_tensor', [kd('u'), kd('Epl')], [kd('ARs')], out=ARs[d][:, 1, :], in0=r_, in1=Epl[d],
              op=ALU.mult)
            X('dve', 'tensor_tensor', [kd('kx'), kd('Emi')], [kd('Kt')], out=Kt[d], in0=kx[d], in1=Emi[d], op=ALU.mult)
            X('dve', 'tensor_tensor', [kd('bb'), kd('Emi')], [kd('Bt')], out=Bt[d], in0=bb[d], in1=Emi[d], op=ALU.mult)
            X('dve', 'tensor_copy', [kd('u')], [kd('Vt')], out=Vt[d], in_=vb_)
            if dbg and os.environ.get('DBG_LIST'):
                lst = [tuple(int(v) for v in it.split(':')) for it in os.environ['DBG_LIST'].split(',')]
                if (d, tt) in lst:
                    qi = lst.index((d, tt))
                    DMA('sp', dbgo[:, qi * 512:(qi + 1) * 512], Emi[d], [kd('Emi')], ['dbgo%d' % qi])
            if PS < 6:
                return
            for p_ in range(4):
                pb = next_ptr()
                X('pe', 'transpose', [kd('ARs'), 'idb'], ['ptr%d' % pb], out=ptr[pb][:, 0:128], in_=ARs[d][:, 0, p_ * 128:(p_ + 1) * 128],
                  identity=idb[:])
                X('pe', 'transpose', [kd('ARs'), 'idb'], ['ptr%d' % pb], out=ptr[pb][:, 128:256], in_=ARs[d][:, 1, p_ * 128:(p_ + 1) * 128],
                  identity=idb[:])
                X('act', 'activation', ['ptr%d' % pb], [kd('ARt')], out=ARt[d][:, p_, :, :].rearrange("p x c -> p (x c)"),
                  in_=ptr[pb][:, 0:256], func=AF.Copy)
                pb = next_ptr()
                X('pe', 'transpose', [kd('Kt'), 'idb'], ['ptr%d' % pb], out=ptr[pb][:, 0:128],
                  in_=Kt[d][:, p_ * 128:(p_ + 1) * 128], identity=idb[:])
                X('pe', 'transpose', [kd('Bt'), 'idb'], ['ptr%d' % pb], out=ptr[pb][:, 128:256],
                  in_=Bt[d][:, p_ * 128:(p_ + 1) * 128], identity=idb[:])
                X('act', 'activation', ['ptr%d' % pb], [kd('KTt'), kd('BTt')],
                  out=KBt[d][:, p_, :, :].rearrange("p x c -> p (x c)"), in_=ptr[pb][:, 0:256], func=AF.Copy)

        def head_gen(d, h, slot):
            kd = lambda n: K('%s%d' % (n, d))
            par = h % 2
            base = par * 64
            p_ = h // 2
            z = 2 + slot
            ev_eng = 'act' if slot < 2 else 'dve'

            def evac(reads, writes, out, in_):
                if ev_eng == 'act':
                    X('act', 'activation', reads, writes, out=out, in_=in_, func=AF.Copy)
                else:
                    X('dve', 'tensor_copy', reads, writes, out=out, in_=in_)
            arf = ARt[d][base:base + 64, p_, :, :].rearrange("p x c -> p (x c)")
            aT = ARt[d][base:base + 64, p_, 0, :]
            X('pe', 'matmul', [kd('BTt'), kd('ARt')], ['pa0'], out=pa[0][:, 0:256], lhsT=KBt[d][base:base + 64, p_, 1, :],
              rhs=arf, start=True, stop=True)
            X('pe', 'matmul', [kd('KTt'), kd('ARt')], ['pa0'], out=pa[0][:, 256:512], lhsT=KBt[d][base:base + 64, p_, 0, :],
              rhs=arf, start=True, stop=True)
            X('pe', 'matmul', [kd('BTt'), kd('ARt')], ['pa1'], out=pa[1][:, 0:128], lhsT=aT,
              rhs=KBt[d][base:base + 64, p_, 1, :], start=True, stop=True)
            sck = kd('SC%d_' % h)
            X('dve', 'tensor_tensor', ['pa0', K('mk4%d' % d)], [sck], out=SC[d][:, h, :], in0=pa[0][:], in1=mk4[d],
              op=ALU.mult)
            l0k = K('L0%d' % slot)
            X('dve', 'tensor_tensor', ['pa1', 'msk'], [l0k], out=L0[slot], in0=pa[1][:, 0:128], in1=msk[:, 2 - 2 * d, :],
              op=ALU.mult)
            yield
            N0 = SC[d][:, h, 0:128]
            wk = [K('Wk%d_%d' % (slot, q)) for q in range(2)]
            W = Wk[slot]
            X('pe', 'matmul', [sck, 'idb'], ['pa%d' % z], out=pa[z][:, 0:128], lhsT=idb[:], rhs=N0, start=True, stop=False)
            X('pe', 'matmul', [sck, 'idb'], ['pa%d' % z], out=pa[z][:, 0:128], lhsT=idb[:], rhs=idb[:], start=False, stop=True)
            X('pe', 'matmul', [l0k, sck], ['pa%d' % z], out=pa[z][:, 128:256], lhsT=L0[slot], rhs=N0, start=True, stop=True)
            X('pe', 'matmul', [l0k, sck], ['pa%d' % z], out=pa[z][:, 256:384], lhsT=N0, rhs=L0[slot], start=True, stop=True)
            evac(['pa%d' % z], [wk[0]], W[0][:, 0:384], pa[z][:, 0:384])
            yield
            cur = 0
            for k_ in range(1, 6):
                a_, b_ = cur, 1 - cur
                rk = [wk[a_], 'idb']
                X('pe', 'matmul', rk, ['pa%d' % z], out=pa[z][:, 0:128], lhsT=idb[:], rhs=W[a_][:, 0:128], start=True, stop=False)
                X('pe', 'matmul', rk, ['pa%d' % z], out=pa[z][:, 0:128], lhsT=W[a_][:, 256:384], rhs=W[a_][:, 0:128],
                  start=False, stop=True)
                X('pe', 'matmul', rk, ['pa%d' % z], out=pa[z][:, 128:256], lhsT=W[a_][:, 256:384], rhs=W[a_][:, 128:256],
                  start=True, stop=True)
                X('pe', 'matmul', rk, ['pa%d' % z], out=pa[z][:, 256:384], lhsT=W[a_][:, 128:256], rhs=W[a_][:, 256:384],
                  start=True, stop=True)
                evac(['pa%d' % z], [wk[b_]], W[b_][:, 0:384], pa[z][:, 0:384])
                cur = b_
                yield
            rk = [wk[cur], 'idb']
            X('pe', 'matmul', rk, ['pa%d' % z], out=pa[z][:, 0:128], lhsT=idb[:], rhs=W[cur][:, 0:128], start=True, stop=False)
            X('pe', 'matmul', rk, ['pa%d' % z], out=pa[z][:, 0:128], lhsT=W[cur][:, 256:384], rhs=W[cur][:, 0:128],
              start=False, stop=True)
            evac(['pa%d' % z], [kd('TI%d_' % h)], TI[d][:, h, :], pa[z][:, 0:128])
            yield

        def run_heads():
            combos = [(d, h) for h in range(8) for d in range(2)]
            for g0 in range(0, 16, 4):
                alive = [head_gen(d, h, slot) for slot, (d, h) in enumerate(combos[g0:g0 + 4])]
                while alive:
                    nxt = []
                    for g in alive:
                        try:
                            next(g)
                            nxt.append(g)
                        except StopIteration:
                            pass
                    alive = nxt

        def chain_gen(d, tt, bx, bu):
            kd = lambda n: K('%s%d' % (n, d))
            for p_ in range(4):
                hk = kd('H%d_' % p_)
                for hh in range(2):
                    h = 2 * p_ + hh
                    base = hh * 64
                    cs = slice(hh * 64, hh * 64 + 64)
                    X('pe', 'matmul', [kd('SC%d_' % h), kd('Vt')], ['pa%d' % bx], out=pa[bx][:, cs], lhsT=SC[d][:, h, 256:384],
                      rhs=Vt[d][:, h * 64:(h + 1) * 64], start=True, stop=False)
                    X('pe', 'matmul', [kd('ARt'), hk + 'b'], ['pa%d' % bx], out=pa[bx][:, cs], lhsT=ARt[d][base:base + 64, p_, 0, :],
                      rhs=Hbf[d][base:base + 64, p_, base:base + 64], start=False, stop=True)
                X('act', 'activation', ['pa%d' % bx], [kd('Xb%d_' % p_)], out=Xb[d][:, p_, :], in_=pa[bx][:, 0:128], func=AF.Copy)
                yield
                for hh in range(2):
                    h = 2 * p_ + hh
                    cs = slice(hh * 64, hh * 64 + 64)
                    X('pe', 'matmul', [kd('TI%d_' % h), kd('Xb%d_' % p_)], ['pa%d' % bu], out=pa[bu][:, cs], lhsT=TI[d][:, h, :],
                      rhs=Xb[d][:, p_, cs], start=True, stop=True)
                X('dve', 'tensor_copy', ['pa%d' % bu], [kd('Ub%d_' % p_)], out=Ub[d][:, p_, :], in_=pa[bu][:, 0:128])
                yield
                for hh in range(2):
                    h = 2 * p_ + hh
                    base = hh * 64
                    cs = slice(hh * 64, hh * 64 + 64)
                    ys = slice(128 + hh * 64, 128 + hh * 64 + 64)
                    X('pe', 'matmul', [kd('ARt'), hk + 'b'], ['pa%d' % bx], out=pa[bx][:, ys], lhsT=ARt[d][base:base + 64, p_, 1, :],
                      rhs=Hbf[d][base:base + 64, p_, base:base + 64], start=True, stop=False)
                    X('pe', 'matmul', [kd('SC%d_' % h), kd('Vt')], ['pa%d' % bx], out=pa[bx][:, ys], lhsT=SC[d][:, h, 384:512],
                      rhs=Vt[d][:, h * 64:(h + 1) * 64], start=False, stop=False)
                    X('pe', 'matmul', [kd('SC%d_' % h), kd('Ub%d_' % p_)], ['pa%d' % bx], out=pa[bx][:, ys], lhsT=SC[d][:, h, 128:256],
                      rhs=Ub[d][:, p_, cs], start=False, stop=True)
                X('dve', 'tensor_tensor', ['pa%d' % bx, K('Yacc%d' % tt)], [K('Yacc%d' % tt)], out=Yacc[:, tt, p_ * 128:(p_ + 1) * 128],
                  in0=pa[bx][:, 128:256], in1=Yacc[:, tt, p_ * 128:(p_ + 1) * 128], op=ALU.add)
                ps_ = slice(p_ * 128, (p_ + 1) * 128)
                X('pe', 'matmul', [kd('Kt'), kd('Vt')], ['pa%d' % bu], out=pa[bu][:, 128:256], lhsT=Kt[d][:, ps_], rhs=Vt[d][:, ps_],
                  start=True, stop=False)
                X('pe', 'matmul', [kd('Bt'), kd('Ub%d_' % p_)], ['pa%d' % bu], out=pa[bu][:, 128:256], lhsT=Bt[d][:, ps_],
                  rhs=Ub[d][:, p_, :], start=False, stop=True)
                X('dve', 'tensor_tensor', ['pa%d' % bu, hk + 'f'], [kd('t32')], out=t32[d], in0=pa[bu][:, 128:256], in1=H32[d][:, p_, :],
                  op=ALU.add)
                X('dve', 'tensor_scalar', [kd('t32'), kd('PC')], [hk + 'f'], out=H32[d][:, p_, :], in0=t32[d],
                  scalar1=PC[d][:, p_:p_ + 1], scalar2=None, op0=ALU.mult)
                X('act', 'activation', [kd('t32'), kd('PC')], [hk + 'b'], out=Hbf[d][:, p_, :], in_=t32[d], func=AF.Copy,
                  scale=PC[d][:, p_:p_ + 1])
                yield

        def run_chains(c):
            alive = [chain_gen(0, c, 2, 3), chain_gen(1, NT - 1 - c, 4, 5)]
            while alive:
                nxt = []
                for g in alive:
                    try:
                        next(g)
                        nxt.append(g)
                    except StopIteration:
                        pass
                alive = nxt

        RW = int(os.environ.get('RW_STAGE', '9'))
        for c in range(NT):
            if RW >= 2:
                prep(0, c)
                prep(1, NT - 1 - c)
            if RW >= 3:
                run_heads()
            if RW >= 4:
                run_chains(c)

        P.barrier()
        ph.off = mark
        uc = A_([128, 1536], F32)
        gtb = A_([128, 512], F32)
        yv = A_([128, 512], F32)
        sqv = A_([128, 512], F32)
        st8 = A_([128, 64], F32)
        for tt in range(NT):
            b = tt % 2
            r0 = tt * 128
            kb2 = lambda n: K('%s%d' % (n, b))
            DMA('sp', uc[b], ubd[r0:r0 + 128, 0:1536], ['ub%d' % tt], [kb2('uc')])
            DMA('sp', gtb[b], proj[r0:r0 + 128, 3072:3584], ['proj%d_6' % tt], [kb2('gtb')])
            X('act', 'activation', [kb2('gtb')], [kb2('gtb')], out=gtb[b], in_=gtb[b], func=AF.Silu)
            yt = Yacc[:, tt, :]
            S = st8[b]
            X('dve', 'tensor_reduce', [K('Yacc%d' % tt)], [kb2('st8')], out=S[:, 0:8], in_=v8(yt), axis=AX.X, op=ALU.add)
            X('pool', 'tensor_tensor', [K('Yacc%d' % tt)], [kb2('sqv')], out=sqv[b], in0=yt, in1=yt, op=ALU.mult)
            X('dve', 'tensor_reduce', [kb2('sqv')], [kb2('st8')], out=S[:, 8:16], in_=v8(sqv[b]), axis=AX.X, op=ALU.add)
            X('dve', 'tensor_scalar', [kb2('st8')], [kb2('st8')], out=S[:, 16:24], in0=S[:, 0:8], scalar1=1.0 / 64,
              scalar2=None, op0=ALU.mult)
            X('dve', 'tensor_tensor', [kb2('st8')], [kb2('st8')], out=S[:, 24:32], in0=S[:, 16:24], in1=S[:, 16:24],
              op=ALU.mult)
            X('dve', 'scalar_tensor_tensor', [kb2('st8')], [kb2('st8')], out=S[:, 32:40], in0=S[:, 8:16], scalar=1.0 / 64,
              in1=S[:, 24:32], op0=ALU.mult, op1=ALU.subtract)
            X('dve', 'tensor_scalar', [kb2('st8')], [kb2('st8')], out=S[:, 32:40], in0=S[:, 32:40], scalar1=64e-5,
              scalar2=None, op0=ALU.add)
            X('act', 'activation', [kb2('st8')], [kb2('st8')], out=S[:, 40:48], in_=S[:, 32:40], func=AF.Sqrt)
            X('dve', 'reciprocal', [kb2('st8')], [kb2('st8')], out=S[:, 48:56], in_=S[:, 40:48])
            X('dve', 'tensor_tensor', [K('Yacc%d' % tt), kb2('st8')], [kb2('yv')], out=v8(yv[b]), in0=v8(yt),
              in1=S[:, 16:24].unsqueeze(2).to_broadcast([128, 8, 64]), op=ALU.subtract)
            X('pool', 'tensor_tensor', [kb2('yv'), kb2('st8')], [kb2('yv')], out=v8(yv[b]), in0=v8(yv[b]),
              in1=S[:, 48:56].unsqueeze(2).to_broadcast([128, 8, 64]), op=ALU.mult)
            X('pool', 'tensor_tensor', [kb2('yv'), K('LNG')], [kb2('yv')], out=yv[b], in0=yv[b], in1=LNG, op=ALU.mult)
            X('pool', 'tensor_tensor', [kb2('yv'), K('LNB')], [kb2('yv')], out=yv[b], in0=yv[b], in1=LNB, op=ALU.add)
            X('dve', 'tensor_tensor', [kb2('uc')], [kb2('sqv')], out=sqv[b], in0=uc[b][:, 0:512], in1=uc[b][:, 512:1024],
              op=ALU.mult)
            X('pool', 'tensor_tensor', [kb2('sqv'), K('RKB')], [kb2('sqv')], out=sqv[b], in0=sqv[b], in1=RKB, op=ALU.mult)
            X('dve', 'tensor_reduce', [kb2('sqv')], [kb2('st8')], out=S[:, 56:64], in_=v8(sqv[b]), axis=AX.X, op=ALU.add)
            X('dve', 'tensor_tensor', [kb2('uc'), kb2('st8')], [kb2('sqv')], out=v8(sqv[b]), in0=v8(uc[b][:, 1024:1536]),
              in1=S[:, 56:64].unsqueeze(2).to_broadcast([128, 8, 64]), op=ALU.mult)
            X('pool', 'tensor_tensor', [kb2('yv'), kb2('sqv')], [kb2('yv')], out=yv[b], in0=yv[b], in1=sqv[b], op=ALU.add)
            X('pool', 'tensor_tensor', [kb2('yv'), kb2('gtb')], [kb2('yv')], out=yv[b], in0=yv[b], in1=gtb[b], op=ALU.mult)
            DMA('sp', mixd[r0:r0 + 128, 512:1024], yv[b], [kb2('yv')], ['mixd%d' % tt])

    def even_mixer(layer):
        i = layer // 2
        if 'a' in parts:
            attn_part(i)
        else:
            zero_mix(0, 512)
        if 'b' in parts:
            rwkv_part(i)
        else:
            zero_mix(512, 1024)

    EVEN_HOOK = globals().get('_even_mixer_builder')
    for s in range(nseq):
        xs = x[s * T:(s + 1) * T, :]
        ys = y[s * T:(s + 1) * T, :]
        first = True
        for layer in layers:
            hsrc = xs if first else ys
            j = layer // 2
            if layer % 2 == 1:
                prenorm_inproj(hsrc, layer, prm['odd_w_in'][j], ODD_IN)
                odd_mixer(layer)
                out_proj(prm['odd_w_out'][j], 2048, hsrc, ys, layer)
            else:
                prenorm_inproj(hsrc, layer, prm['even_w_in'][j], EVEN_IN)
                even_mixer(layer)
                out_proj(prm['even_w_out'][j], 1024, hsrc, ys, layer)
            first = False
    if dbg:
        print('instr counts', P.n_instr())
    return P.build()


def make_in_maps(inputs, ncores=8, nseq=4):
    consts = host_consts()
    x = np.ascontiguousarray(inputs['x'], dtype=np.float32)
    maps = []
    for c in range(ncores):
        m = {'x': x[c * nseq:(c + 1) * nseq].reshape(nseq * T, D)}
        for k, shp in PARAM_SHAPES.items():
            m[k] = np.ascontiguousarray(inputs[k], dtype=np.float32).reshape(shp)
        m.update({'c_' + k: v for k, v in consts.items()})
        maps.append(m)
    return maps


def kernel(**inputs):
    nc = build_program()
    maps = make_in_maps(inputs)
    res = run_bass_kernel_spmd(nc, maps, core_ids=list(range(8)))
    out = np.concatenate([np.asarray(r['y']).reshape(4, T, D) for r in res.results], axis=0)
    return out.astype(np.float32)
```

```python
import os
import numpy as np
from contextlib import ExitStack
import concourse.bass as bass
import concourse.mybir as mybir
from concourse.bass_utils import run_bass_kernel_spmd

F32 = mybir.dt.float32
BF16 = mybir.dt.bfloat16
AF = mybir.ActivationFunctionType
ALU = mybir.AluOpType
AX = mybir.AxisListType

ENGS = ['pe', 'dve', 'act', 'pool', 'sp']
ENG_ATTR = {'pe': 'tensor', 'dve': 'vector', 'act': 'scalar', 'pool': 'gpsimd', 'sp': 'sync'}
N_DMA_SEM = 12

T = 2048
D = 1024
NT = T // 128
EVEN_IN = 3584
ODD_IN = 6144
EPS = 1e-6
WDEC = float(np.exp(-0.5))


class Prog:
    def __init__(self, nc):
        self.nc = nc
        self.st = ExitStack()
        self.ops = {e: [] for e in ENGS}
        self.cnt = {e: 0 for e in ENGS}
        self.seen = {e: {} for e in ENGS}
        self.last_w = {}
        self.readers = {}
        self.dma_rr = {e: 0 for e in ENGS}
        self.dma_use = {}
        self.semnames = set()

    def sb(self, name, shape, dt):
        return self.st.enter_context(self.nc.sbuf_tensor(name, list(shape), dt))

    def ps(self, name, shape, dt):
        return self.st.enter_context(self.nc.psum_tensor(name, list(shape), dt))

    def dram(self, name, shape, dt, kind=None):
        if kind is None:
            return self.nc.dram_tensor(name, list(shape), dt).ap()
        return self.nc.dram_tensor(name, list(shape), dt, kind=kind).ap()

    def _deps(self, eng, reads, writes):
        deps = []
        for r in reads:
            w = self.last_w.get(r)
            if w is not None:
                deps.append(w)
        for w_ in writes:
            w = self.last_w.get(w_)
            if w is not None:
                deps.append(w)
            deps.extend(self.readers.get(w_, ()))
        waits = []
        seen = self.seen[eng]
        best = {}
        for s, v in deps:
            if seen.get(s, 0) >= v:
                continue
            if best.get(s, 0) < v:
                best[s] = v
        for s, v in best.items():
            seen[s] = v
            waits.append((s, v))
        return waits

    def _update(self, me, reads, writes):
        for r in reads:
            self.readers.setdefault(r, []).append(me)
        for w in writes:
            self.last_w[w] = me
            self.readers[w] = []

    def op(self, eng, fn, reads=(), writes=()):
        waits = self._deps(eng, reads, writes)
        self.cnt[eng] += 1
        s = 'c_' + eng
        self.semnames.add(s)
        me = (s, self.cnt[eng])
        if eng == 'pe':
            self.seen[eng][s] = self.cnt[eng]
        self.ops[eng].append((waits, fn, s, 1))
        self._update(me, reads, writes)

    def dma(self, eng, fn, reads=(), writes=()):
        i = self.dma_rr[eng]
        self.dma_rr[eng] = (i + 1) % N_DMA_SEM
        s = 'd_%s_%d' % (eng, i)
        self.semnames.add(s)
        u = self.dma_use.get(s, 0)
        waits = self._deps(eng, reads, writes)
        if u > 0 and self.seen[eng].get(s, 0) < 16 * u:
            waits.append((s, 16 * u))
            self.seen[eng][s] = 16 * u
        self.dma_use[s] = u + 1
        me = (s, 16 * (u + 1))
        self.ops[eng].append((waits, fn, s, 16))
        self._update(me, reads, writes)

    def barrier(self):
        for e in ENGS:
            waits = []
            seen = self.seen[e]
            for s, u in self.dma_use.items():
                if seen.get(s, 0) < 16 * u:
                    waits.append((s, 16 * u))
                    seen[s] = 16 * u
            for o in ENGS:
                if self.cnt[o] > 0 and seen.get('c_' + o, 0) < self.cnt[o]:
                    waits.append(('c_' + o, self.cnt[o]))
                    seen['c_' + o] = self.cnt[o]
            if waits:
                self.ops[e].append((waits, None, None, 0))

    def finish(self):
        eng = 'sp'
        waits = []
        for s, u in self.dma_use.items():
            if self.seen[eng].get(s, 0) < 16 * u:
                waits.append((s, 16 * u))
        for e in ENGS:
            if self.cnt[e] > 0 and e != eng:
                waits.append(('c_' + e, self.cnt[e]))
        self.ops[eng].append((waits, None, None, 0))

    def build(self):
        nc = self.nc
        self.finish()
        sems = {}
        for name in sorted(self.semnames):
            sems[name] = self.st.enter_context(nc.semaphore(name))
        block = self.st.enter_context(nc.Block())
        for eng in ENGS:
            if not self.ops[eng]:
                continue
            deco = getattr(block, ENG_ATTR[eng])

            def body(e, eng=eng):
                for waits, fn, s, inc in self.ops[eng]:
                    for ws, wv in waits:
                        e.wait_ge(sems[ws], wv)
                    if fn is not None:
                        ins = fn(e)
                        ins.then_inc(sems[s], inc)
            deco(body)
        self.st.close()
        return nc

    def n_instr(self):
        return {e: len(self.ops[e]) for e in ENGS}


def _rope_tables(dim):
    half = dim // 2
    q = dim // 4
    t = np.arange(T)
    row = (t // 64).astype(np.float32)
    col = (t % 64).astype(np.float32)
    inv = (10000.0 ** (-np.arange(0, half, 2, dtype=np.float32) / half)).astype(np.float32)
    ar = row[:, None] * inv
    ac = col[:, None] * inv
    cos = np.concatenate([np.cos(ar), np.cos(ar), np.cos(ac), np.cos(ac)], axis=1)
    sin = np.concatenate([-np.sin(ar), np.sin(ar), -np.sin(ac), np.sin(ac)], axis=1)
    return cos.astype(np.float32), sin.astype(np.float32)


def host_consts():
    c = {}
    c['ident'] = np.eye(128, dtype=np.float32)
    i = np.arange(128)
    su = (i[:, None] < i[None, :]).astype(np.float32)
    iu = (i[:, None] <= i[None, :]).astype(np.float32)
    sl = (i[:, None] > i[None, :]).astype(np.float32)
    il = (i[:, None] >= i[None, :]).astype(np.float32)
    m = np.stack([su, iu, sl, il], axis=1)
    c['masks'] = m.reshape(128, 512).astype(np.float32)
    c['tri'] = (-WDEC * m).reshape(128, 512).astype(np.float32)
    cos64, sin64 = _rope_tables(64)
    c['cosA'] = np.tile(cos64, (1, 10)).astype(np.float32)
    c['sinA'] = np.tile(sin64, (1, 10)).astype(np.float32)
    cos256, sin256 = _rope_tables(256)
    c['cosC'] = np.concatenate([cos256, cos256 / 16.0], axis=1).astype(np.float32)
    c['sinC'] = np.concatenate([sin256, sin256 / 16.0], axis=1).astype(np.float32)
    lgf = np.log(1.0 - 2.0 ** (-5.0 - np.arange(4, dtype=np.float64)))
    lgb = lgf[::-1]
    cc = np.arange(3968)[None, :] - 1920 - np.arange(128)[:, None]
    td = np.zeros((4, 128, 3968), np.float32)
    for h in range(4):
        td[h] = np.where(cc >= 0, np.exp(cc * lgf[h]), np.exp(-cc * lgb[h])).astype(np.float32)
    c['retdec'] = td.transpose(1, 0, 2).reshape(128, 4 * 3968).copy()
    c['negcol'] = np.full((128, 1), -WDEC, np.float32)
    return c


CONST_SHAPES = {'ident': [128, 128], 'masks': [128, 512], 'tri': [128, 512], 'cosA': [T, 640], 'sinA': [T, 640],
                'cosC': [T, 512], 'sinC': [T, 512], 'retdec': [128, 4 * 3968], 'negcol': [128, 1]}

PARAM_SHAPES = {
    'pre_gain': [4, 1024], 'post_gain': [4, 1024], 'even_w_in': [2, 1024, 3584], 'even_mu': [2, 1792],
    'even_q_gain': [2, 64], 'even_k_gain': [2, 64], 'even_k_k': [2, 512], 'even_k_a': [2, 512],
    'even_r_k': [2, 512], 'even_w0_f': [2, 512], 'even_w2_f': [2, 64, 512], 'even_a0_f': [2, 512],
    'even_a2_f': [2, 64, 512], 'even_w0_b': [2, 512], 'even_w2_b': [2, 64, 512], 'even_a0_b': [2, 512],
    'even_a2_b': [2, 64, 512], 'even_lnx_g': [2, 512], 'even_lnx_b': [2, 512], 'even_w_out': [2, 1024, 1024],
    'odd_w_in': [2, 1024, 6144], 'odd_gn_g': [2, 2048], 'odd_w_out': [2, 2048, 1024],
}


ARENA = 94 * 1024


def build_program(nseq=4, layers=(0, 1, 2, 3), dbg=False, parts=('a', 'b')):
    nc = bass.Bass("TRN2", target_bir_lowering=False)
    P = Prog(nc)

    def X(eng, meth, reads, writes, **kw):
        P.op(eng, lambda e: getattr(e, meth)(**kw), reads, writes)

    def DMA(eng, out, in_, reads, writes):
        P.dma(eng, lambda e: e.dma_start(out=out, in_=in_), reads, writes)

    x = P.dram('x', [nseq * T, D], F32, 'ExternalInput')
    y = P.dram('y', [nseq * T, D], F32, 'ExternalOutput')
    prm = {k: P.dram(k, shp, F32, 'ExternalInput') for k, shp in PARAM_SHAPES.items()}
    cst = {k: P.dram('c_' + k, shp, F32, 'ExternalInput') for k, shp in CONST_SHAPES.items()}
    proj = P.dram('proj', [T, ODD_IN], F32)
    mixd = P.dram('mixd', [T, 2048], F32)
    ubd = P.dram('ubd', [T, 1792], F32)
    dbgo = P.dram('dbgo', [128, 8 * 512], F32, 'ExternalOutput') if dbg else None

    idb = P.sb('idb', [128, 128], BF16)
    msk = P.sb('msk', [128, 4, 128], BF16)
    tri = P.sb('tri', [128, 4, 128], F32)
    negcol = P.sb('negcol', [128, 1], F32)
    gpre = P.sb('gpre', [128, D], F32)
    gpost = P.sb('gpost', [128, D], F32)
    G = {}
    stt = [P.sb('stt%d' % i, [128, 8], F32) for i in range(2)]
    ev = [P.sb('ev%d' % i, [128, 512], F32) for i in range(3)]
    arena = P.sb('arena', [128, ARENA], BF16)
    ptr = [P.ps('ptr%d' % i, [128, 1024], BF16) for i in range(2)]
    pa = [P.ps('pa%d' % i, [128, 512], F32) for i in range(6)]

    DMA('pool', idb[:], cst['ident'], [], ['idb'])
    DMA('pool', msk[:].rearrange("p a b -> p (a b)"), cst['masks'], [], ['msk'])
    DMA('sp', tri[:].rearrange("p a b -> p (a b)"), cst['tri'], [], ['tri'])
    DMA('sp', negcol[:], cst['negcol'], [], ['negcol'])

    cnt = {'ev': 0, 'ptr': 0, 'ph': 0}

    class Phase:
        def __init__(self):
            P.barrier()
            self.off = 0
            cnt['ph'] += 1
            self.id = cnt['ph']

        def alloc(self, shape, dt):
            n = int(np.prod(shape[1:]))
            sz = n * (2 if dt == F32 else 1)
            self.off = (self.off + 7) // 8 * 8
            v = arena[:, self.off:self.off + sz]
            self.off += sz
            assert self.off <= ARENA, ('arena overflow', self.off)
            if dt == F32:
                v = v.bitcast(F32)
            if len(shape) == 3:
                v = v.rearrange("p (a b) -> p a b", a=shape[1])
            elif len(shape) == 4:
                v = v.rearrange("p (a b c) -> p a b c", a=shape[1], b=shape[2])
            return v

        def key(self, name):
            return 'ph%d_%s' % (self.id, name)

    def next_ev():
        cnt['ev'] = (cnt['ev'] + 1) % 3
        return cnt['ev']

    def next_ptr():
        cnt['ptr'] = (cnt['ptr'] + 1) % 2
        return cnt['ptr']

    def rms_rows(src_reads, src, b, width, eps):
        X('act', 'activation', src_reads, ['junk', 'stt%d' % b], out=G['junk'][:, 0:width], in_=src, func=AF.Square,
          accum_out=stt[b][:, 0:1])
        X('dve', 'tensor_scalar', ['stt%d' % b], ['stt%d' % b], out=stt[b][:, 1:2], in0=stt[b][:, 0:1],
          scalar1=1.0 / width, scalar2=eps, op0=ALU.mult, op1=ALU.add)
        X('act', 'activation', ['stt%d' % b], ['stt%d' % b], out=stt[b][:, 2:3], in_=stt[b][:, 1:2], func=AF.Sqrt)
        X('dve', 'reciprocal', ['stt%d' % b], ['stt%d' % b], out=stt[b][:, 3:4], in_=stt[b][:, 2:3])

    def transpose_tile(src_bf, src_key, nchunks, dst_fn, dst_key):
        for c0 in range(0, nchunks, 4):
            n = min(4, nchunks - c0)
            pb = next_ptr()
            for k in range(n):
                X('pe', 'transpose', [src_key, 'idb'], ['ptr%d' % pb], out=ptr[pb][:, k * 128:(k + 1) * 128],
                  in_=src_bf[:, (c0 + k) * 128:(c0 + k + 1) * 128], identity=idb[:])
            src = ptr[pb][:, 0:n * 128].rearrange("p (k t) -> p k t", k=n)
            if (c0 // 4) % 2 == 0:
                X('act', 'activation', ['ptr%d' % pb], [dst_key], out=dst_fn(c0, n), in_=src, func=AF.Copy)
            else:
                X('dve', 'tensor_copy', ['ptr%d' % pb], [dst_key], out=dst_fn(c0, n), in_=src)

    def rope(eng2, xin, xkey, cos, sin, ckey, tmpA, tmpB, tkey, out_bf, okey, H, dim):
        q = dim // 4
        v5 = lambda a: a.rearrange("p (h x f q) -> p h x f q", h=H, x=2, f=2)
        X('dve', 'tensor_tensor', [xkey, ckey], [tkey + 'A'], out=tmpA, in0=xin, in1=cos, op=ALU.mult)
        for f in range(2):
            X(eng2, 'tensor_tensor', [xkey, ckey], [tkey + 'B'], out=v5(tmpB)[:, :, :, f, :],
              in0=v5(xin)[:, :, :, 1 - f, :], in1=v5(sin)[:, :, :, f, :], op=ALU.mult)
        X('dve', 'tensor_tensor', [tkey + 'A', tkey + 'B'], [okey], out=out_bf, in0=tmpA, in1=tmpB, op=ALU.add)

    def prenorm_inproj(hsrc, layer, w_dram, ncols):
        ph = Phase()
        hnT = ph.alloc([128, 8, T], BF16)
        wt = [ph.alloc([128, 8, 512], BF16) for _ in range(2)]
        G['hx'] = [ph.alloc([128, D], F32) for _ in range(2)]
        G['hnb'] = [ph.alloc([128, 2048], BF16) for _ in range(2)]
        G['junk'] = ph.alloc([128, 2048], BF16)
        K = ph.key
        DMA('sp', gpre[:], prm['pre_gain'][layer:layer + 1, :].partition_broadcast(128), [], ['gpre'])
        for tt in range(NT):
            b = tt % 2
            DMA('sp', G['hx'][b][:], hsrc[tt * 128:(tt + 1) * 128, :], ['h%d' % tt], ['hx%d' % b])
            rms_rows(['hx%d' % b], G['hx'][b][:], b, D, EPS)
            X('dve', 'scalar_tensor_tensor', ['hx%d' % b, 'stt%d' % b, 'gpre'], ['hnb%d' % b], out=G['hnb'][b][:, 0:D],
              in0=G['hx'][b][:], scalar=stt[b][:, 3:4], in1=gpre[:], op0=ALU.mult, op1=ALU.mult)
            transpose_tile(G['hnb'][b], 'hnb%d' % b, 8,
                           lambda c0, n, tt=tt: hnT[:, c0:c0 + n, tt * 128:(tt + 1) * 128], K('hnT%d' % tt))
        for cb in range(ncols // 512):
            wb = cb % 2
            DMA('pool', wt[wb], w_dram[:, cb * 512:(cb + 1) * 512].rearrange("(kc p) n -> p kc n", p=128),
                [], [K('wt%d' % wb)])
            for tt in range(NT):
                pb = (cb * NT + tt) % 2
                for kc in range(8):
                    X('pe', 'matmul', [K('hnT%d' % tt), K('wt%d' % wb)], ['pa%d' % pb], out=pa[pb][:],
                      lhsT=hnT[:, kc, tt * 128:(tt + 1) * 128], rhs=wt[wb][:, kc, :], start=(kc == 0), stop=(kc == 7))
                e = next_ev()
                if tt % 2 == 0:
                    X('act', 'activation', ['pa%d' % pb], ['ev%d' % e], out=ev[e][:], in_=pa[pb][:], func=AF.Copy)
                else:
                    X('dve', 'tensor_copy', ['pa%d' % pb], ['ev%d' % e], out=ev[e][:], in_=pa[pb][:])
                DMA('sp', proj[tt * 128:(tt + 1) * 128, cb * 512:(cb + 1) * 512], ev[e][:], ['ev%d' % e],
                    ['proj%d_%d' % (tt, cb)])

    def out_proj(w_dram, Kdim, hsrc, hdst, layer):
        ph = Phase()
        KC = Kdim // 128
        wout = ph.alloc([128, KC, 1024], BF16)
        xT = [ph.alloc([128, KC, 128], BF16) for _ in range(2)]
        G['hx'] = [ph.alloc([128, D], F32) for _ in range(2)]
        G['hnb'] = [ph.alloc([128, 2048], BF16) for _ in range(2)]
        G['junk'] = ph.alloc([128, 2048], BF16)
        K = ph.key
        DMA('sp', gpost[:], prm['post_gain'][layer:layer + 1, :].partition_broadcast(128), [], ['gpost'])
        for k4 in range(0, KC, 4):
            DMA('pool', wout[:, k4:k4 + 4, :],
                w_dram[k4 * 128:(k4 + 4) * 128, :].rearrange("(kc p) n -> p kc n", p=128), [], [K('wout')])
        for tt in range(NT):
            b = tt % 2
            DMA('pool', G['hnb'][b][:, 0:Kdim], mixd[tt * 128:(tt + 1) * 128, 0:Kdim], ['mixd%d' % tt], ['hnb%d' % b])
            transpose_tile(G['hnb'][b], 'hnb%d' % b, KC, lambda c0, n, b=b: xT[b][:, c0:c0 + n, :], K('xT%d' % b))
            DMA('sp', G['hx'][b][:], hsrc[tt * 128:(tt + 1) * 128, :], ['h%d' % tt], ['hx%d' % b])
            for half in range(2):
                for kc in range(KC):
                    X('pe', 'matmul', [K('xT%d' % b), K('wout')], ['pa%d' % half], out=pa[half][:],
                      lhsT=xT[b][:, kc, :], rhs=wout[:, kc, half * 512:(half + 1) * 512], start=(kc == 0),
                      stop=(kc == KC - 1))
            X('act', 'activation', ['pa0'], ['junk', 'stt%d' % b], out=G['junk'][:, 0:512], in_=pa[0][:], func=AF.Square,
              accum_out=stt[b][:, 4:5])
            X('act', 'activation', ['pa1'], ['junk', 'stt%d' % b], out=G['junk'][:, 512:1024], in_=pa[1][:], func=AF.Square,
              accum_out=stt[b][:, 5:6])
            X('dve', 'tensor_tensor', ['stt%d' % b], ['stt%d' % b], out=stt[b][:, 0:1], in0=stt[b][:, 4:5],
              in1=stt[b][:, 5:6], op=ALU.add)
            X('dve', 'tensor_scalar', ['stt%d' % b], ['stt%d' % b], out=stt[b][:, 1:2], in0=stt[b][:, 0:1],
              scalar1=1.0 / D, scalar2=EPS, op0=ALU.mult, op1=ALU.add)
            X('act', 'activation', ['stt%d' % b], ['stt%d' % b], out=stt[b][:, 2:3], in_=stt[b][:, 1:2], func=AF.Sqrt)
            X('dve', 'reciprocal', ['stt%d' % b], ['stt%d' % b], out=stt[b][:, 3:4], in_=stt[b][:, 2:3])
            for half in range(2):
                e = next_ev()
                X('dve', 'scalar_tensor_tensor', ['pa%d' % half, 'stt%d' % b, 'gpost'], ['ev%d' % e], out=ev[e][:],
                  in0=pa[half][:], scalar=stt[b][:, 3:4], in1=gpost[:, half * 512:(half + 1) * 512], op0=ALU.mult,
                  op1=ALU.mult)
                X('pool', 'tensor_tensor', ['ev%d' % e, 'hx%d' % b], ['ev%d' % e], out=ev[e][:], in0=ev[e][:],
                  in1=G['hx'][b][:, half * 512:(half + 1) * 512], op=ALU.add)
                DMA('sp', hdst[tt * 128:(tt + 1) * 128, half * 512:(half + 1) * 512], ev[e][:], ['ev%d' % e],
                    ['h%d' % tt])

    def odd_mixer(layer):
        j = layer // 2
        ph = Phase()
        K = ph.key
        qkT = ph.alloc([128, 4, T], BF16)
        vS = ph.alloc([128, NT, 512], BF16)
        rdec = ph.alloc([128, 3968], BF16)
        pT = [ph.alloc([128, 512], BF16) for _ in range(2)]
        gng = ph.alloc([128, 2048], F32)
        rq = [ph.alloc([128, 512], F32) for _ in range(2)]
        ct = [ph.alloc([128, 512], F32) for _ in range(2)]
        sn = [ph.alloc([128, 512], F32) for _ in range(2)]
        tA = [ph.alloc([128, 512], F32) for _ in range(2)]
        tB = [ph.alloc([128, 512], F32) for _ in range(2)]
        rb = [ph.alloc([128, 512], BF16) for _ in range(2)]
        ob = [ph.alloc([128, 512], F32) for _ in range(2)]
        gt = [ph.alloc([128, 512], F32) for _ in range(2)]
        bst = [ph.alloc([128, 8], F32) for _ in range(2)]
        G['junk'] = ph.alloc([128, 2048], BF16)
        DMA('sp', gng, prm['odd_gn_g'][j:j + 1, :].partition_broadcast(128), [], [K('gng')])
        for h in range(4):
            DMA('pool', rdec, cst['retdec'][:, h * 3968:(h + 1) * 3968], [], [K('rdec')])
            for tt in range(NT):
                b = tt % 2
                r0 = slice(tt * 128, (tt + 1) * 128)
                DMA('sp', rq[b][:, 0:256], proj[r0, h * 256:(h + 1) * 256], ['proj%d_%d' % (tt, h // 2)], [K('rq%d' % b)])
                DMA('sp', rq[b][:, 256:512], proj[r0, 1024 + h * 256:1024 + (h + 1) * 256],
                    ['proj%d_%d' % (tt, 2 + h // 2)], [K('rq%d' % b)])
                DMA('sp', ct[b], cst['cosC'][r0, :], [], [K('cs%d' % b)])
                DMA('sp', sn[b], cst['sinC'][r0, :], [], [K('cs%d' % b)])
                rope('pool', rq[b], K('rq%d' % b), ct[b], sn[b], K('cs%d' % b), tA[b], tB[b], K('t%d' % b), rb[b],
                     K('rb%d' % b), 2, 256)
                transpose_tile(rb[b], K('rb%d' % b), 4, lambda c0, n, tt=tt: qkT[:, 0:4, tt * 128:(tt + 1) * 128],
                               K('qkT%d' % tt))
            for t4 in range(0, NT, 4):
                DMA('pool', vS[:, t4:t4 + 4, :],
                    proj[t4 * 128:(t4 + 4) * 128, 2048 + h * 512:2048 + (h + 1) * 512].rearrange("(t p) n -> p t n", p=128),
                    ['proj%d_%d' % (tt, 4 + h) for tt in range(t4, t4 + 4)], [K('vS%d' % tt) for tt in range(t4, t4 + 4)])
            iters = [(cn4, cm) for cn4 in range(4) for cm in range(NT)]

            def r_scores(i):
                cn4, cm = iters[i]
                s = i % 2
                qkeys = [K('qkT%d' % tt) for tt in range(cn4 * 4, cn4 * 4 + 4)]
                for dc in range(2):
                    X('pe', 'matmul', qkeys + [K('qkT%d' % cm)], ['pa%d' % s], out=pa[s][:],
                      lhsT=qkT[:, 2 + dc, cm * 128:(cm + 1) * 128], rhs=qkT[:, dc, cn4 * 512:(cn4 + 1) * 512],
                      start=(dc == 0), stop=(dc == 1))

            def r_rest(i):
                cn4, cm = iters[i]
                s = i % 2
                off = cn4 * 512 - cm * 128 + 1920
                X('dve', 'tensor_tensor', ['pa%d' % s, K('rdec')], [K('pT%d' % s)], out=pT[s], in0=pa[s][:],
                  in1=rdec[:, off:off + 512], op=ALU.mult)
                for qs in range(4):
                    X('pe', 'matmul', [K('pT%d' % s), K('vS%d' % cm)], ['pa%d' % (2 + qs)], out=pa[2 + qs][:],
                      lhsT=pT[s][:, qs * 128:(qs + 1) * 128], rhs=vS[:, cm, :], start=(cm == 0), stop=(cm == NT - 1))

            r_scores(0)
            for i_ in range(len(iters)):
                cn4, cm = iters[i_]
                if i_ + 1 < len(iters):
                    r_scores(i_ + 1)
                r_rest(i_)
                if cm != NT - 1:
                    continue
                for qs in range(4):
                    tt = cn4 * 4 + qs
                    b = qs % 2
                    r0 = slice(tt * 128, (tt + 1) * 128)
                    X('act', 'activation', ['pa%d' % (2 + qs)], [K('ob%d' % b)], out=ob[b], in_=pa[2 + qs][:], func=AF.Copy)
                    DMA('sp', gt[b], proj[r0, 4096 + h * 512:4096 + (h + 1) * 512], ['proj%d_%d' % (tt, 8 + h)], [K('gt%d' % b)])
                    X('act', 'activation', [K('gt%d' % b)], [K('gt%d' % b)], out=gt[b], in_=gt[b], func=AF.Silu)
                    X('act', 'activation', [K('ob%d' % b)], ['junk', K('bst%d' % b)], out=G['junk'][:, 0:512], in_=ob[b],
                      func=AF.Square, accum_out=bst[b][:, 0:1])
                    X('dve', 'tensor_reduce', [K('ob%d' % b)], [K('bst%d' % b)], out=bst[b][:, 1:2], in_=ob[b], axis=AX.X,
                      op=ALU.add)
                    X('dve', 'tensor_scalar', [K('bst%d' % b)], [K('bst%d' % b)], out=bst[b][:, 2:3], in0=bst[b][:, 1:2],
                      scalar1=1.0 / 512, scalar2=None, op0=ALU.mult)
                    X('dve', 'tensor_tensor', [K('bst%d' % b)], [K('bst%d' % b)], out=bst[b][:, 3:4], in0=bst[b][:, 2:3],
                      in1=bst[b][:, 2:3], op=ALU.mult)
                    X('dve', 'scalar_tensor_tensor', [K('bst%d' % b)], [K('bst%d' % b)], out=bst[b][:, 4:5],
                      in0=bst[b][:, 0:1], scalar=1.0 / 512, in1=bst[b][:, 3:4], op0=ALU.mult, op1=ALU.subtract)
                    X('dve', 'tensor_scalar', [K('bst%d' % b)], [K('bst%d' % b)], out=bst[b][:, 4:5], in0=bst[b][:, 4:5],
                      scalar1=1e-5, scalar2=None, op0=ALU.add)
                    X('act', 'activation', [K('bst%d' % b)], [K('bst%d' % b)], out=bst[b][:, 5:6], in_=bst[b][:, 4:5],
                      func=AF.Sqrt)
                    X('dve', 'reciprocal', [K('bst%d' % b)], [K('bst%d' % b)], out=bst[b][:, 6:7], in_=bst[b][:, 5:6])
                    X('dve', 'tensor_scalar', [K('ob%d' % b), K('bst%d' % b)], [K('ob%d' % b)], out=ob[b], in0=ob[b],
                      scalar1=bst[b][:, 2:3], scalar2=bst[b][:, 6:7], op0=ALU.subtract, op1=ALU.mult)
                    X('pool', 'tensor_tensor', [K('ob%d' % b), K('gng')], [K('ob%d' % b)], out=ob[b], in0=ob[b],
                      in1=gng[:, h * 512:(h + 1) * 512], op=ALU.mult)
                    X('pool', 'tensor_tensor', [K('ob%d' % b), K('gt%d' % b)], [K('ob%d' % b)], out=ob[b], in0=ob[b],
                      in1=gt[b], op=ALU.mult)
                    DMA('sp', mixd[r0, h * 512:(h + 1) * 512], ob[b], [K('ob%d' % b)], ['mixd%d' % tt])


    def bcast_row(ph, key, src_row, width, eng='sp'):
        t = ph.alloc([128, width], F32)
        DMA(eng, t, src_row.partition_broadcast(128), [], [ph.key(key)])
        return t

    def zero_mix(c0, c1):
        X('dve', 'memset', [], ['ev0'], ap=ev[0][:], constant=0.0)
        for tt in range(NT):
            DMA('sp', mixd[tt * 128:(tt + 1) * 128, c0:c1], ev[0][:, 0:c1 - c0], ['ev0'], ['mixd%d' % tt])

    def attn_part(i):
        ph = Phase()
        K = ph.key
        qT6 = ph.alloc([128, 6, T], BF16)
        vA = ph.alloc([128, NT, 2, 66], BF16)
        outA = ph.alloc([128, NT, 512], F32)
        pT = [ph.alloc([128, 512], BF16) for _ in range(2)]
        qg = ph.alloc([128, 640], F32)
        aq = [ph.alloc([128, 640], F32) for _ in range(2)]
        sq = [ph.alloc([128, 640], F32) for _ in range(2)]
        ss = [ph.alloc([128, 32], F32) for _ in range(2)]
        ct = [ph.alloc([128, 640], F32) for _ in range(2)]
        sn = [ph.alloc([128, 640], F32) for _ in range(2)]
        tA = [ph.alloc([128, 640], F32) for _ in range(2)]
        tB = [ph.alloc([128, 640], F32) for _ in range(2)]
        rb = [ph.alloc([128, 768], BF16) for _ in range(2)]
        gt = [ph.alloc([128, 512], F32) for _ in range(2)]
        rs = [ph.alloc([128, 4], F32) for _ in range(4)]
        for h in range(8):
            DMA('sp', qg[:, h * 64:(h + 1) * 64], prm['even_q_gain'][i:i + 1, :].partition_broadcast(128), [], [K('qg')])
        for h in range(2):
            DMA('sp', qg[:, 512 + h * 64:512 + (h + 1) * 64], prm['even_k_gain'][i:i + 1, :].partition_broadcast(128), [],
                [K('qg')])
        X('pool', 'memset', [], [K('vA')], ap=vA.rearrange("p a b c -> p (a b c)"), constant=1.0)
        for t4 in range(0, NT, 4):
            for g in range(2):
                DMA('pool', vA[:, t4:t4 + 4, g, 0:64],
                    proj[t4 * 128:(t4 + 4) * 128, 640 + g * 64:640 + (g + 1) * 64].rearrange("(t p) d -> p t d", p=128),
                    ['proj%d_1' % tt for tt in range(t4, t4 + 4)], [K('vA')])
        v3 = lambda a: a.rearrange("p (h d) -> p h d", d=64)
        for tt in range(NT):
            b = tt % 2
            r0 = slice(tt * 128, (tt + 1) * 128)
            DMA('sp', aq[b], proj[r0, 0:640], ['proj%d_0' % tt, 'proj%d_1' % tt], [K('aq%d' % b)])
            DMA('sp', ct[b], cst['cosA'][r0, :], [], [K('cs%d' % b)])
            DMA('sp', sn[b], cst['sinA'][r0, :], [], [K('cs%d' % b)])
            X('pool', 'tensor_tensor', [K('aq%d' % b)], [K('sq%d' % b)], out=sq[b], in0=aq[b], in1=aq[b], op=ALU.mult)
            X('dve', 'tensor_reduce', [K('sq%d' % b)], [K('ss%d' % b)], out=ss[b][:, 0:10], in_=v3(sq[b]), axis=AX.X, op=ALU.add)
            X('dve', 'tensor_scalar', [K('ss%d' % b)], [K('ss%d' % b)], out=ss[b][:, 10:20], in0=ss[b][:, 0:10],
              scalar1=1.0 / 64, scalar2=EPS, op0=ALU.mult, op1=ALU.add)
            X('act', 'activation', [K('ss%d' % b)], [K('ss%d' % b)], out=ss[b][:, 20:30], in_=ss[b][:, 10:20], func=AF.Sqrt)
            X('dve', 'reciprocal', [K('ss%d' % b)], [K('ss%d' % b)], out=ss[b][:, 0:10], in_=ss[b][:, 20:30])
            X('dve', 'tensor_tensor', [K('aq%d' % b), K('ss%d' % b)], [K('aq%d' % b)], out=v3(aq[b]), in0=v3(aq[b]),
              in1=ss[b][:, 0:10].unsqueeze(2).to_broadcast([128, 10, 64]), op=ALU.mult)
            X('pool', 'tensor_tensor', [K('aq%d' % b), K('qg')], [K('aq%d' % b)], out=aq[b], in0=aq[b], in1=qg, op=ALU.mult)
            rope('pool', aq[b], K('aq%d' % b), ct[b], sn[b], K('cs%d' % b), tA[b], tB[b], K('t%d' % b), rb[b][:, 0:640],
                 K('rb%d' % b), 10, 64)
            X('pool', 'tensor_copy', [K('rb%d' % b)], [K('rb%d' % b)], out=rb[b][:, 640:704], in_=rb[b][:, 576:640])
            X('pool', 'tensor_copy', [K('rb%d' % b)], [K('rb%d' % b)], out=rb[b][:, 704:768], in_=rb[b][:, 512:576])
            transpose_tile(rb[b], K('rb%d' % b), 6, lambda c0, n, tt=tt: qT6[:, c0:c0 + n, tt * 128:(tt + 1) * 128],
                           K('qT%d' % tt))
        aiters = [(hq, cn4, cm) for hq in range(8) for cn4 in range(4) for cm in range(NT)]

        def a_par(hq):
            g = hq // 4
            base = (hq % 2) * 64
            pair = hq // 2
            kc = 4 + (1 if g != base // 64 else 0)
            return g, base, pair, kc

        def a_scores(i):
            hq, cn4, cm = aiters[i]
            g, base, pair, kc = a_par(hq)
            s_ = i % 2
            qkeys = [K('qT%d' % tt) for tt in range(cn4 * 4, cn4 * 4 + 4)]
            X('pe', 'matmul', qkeys + [K('qT%d' % cm)], ['pa%d' % s_], out=pa[s_][:],
              lhsT=qT6[base:base + 64, kc, cm * 128:(cm + 1) * 128],
              rhs=qT6[base:base + 64, pair, cn4 * 512:(cn4 + 1) * 512], start=True, stop=True)

        def a_rest(i):
            hq, cn4, cm = aiters[i]
            g, base, pair, kc = a_par(hq)
            s_ = i % 2
            X('act', 'activation', ['pa%d' % s_], [K('pT%d' % s_)], out=pT[s_], in_=pa[s_][:], func=AF.Exp, scale=0.125)
            for qs in range(4):
                X('pe', 'matmul', [K('pT%d' % s_), K('vA')], ['pa%d' % (2 + qs)], out=pa[2 + qs][:, 0:65],
                  lhsT=pT[s_][:, qs * 128:(qs + 1) * 128], rhs=vA[:, cm, g, 0:65], start=(cm == 0), stop=(cm == NT - 1))

        a_scores(0)
        for i_ in range(len(aiters)):
            hq, cn4, cm = aiters[i_]
            if i_ + 1 < len(aiters):
                a_scores(i_ + 1)
            a_rest(i_)
            if cm == NT - 1:
                for qs in range(4):
                    tt = cn4 * 4 + qs
                    X('dve', 'reciprocal', ['pa%d' % (2 + qs)], [K('rs%d' % qs)], out=rs[qs][:, 0:1], in_=pa[2 + qs][:, 64:65])
                    X('dve', 'tensor_scalar', ['pa%d' % (2 + qs), K('rs%d' % qs)], [K('outA%d' % tt)],
                      out=outA[:, tt, hq * 64:(hq + 1) * 64], in0=pa[2 + qs][:, 0:64], scalar1=rs[qs][:, 0:1], scalar2=None,
                      op0=ALU.mult)
        for tt in range(NT):
            b = tt % 2
            r0 = slice(tt * 128, (tt + 1) * 128)
            DMA('sp', gt[b], proj[r0, 768:1280], ['proj%d_1' % tt, 'proj%d_2' % tt], [K('gt%d' % b)])
            X('act', 'activation', [K('gt%d' % b)], [K('gt%d' % b)], out=gt[b], in_=gt[b], func=AF.Silu)
            X('pool', 'tensor_tensor', [K('gt%d' % b), K('outA%d' % tt)], [K('gt%d' % b)], out=gt[b], in0=gt[b],
              in1=outA[:, tt, :], op=ALU.mult)
            DMA('sp', mixd[r0, 0:512], gt[b], [K('gt%d' % b)], ['mixd%d' % tt])

    def rwkv_part(i):
        ph = Phase()
        K = ph.key
        MU = bcast_row(ph, 'MU', prm['even_mu'][i:i + 1, :], 1792)
        s0 = [ph.alloc([128, 1792], F32) for _ in range(2)]
        sm = [ph.alloc([128, 1792], F32) for _ in range(2)]
        sp_ = [ph.alloc([128, 1792], F32) for _ in range(2)]
        pk = lambda tt: ['proj%d_%d' % (tt, cb) for cb in range(2, 6)]
        for tt in range(NT):
            b = tt % 2
            r0 = tt * 128
            DMA('sp', s0[b], proj[r0:r0 + 128, 1280:3072], pk(tt), [K('s0%d' % b)])
            if tt == 0:
                X('pool', 'memset', [], [K('sm%d' % b)], ap=sm[b], constant=0.0)
                DMA('sp', sm[b][1:128, :], proj[0:127, 1280:3072], pk(tt), [K('sm%d' % b)])
            else:
                DMA('sp', sm[b], proj[r0 - 1:r0 + 127, 1280:3072], pk(tt) + pk(tt - 1), [K('sm%d' % b)])
            if tt == NT - 1:
                X('pool', 'memset', [], [K('sp%d' % b)], ap=sp_[b], constant=0.0)
                DMA('sp', sp_[b][0:127, :], proj[r0 + 1:r0 + 128, 1280:3072], pk(tt), [K('sp%d' % b)])
            else:
                DMA('sp', sp_[b], proj[r0 + 1:r0 + 129, 1280:3072], pk(tt) + pk(tt + 1), [K('sp%d' % b)])
            X('pool', 'tensor_tensor', [K('sm%d' % b), K('sp%d' % b)], [K('sm%d' % b)], out=sm[b], in0=sm[b], in1=sp_[b],
              op=ALU.add)
            X('dve', 'scalar_tensor_tensor', [K('sm%d' % b), K('s0%d' % b)], [K('sm%d' % b)], out=sm[b], in0=sm[b],
              scalar=0.5, in1=s0[b], op0=ALU.mult, op1=ALU.subtract)
            X('pool', 'tensor_tensor', [K('sm%d' % b), K('MU')], [K('sm%d' % b)], out=sm[b], in0=sm[b], in1=MU, op=ALU.mult)
            X('dve', 'tensor_tensor', [K('sm%d' % b), K('s0%d' % b)], [K('sm%d' % b)], out=sm[b], in0=sm[b], in1=s0[b],
              op=ALU.add)
            DMA('sp', ubd[r0:r0 + 128, :], sm[b], [K('sm%d' % b)], ['ub%d' % tt])

        ph = Phase()
        K = ph.key
        Yacc = ph.alloc([128, NT, 512], F32)
        KKB = bcast_row(ph, 'KKB', prm['even_k_k'][i:i + 1, :], 512)
        KAB = bcast_row(ph, 'KAB', prm['even_k_a'][i:i + 1, :], 512)
        RKB = bcast_row(ph, 'RKB', prm['even_r_k'][i:i + 1, :], 512)
        LNG = bcast_row(ph, 'LNG', prm['even_lnx_g'][i:i + 1, :], 512)
        LNB = bcast_row(ph, 'LNB', prm['even_lnx_b'][i:i + 1, :], 512)
        W0 = [bcast_row(ph, 'W0%d' % d, prm['even_w0_' + 'fb'[d]][i:i + 1, :], 512) for d in range(2)]
        A0 = [bcast_row(ph, 'A0%d' % d, prm['even_a0_' + 'fb'[d]][i:i + 1, :], 512) for d in range(2)]
        NEGB = ph.alloc([128, 512], F32)
        X('dve', 'memset', [], [K('NEGB')], ap=NEGB, constant=-1.0)
        mark = ph.off
        W2A2 = [ph.alloc([128, 512], BF16) for _ in range(2)]
        for d in range(2):
            DMA('pool', W2A2[d][0:64, :], prm['even_w2_' + 'fb'[d]][i], [], [K('W2A2%d' % d)])
            DMA('pool', W2A2[d][64:128, :], prm['even_a2_' + 'fb'[d]][i], [], [K('W2A2%d' % d)])
        A_ = lambda shape, dt: [ph.alloc(shape, dt) for _ in range(2)]
        u = A_([128, 1792], F32)
        lo = A_([128, 128], BF16)
        loT = A_([128, 128], BF16)
        sgm = A_([128, 512], F32)
        alp = A_([128, 512], F32)
        Epl = A_([128, 512], F32)
        Emi = A_([128, 512], F32)
        Epr = A_([128, 512], F32)
        kk = A_([128, 512], F32)
        kx = A_([128, 512], F32)
        bb = A_([128, 512], F32)
        tm = A_([128, 512], F32)
        ss8 = A_([128, 32], F32)
        A2 = lambda shape, dt: [[ph.alloc(shape, dt) for _ in range(2)] for _ in range(2)]
        Kt2 = A2([128, 512], BF16)
        Bt2 = A2([128, 512], BF16)
        Vt2 = A2([128, 512], BF16)
        ARs = A_([128, 2, 512], BF16)
        ARt2 = A2([128, 4, 2, 128], BF16)
        KBt2 = A2([128, 4, 2, 128], BF16)
        PC2 = A2([128, 4], F32)
        SC = A_([128, 8, 512], BF16)
        TI = A_([128, 8, 128], BF16)
        L0 = [ph.alloc([128, 128], BF16) for _ in range(4)]
        Wk = [[ph.alloc([128, 384], BF16) for _ in range(2)] for _ in range(4)]
        Xb = A_([128, 4, 128], BF16)
        Ub = A_([128, 4, 128], BF16)
        H32 = A_([128, 4, 128], F32)
        Hbf = A_([128, 4, 128], BF16)
        t32 = A_([128, 128], F32)
        mk4 = A_([128, 512], BF16)
        for d in range(2):
            for rep in range(2):
                X('pool', 'tensor_copy', ['msk'], [K('mk4%d' % d)], out=mk4[d][:, rep * 256:(rep + 1) * 256],
                  in_=msk[:, 2 * d:2 * d + 2, :].rearrange("p a b -> p (a b)"))
            X('pool', 'memset', [], [K('H32%d' % d)], ap=H32[d].rearrange("p a b -> p (a b)"), constant=0.0)
            X('pool', 'memset', [], [K('Hbf%d' % d)], ap=Hbf[d].rearrange("p a b -> p (a b)"), constant=0.0)
        X('pool', 'memset', [], [K('Yacc%d' % tt) for tt in range(NT)], ap=Yacc.rearrange("p a b -> p (a b)"), constant=0.0)
        v8 = lambda a: a.rearrange("p (h d) -> p h d", d=64)

        PS = float(os.environ.get('PREP_STAGE', '9'))

        def prep_gen(d, tt, cpar):
            kd = lambda n: K('%s%d' % (n, d))
            kp = lambda n: K('%s%d_%d' % (n, d, cpar))
            r0 = tt * 128
            DMA('sp', u[d], ubd[r0:r0 + 128, :], ['ub%d' % tt], [kd('u')])
            r_ = u[d][:, 0:512]
            kb_ = u[d][:, 512:1024]
            vb_ = u[d][:, 1024:1536]
            X('act', 'activation', [kd('u')], [kd('lo')], out=lo[d][:, 0:64], in_=u[d][:, 1536 + d * 64:1600 + d * 64],
              func=AF.Tanh)
            X('dve', 'tensor_copy', [kd('u')], [kd('lo')], out=lo[d][:, 64:128], in_=u[d][:, 1664 + d * 64:1728 + d * 64])
            pb = next_ptr()
            X('pe', 'transpose', [kd('lo'), 'idb'], ['ptr%d' % pb], out=ptr[pb][:, 0:128], in_=lo[d], identity=idb[:])
            X('act', 'activation', ['ptr%d' % pb], [kd('loT')], out=loT[d], in_=ptr[pb][:, 0:128], func=AF.Copy)
            X('pe', 'matmul', [kd('loT'), K('W2A2%d' % d)], ['pa0'], out=pa[0][:], lhsT=loT[d][0:64, :],
              rhs=W2A2[d][0:64, :], start=True, stop=True)
            X('pe', 'matmul', [kd('loT'), K('W2A2%d' % d)], ['pa1'], out=pa[1][:], lhsT=loT[d][64:128, :],
              rhs=W2A2[d][64:128, :], start=True, stop=True)
            X('dve', 'tensor_tensor', ['pa0', K('W0%d' % d)], [kd('sgm')], out=sgm[d], in0=pa[0][:], in1=W0[d], op=ALU.add)
            X('act', 'activation', [kd('sgm')], [kd('sgm')], out=sgm[d], in_=sgm[d], func=AF.Sigmoid)
            X('dve', 'tensor_tensor', ['pa1', K('A0%d' % d)], [kd('alp')], out=alp[d], in0=pa[1][:], in1=A0[d], op=ALU.add)
            X('act', 'activation', [kd('alp')], [kd('alp')], out=alp[d], in_=alp[d], func=AF.Sigmoid)
            yield
            X('pe', 'matmul', [kd('sgm'), 'tri'], ['pa0'], out=pa[0][:], lhsT=tri[:, 2 * d + 1, :], rhs=sgm[d], start=True,
              stop=True)
            X('pe', 'matmul', [kd('sgm'), 'tri'], ['pa1'], out=pa[1][:], lhsT=tri[:, 2 * d, :], rhs=sgm[d], start=True,
              stop=True)
            X('act', 'activation', ['pa0'], [kd('Epl')], out=Epl[d], in_=pa[0][:], func=AF.Exp)
            X('act', 'activation', ['pa0'], [kd('Emi')], out=Emi[d], in_=pa[0][:], func=AF.Exp, scale=-1.0)
            X('act', 'activation', ['pa1'], [kd('Epr')], out=Epr[d], in_=pa[1][:], func=AF.Exp)
            yield
            for p_ in range(4):
                X('pe', 'matmul', [kd('sgm'), 'negcol'], ['pa0'], out=pa[0][:, p_:p_ + 1],
                  lhsT=sgm[d][:, p_ * 128:(p_ + 1) * 128], rhs=negcol[:, 0:1], start=True, stop=True)
            X('act', 'activation', ['pa0'], [kp('PC')], out=PC2[cpar][d], in_=pa[0][:, 0:4], func=AF.Exp)
            yield
            X('pool', 'tensor_tensor', [kd('u'), K('KKB')], [kd('kk')], out=kk[d], in0=kb_, in1=KKB, op=ALU.mult)
            X('pool', 'tensor_tensor', [kd('kk')], [kd('tm')], out=tm[d], in0=kk[d], in1=kk[d], op=ALU.mult)
            yield
            X('dve', 'tensor_reduce', [kd('tm')], [kd('ss8')], out=ss8[d][:, 0:8], in_=v8(tm[d]), axis=AX.X, op=ALU.add)
            X('act', 'activation', [kd('ss8')], [kd('ss8')], out=ss8[d][:, 8:16], in_=ss8[d][:, 0:8], func=AF.Sqrt)
            X('dve', 'tensor_scalar', [kd('ss8')], [kd('ss8')], out=ss8[d][:, 8:16], in0=ss8[d][:, 8:16], scalar1=1e-12,
              scalar2=None, op0=ALU.max)
            X('dve', 'reciprocal', [kd('ss8')], [kd('ss8')], out=ss8[d][:, 16:24], in_=ss8[d][:, 8:16])
            yield
            X('pool', 'tensor_tensor', [kd('kk'), kd('ss8')], [kd('kk')], out=v8(kk[d]), in0=v8(kk[d]),
              in1=ss8[d][:, 16:24].unsqueeze(2).to_broadcast([128, 8, 64]), op=ALU.mult)
            yield
            X('dve', 'tensor_tensor', [kd('alp'), K('KAB')], [kd('kx')], out=kx[d], in0=alp[d], in1=KAB, op=ALU.mult)
            X('dve', 'tensor_tensor', [kd('kx'), K('KAB')], [kd('kx')], out=kx[d], in0=kx[d], in1=KAB, op=ALU.subtract)
            X('dve', 'tensor_tensor', [kd('kx'), kd('u')], [kd('kx')], out=kx[d], in0=kx[d], in1=kb_, op=ALU.mult)
            X('dve', 'tensor_tensor', [kd('kx'), kd('u')], [kd('kx')], out=kx[d], in0=kx[d], in1=kb_, op=ALU.add)
            X('pool', 'tensor_tensor', [kd('kk'), kd('alp')], [kd('bb')], out=bb[d], in0=kk[d], in1=alp[d], op=ALU.mult)
            yield
            v4 = lambda a: a.rearrange("p (a c) -> p a c", a=4)
            X('dve', 'tensor_tensor', [kd('kk'), kd('Epr')], [kd('tm')], out=tm[d], in0=kk[d], in1=Epr[d], op=ALU.mult)
            X('dve', 'tensor_tensor', [kd('tm'), K('NEGB')], [kd('ARs')], out=ARs[d][:, 0, :], in0=tm[d], in1=NEGB,
              op=ALU.mult)
            X('dve', 'tensor_tensor', [kd('u'), kd('Epl')], [kd('ARs')], out=ARs[d][:, 1, :], in0=r_, in1=Epl[d],
              op=ALU.mult)
            X('dve', 'tensor_tensor', [kd('kx'), kd('Emi')], [kp('Kt')], out=Kt2[cpar][d], in0=kx[d], in1=Emi[d], op=ALU.mult)
            X('dve', 'tensor_tensor', [kd('bb'), kd('Emi')], [kp('Bt')], out=Bt2[cpar][d], in0=bb[d], in1=Emi[d], op=ALU.mult)
            X('dve', 'tensor_copy', [kd('u')], [kp('Vt')], out=Vt2[cpar][d], in_=vb_)
            if dbg and os.environ.get('DBG_LIST'):
                lst = [tuple(int(v) for v in it.split(':')) for it in os.environ['DBG_LIST'].split(',')]
                if (d, tt) in lst:
                    qi = lst.index((d, tt))
                    DMA('sp', dbgo[:, qi * 512:(qi + 1) * 512], Emi[d], [kd('Emi')], ['dbgo%d' % qi])
            yield
            for p_ in range(4):
                pb = next_ptr()
                X('pe', 'transpose', [kd('ARs'), 'idb'], ['ptr%d' % pb], out=ptr[pb][:, 0:128], in_=ARs[d][:, 0, p_ * 128:(p_ + 1) * 128],
                  identity=idb[:])
                X('pe', 'transpose', [kd('ARs'), 'idb'], ['ptr%d' % pb], out=ptr[pb][:, 128:256], in_=ARs[d][:, 1, p_ * 128:(p_ + 1) * 128],
                  identity=idb[:])
                X('act', 'activation', ['ptr%d' % pb], [kp('ARt')], out=ARt2[cpar][d][:, p_, :, :].rearrange("p x c -> p (x c)"),
                  in_=ptr[pb][:, 0:256], func=AF.Copy)
                pb = next_ptr()
                X('pe', 'transpose', [kp('Kt'), 'idb'], ['ptr%d' % pb], out=ptr[pb][:, 0:128],
                  in_=Kt2[cpar][d][:, p_ * 128:(p_ + 1) * 128], identity=idb[:])
                X('pe', 'transpose', [kp('Bt'), 'idb'], ['ptr%d' % pb], out=ptr[pb][:, 128:256],
                  in_=Bt2[cpar][d][:, p_ * 128:(p_ + 1) * 128], identity=idb[:])
                X('act', 'activation', ['ptr%d' % pb], [kp('KTt'), kp('BTt')],
                  out=KBt2[cpar][d][:, p_, :, :].rearrange("p x c -> p (x c)"), in_=ptr[pb][:, 0:256], func=AF.Copy)
                yield

            yield

        def head_gen(d, h, slot, cpar):
            kd = lambda n: K('%s%d' % (n, d))
            kp = lambda n: K('%s%d_%d' % (n, d, cpar))
            par = h % 2
            base = par * 64
            p_ = h // 2
            z = 2 + slot
            ev_eng = 'act' if slot < 2 else 'dve'

            def evac(reads, writes, out, in_):
                if ev_eng == 'act':
                    X('act', 'activation', reads, writes, out=out, in_=in_, func=AF.Copy)
                else:
                    X('dve', 'tensor_copy', reads, writes, out=out, in_=in_)
            arf = ARt2[cpar][d][base:base + 64, p_, :, :].rearrange("p x c -> p (x c)")
            aT = ARt2[cpar][d][base:base + 64, p_, 0, :]
            X('pe', 'matmul', [kp('BTt'), kp('ARt')], ['pa0'], out=pa[0][:, 0:256], lhsT=KBt2[cpar][d][base:base + 64, p_, 1, :],
              rhs=arf, start=True, stop=True)
            X('pe', 'matmul', [kp('KTt'), kp('ARt')], ['pa0'], out=pa[0][:, 256:512], lhsT=KBt2[cpar][d][base:base + 64, p_, 0, :],
              rhs=arf, start=True, stop=True)
            X('pe', 'matmul', [kp('BTt'), kp('ARt')], ['pa1'], out=pa[1][:, 0:128], lhsT=aT,
              rhs=KBt2[cpar][d][base:base + 64, p_, 1, :], start=True, stop=True)
            sck = kd('SC%d_' % h)
            X('dve', 'tensor_tensor', ['pa0', K('mk4%d' % d)], [sck], out=SC[d][:, h, :], in0=pa[0][:], in1=mk4[d],
              op=ALU.mult)
            l0k = K('L0%d' % slot)
            X('dve', 'tensor_tensor', ['pa1', 'msk'], [l0k], out=L0[slot], in0=pa[1][:, 0:128], in1=msk[:, 2 - 2 * d, :],
              op=ALU.mult)
            yield
            N0 = SC[d][:, h, 0:128]
            wk = [K('Wk%d_%d' % (slot, q)) for q in range(2)]
            W = Wk[slot]
            X('pe', 'matmul', [sck, 'idb'], ['pa%d' % z], out=pa[z][:, 0:128], lhsT=idb[:], rhs=N0, start=True, stop=False)
            X('pe', 'matmul', [sck, 'idb'], ['pa%d' % z], out=pa[z][:, 0:128], lhsT=idb[:], rhs=idb[:], start=False, stop=True)
            X('pe', 'matmul', [l0k, sck], ['pa%d' % z], out=pa[z][:, 128:256], lhsT=L0[slot], rhs=N0, start=True, stop=True)
            X('pe', 'matmul', [l0k, sck], ['pa%d' % z], out=pa[z][:, 256:384], lhsT=N0, rhs=L0[slot], start=True, stop=True)
            evac(['pa%d' % z], [wk[0]], W[0][:, 0:384], pa[z][:, 0:384])
            yield
            cur = 0
            for k_ in range(1, 6):
                a_, b_ = cur, 1 - cur
                rk = [wk[a_], 'idb']
                X('pe', 'matmul', rk, ['pa%d' % z], out=pa[z][:, 0:128], lhsT=idb[:], rhs=W[a_][:, 0:128], start=True, stop=False)
                X('pe', 'matmul', rk, ['pa%d' % z], out=pa[z][:, 0:128], lhsT=W[a_][:, 256:384], rhs=W[a_][:, 0:128],
                  start=False, stop=True)
                X('pe', 'matmul', rk, ['pa%d' % z], out=pa[z][:, 128:256], lhsT=W[a_][:, 256:384], rhs=W[a_][:, 128:256],
                  start=True, stop=True)
                X('pe', 'matmul', rk, ['pa%d' % z], out=pa[z][:, 256:384], lhsT=W[a_][:, 128:256], rhs=W[a_][:, 256:384],
                  start=True, stop=True)
                evac(['pa%d' % z], [wk[b_]], W[b_][:, 0:384], pa[z][:, 0:384])
                cur = b_
                yield
            rk = [wk[cur], 'idb']
            X('pe', 'matmul', rk, ['pa%d' % z], out=pa[z][:, 0:128], lhsT=idb[:], rhs=W[cur][:, 0:128], start=True, stop=False)
            X('pe', 'matmul', rk, ['pa%d' % z], out=pa[z][:, 0:128], lhsT=W[cur][:, 256:384], rhs=W[cur][:, 0:128],
              start=False, stop=True)
            evac(['pa%d' % z], [kd('TI%d_' % h)], TI[d][:, h, :], pa[z][:, 0:128])
            yield

        def run_rr(gens):
            alive = list(gens)
            while alive:
                nxt = []
                for g in alive:
                    try:
                        next(g)
                        nxt.append(g)
                    except StopIteration:
                        pass
                alive = nxt

        def run_heads(par, extra):
            combos = [(d, h) for h in range(8) for d in range(2)]
            extra = list(extra)
            for g0 in range(0, 16, 4):
                alive = [head_gen(d, h, slot, par) for slot, (d, h) in enumerate(combos[g0:g0 + 4])]
                while alive:
                    nxt = []
                    for g in alive:
                        try:
                            next(g)
                            nxt.append(g)
                        except StopIteration:
                            pass
                    alive = nxt
                    ex2 = []
                    for g in extra:
                        try:
                            next(g)
                            ex2.append(g)
                        except StopIteration:
                            pass
                    extra = ex2
            run_rr(extra)

        def run_chains(c, par):
            run_rr([chain_gen(0, c, 2, par, [0, 2]), chain_gen(0, c, 3, par, [1, 3]),
                    chain_gen(1, NT - 1 - c, 4, par, [0, 2]), chain_gen(1, NT - 1 - c, 5, par, [1, 3])])

        def chain_gen(d, tt, bx, cpar, pairs):
            kd = lambda n: K('%s%d' % (n, d))
            kp = lambda n: K('%s%d_%d' % (n, d, cpar))
            for p_ in pairs:
                hk = kd('H%d_' % p_)
                for hh in range(2):
                    h = 2 * p_ + hh
                    base = hh * 64
                    cs = slice(hh * 64, hh * 64 + 64)
                    X('pe', 'matmul', [kd('SC%d_' % h), kp('Vt')], ['pa%d' % bx], out=pa[bx][:, cs], lhsT=SC[d][:, h, 256:384],
                      rhs=Vt2[cpar][d][:, h * 64:(h + 1) * 64], start=True, stop=False)
                    X('pe', 'matmul', [kp('ARt'), hk + 'b'], ['pa%d' % bx], out=pa[bx][:, cs], lhsT=ARt2[cpar][d][base:base + 64, p_, 0, :],
                      rhs=Hbf[d][base:base + 64, p_, base:base + 64], start=False, stop=True)
                X('act', 'activation', ['pa%d' % bx], [kd('Xb%d_' % p_)], out=Xb[d][:, p_, :], in_=pa[bx][:, 0:128], func=AF.Copy)
                yield
                for hh in range(2):
                    h = 2 * p_ + hh
                    cs = slice(hh * 64, hh * 64 + 64)
                    X('pe', 'matmul', [kd('TI%d_' % h), kd('Xb%d_' % p_)], ['pa%d' % bx], out=pa[bx][:, 256 + hh * 64:256 + hh * 64 + 64], lhsT=TI[d][:, h, :],
                      rhs=Xb[d][:, p_, cs], start=True, stop=True)
                X('dve', 'tensor_copy', ['pa%d' % bx], [kd('Ub%d_' % p_)], out=Ub[d][:, p_, :], in_=pa[bx][:, 256:384])
                yield
                for hh in range(2):
                    h = 2 * p_ + hh
                    base = hh * 64
                    cs = slice(hh * 64, hh * 64 + 64)
                    ys = slice(128 + hh * 64, 128 + hh * 64 + 64)
                    X('pe', 'matmul', [kp('ARt'), hk + 'b'], ['pa%d' % bx], out=pa[bx][:, ys], lhsT=ARt2[cpar][d][base:base + 64, p_, 1, :],
                      rhs=Hbf[d][base:base + 64, p_, base:base + 64], start=True, stop=False)
                    X('pe', 'matmul', [kd('SC%d_' % h), kp('Vt')], ['pa%d' % bx], out=pa[bx][:, ys], lhsT=SC[d][:, h, 384:512],
                      rhs=Vt2[cpar][d][:, h * 64:(h + 1) * 64], start=False, stop=False)
                    X('pe', 'matmul', [kd('SC%d_' % h), kd('Ub%d_' % p_)], ['pa%d' % bx], out=pa[bx][:, ys], lhsT=SC[d][:, h, 128:256],
                      rhs=Ub[d][:, p_, cs], start=False, stop=True)
                X('dve', 'tensor_tensor', ['pa%d' % bx, K('Yacc%d' % tt)], [K('Yacc%d' % tt)], out=Yacc[:, tt, p_ * 128:(p_ + 1) * 128],
                  in0=pa[bx][:, 128:256], in1=Yacc[:, tt, p_ * 128:(p_ + 1) * 128], op=ALU.add)
                ps_ = slice(p_ * 128, (p_ + 1) * 128)
                X('pe', 'matmul', [kp('Kt'), kp('Vt')], ['pa%d' % bx], out=pa[bx][:, 384:512], lhsT=Kt2[cpar][d][:, ps_], rhs=Vt2[cpar][d][:, ps_],
                  start=True, stop=False)
                X('pe', 'matmul', [kp('Bt'), kd('Ub%d_' % p_)], ['pa%d' % bx], out=pa[bx][:, 384:512], lhsT=Bt2[cpar][d][:, ps_],
                  rhs=Ub[d][:, p_, :], start=False, stop=True)
                X('dve', 'tensor_tensor', ['pa%d' % bx, hk + 'f'], [kd('t32')], out=t32[d], in0=pa[bx][:, 384:512], in1=H32[d][:, p_, :],
                  op=ALU.add)
                X('dve', 'tensor_scalar', [kd('t32'), kp('PC')], [hk + 'f'], out=H32[d][:, p_, :], in0=t32[d],
                  scalar1=PC2[cpar][d][:, p_:p_ + 1], scalar2=None, op0=ALU.mult)
                X('act', 'activation', [kd('t32'), kp('PC')], [hk + 'b'], out=Hbf[d][:, p_, :], in_=t32[d], func=AF.Copy,
                  scale=PC2[cpar][d][:, p_:p_ + 1])
                yield

        MODE = os.environ.get('RW_MODE', 'overlap')
        if MODE == 'seq':
            for c in range(NT):
                par = c % 2
                run_rr([prep_gen(0, c, par), prep_gen(1, NT - 1 - c, par)])
                run_heads(par, [])
                run_chains(c, par)
        else:
            run_rr([prep_gen(0, 0, 0), prep_gen(1, NT - 1, 0)])
            for c in range(NT):
                par = c % 2
                extra = [prep_gen(0, c + 1, 1 - par), prep_gen(1, NT - 2 - c, 1 - par)] if c + 1 < NT else []
                run_heads(par, extra)
                run_chains(c, par)

        P.barrier()
        ph.off = mark
        uc = A_([128, 1536], F32)
        gtb = A_([128, 512], F32)
        yv = A_([128, 512], F32)
        sqv = A_([128, 512], F32)
        st8 = A_([128, 64], F32)
        for tt in range(NT):
            b = tt % 2
            r0 = tt * 128
            kb2 = lambda n: K('%s%d' % (n, b))
            DMA('sp', uc[b], ubd[r0:r0 + 128, 0:1536], ['ub%d' % tt], [kb2('uc')])
            DMA('sp', gtb[b], proj[r0:r0 + 128, 3072:3584], ['proj%d_6' % tt], [kb2('gtb')])
            X('act', 'activation', [kb2('gtb')], [kb2('gtb')], out=gtb[b], in_=gtb[b], func=AF.Silu)
            yt = Yacc[:, tt, :]
            S = st8[b]
            X('dve', 'tensor_reduce', [K('Yacc%d' % tt)], [kb2('st8')], out=S[:, 0:8], in_=v8(yt), axis=AX.X, op=ALU.add)
            X('pool', 'tensor_tensor', [K('Yacc%d' % tt)], [kb2('sqv')], out=sqv[b], in0=yt, in1=yt, op=ALU.mult)
            X('dve', 'tensor_reduce', [kb2('sqv')], [kb2('st8')], out=S[:, 8:16], in_=v8(sqv[b]), axis=AX.X, op=ALU.add)
            X('dve', 'tensor_scalar', [kb2('st8')], [kb2('st8')], out=S[:, 16:24], in0=S[:, 0:8], scalar1=1.0 / 64,
              scalar2=None, op0=ALU.mult)
            X('dve', 'tensor_tensor', [kb2('st8')], [kb2('st8')], out=S[:, 24:32], in0=S[:, 16:24], in1=S[:, 16:24],
              op=ALU.mult)
            X('dve', 'scalar_tensor_tensor', [kb2('st8')], [kb2('st8')], out=S[:, 32:40], in0=S[:, 8:16], scalar=1.0 / 64,
              in1=S[:, 24:32], op0=ALU.mult, op1=ALU.subtract)
            X('dve', 'tensor_scalar', [kb2('st8')], [kb2('st8')], out=S[:, 32:40], in0=S[:, 32:40], scalar1=64e-5,
              scalar2=None, op0=ALU.add)
            X('act', 'activation', [kb2('st8')], [kb2('st8')], out=S[:, 40:48], in_=S[:, 32:40], func=AF.Sqrt)
            X('dve', 'reciprocal', [kb2('st8')], [kb2('st8')], out=S[:, 48:56], in_=S[:, 40:48])
            X('dve', 'tensor_tensor', [K('Yacc%d' % tt), kb2('st8')], [kb2('yv')], out=v8(yv[b]), in0=v8(yt),
              in1=S[:, 16:24].unsqueeze(2).to_broadcast([128, 8, 64]), op=ALU.subtract)
            X('pool', 'tensor_tensor', [kb2('yv'), kb2('st8')], [kb2('yv')], out=v8(yv[b]), in0=v8(yv[b]),
              in1=S[:, 48:56].unsqueeze(2).to_broadcast([128, 8, 64]), op=ALU.mult)
            X('pool', 'tensor_tensor', [kb2('yv'), K('LNG')], [kb2('yv')], out=yv[b], in0=yv[b], in1=LNG, op=ALU.mult)
            X('pool', 'tensor_tensor', [kb2('yv'), K('LNB')], [kb2('yv')], out=yv[b], in0=yv[b], in1=LNB, op=ALU.add)
            X('dve', 'tensor_tensor', [kb2('uc')], [kb2('sqv')], out=sqv[b], in0=uc[b][:, 0:512], in1=uc[b][:, 512:1024],
              op=ALU.mult)
            X('pool', 'tensor_tensor', [kb2('sqv'), K('RKB')], [kb2('sqv')], out=sqv[b], in0=sqv[b], in1=RKB, op=ALU.mult)
            X('dve', 'tensor_reduce', [kb2('sqv')], [kb2('st8')], out=S[:, 56:64], in_=v8(sqv[b]), axis=AX.X, op=ALU.add)
            X('dve', 'tensor_tensor', [kb2('uc'), kb2('st8')], [kb2('sqv')], out=v8(sqv[b]), in0=v8(uc[b][:, 1024:1536]),
              in1=S[:, 56:64].unsqueeze(2).to_broadcast([128, 8, 64]), op=ALU.mult)
            X('pool', 'tensor_tensor', [kb2('yv'), kb2('sqv')], [kb2('yv')], out=yv[b], in0=yv[b], in1=sqv[b], op=ALU.add)
            X('pool', 'tensor_tensor', [kb2('yv'), kb2('gtb')], [kb2('yv')], out=yv[b], in0=yv[b], in1=gtb[b], op=ALU.mult)
            DMA('sp', mixd[r0:r0 + 128, 512:1024], yv[b], [kb2('yv')], ['mixd%d' % tt])

    def even_mixer(layer):
        i = layer // 2
        if 'a' in parts:
            attn_part(i)
        else:
            zero_mix(0, 512)
        if 'b' in parts:
            rwkv_part(i)
        else:
            zero_mix(512, 1024)

    EVEN_HOOK = globals().get('_even_mixer_builder')
    for s in range(nseq):
        xs = x[s * T:(s + 1) * T, :]
        ys = y[s * T:(s + 1) * T, :]
        first = True
        for layer in layers:
            hsrc = xs if first else ys
            j = layer // 2
            if layer % 2 == 1:
                prenorm_inproj(hsrc, layer, prm['odd_w_in'][j], ODD_IN)
                odd_mixer(layer)
                out_proj(prm['odd_w_out'][j], 2048, hsrc, ys, layer)
            else:
                prenorm_inproj(hsrc, layer, prm['even_w_in'][j], EVEN_IN)
                even_mixer(layer)
                out_proj(prm['even_w_out'][j], 1024, hsrc, ys, layer)
            first = False
    if dbg:
        print('instr counts', P.n_instr())
    return P.build()


def make_in_maps(inputs, ncores=8, nseq=4):
    consts = host_consts()
    x = np.ascontiguousarray(inputs['x'], dtype=np.float32)
    maps = []
    for c in range(ncores):
        m = {'x': x[c * nseq:(c + 1) * nseq].reshape(nseq * T, D)}
        for k, shp in PARAM_SHAPES.items():
            m[k] = np.ascontiguousarray(inputs[k], dtype=np.float32).reshape(shp)
        m.update({'c_' + k: v for k, v in consts.items()})
        maps.append(m)
    return maps


def kernel(**inputs):
    nc = build_program()
    maps = make_in_maps(inputs)
    res = run_bass_kernel_spmd(nc, maps, core_ids=list(range(8)))
    out = np.concatenate([np.asarray(r['y']).reshape(4, T, D) for r in res.results], axis=0)
    return out.astype(np.float32)
```
